# Optimizing a Trainium2 kernel written in Bass

```python
import jax, jax.numpy as jnp
from jax import lax
import numpy as np

D_MODEL = 2048
BATCH = 8
SEQ = 4096
DEPTH = 4

GRID_W = 64
CTX_LEN = 256
N_BRANCH = 4
BRANCH_W = D_MODEL // N_BRANCH
CHUNK = 128
A_GROUPS = 4
A_GW = BRANCH_W // A_GROUPS
B_HD = 64
B_QH = BRANCH_W // B_HD
B_KVH = B_QH // 4
B_WIN = 128
B_BLK = 128
C_GROUPS = 4
C_GW = BRANCH_W // C_GROUPS
D_HD = 32
D_H = BRANCH_W // D_HD
NA_ROWS = 8
NA_COLS = 16
ROPE_BASE = 10000.0
EPS = 1e-6
NEG = -1e30

PARTS = ("a_u", "a_v", "a_z",
         "b_q", "b_k", "b_v", "b_z",
         "f_in", "f_z",
         "d_q", "d_k", "d_v", "d_z")
IN_SIZES = (BRANCH_W, BRANCH_W, BRANCH_W,
            B_QH * B_HD, B_KVH * B_HD, B_KVH * B_HD, BRANCH_W,
            BRANCH_W, BRANCH_W,
            BRANCH_W, BRANCH_W, BRANCH_W, BRANCH_W)
IN_WIDTH = sum(IN_SIZES)
CTX_KV_PARTS = ("b_k", "b_v", "d_k", "d_v")

kernel_name = "hybrid_gated_parallel_mixer_dit"


def rmsnorm(x, g):
    xf = x.astype(jnp.float32)
    y = xf * lax.rsqrt(jnp.mean(xf * xf, axis=-1, keepdims=True) + EPS)
    return (y * g.astype(jnp.float32)).astype(x.dtype)


def heads(t, n):
    return t.reshape(t.shape[:-1] + (n, -1))


def _rope_1d(x, pos):
    half = x.shape[-1] // 2
    inv = ROPE_BASE ** (-jnp.arange(half, dtype=jnp.float32) / half)
    ang = pos.astype(jnp.float32)[:, None] * inv[None, :]
    cos = jnp.cos(ang)[None, :, None, :]
    sin = jnp.sin(ang)[None, :, None, :]
    xf = x.astype(jnp.float32)
    x1, x2 = xf[..., :half], xf[..., half:]
    return jnp.concatenate([x1 * cos - x2 * sin, x2 * cos + x1 * sin], axis=-1).astype(x.dtype)


def rope_2d(x, pos_r, pos_c):
    d = x.shape[-1] // 2
    return jnp.concatenate([_rope_1d(x[..., :d], pos_r), _rope_1d(x[..., d:], pos_c)], axis=-1)


def project_all(n, w):
    z = n @ w
    offs = [int(o) for o in np.cumsum((0,) + IN_SIZES)]
    return {name: z[..., offs[i]:offs[i + 1]] for i, name in enumerate(PARTS)}


def project_some(n, w, names):
    offs = [int(o) for o in np.cumsum((0,) + IN_SIZES)]
    return {name: n @ w[:, offs[i]:offs[i + 1]] for i, name in enumerate(PARTS) if name in names}


def chunk_mlp(u, v, g_v, w_s, b_s):
    Bn, L, _ = u.shape
    u = jax.nn.gelu(u)
    v = rmsnorm(jax.nn.gelu(v), g_v)
    vb = v.reshape(Bn, L // CHUNK, CHUNK, A_GROUPS, A_GW)
    mixed = jnp.einsum('gts,bnsgc->bntgc', w_s, vb) + b_s.T[None, None, :, :, None]
    return u * mixed.reshape(Bn, L, BRANCH_W)


def fourier_mix(xf_in, w_f):
    Bn, L, _ = xf_in.shape
    xg = xf_in.astype(jnp.float32).reshape(Bn, L, C_GROUPS, C_GW)
    f = jnp.real(jnp.fft.fft2(xg, axes=(1, 3), norm="ortho")).astype(xf_in.dtype)
    return jnp.einsum('blgc,gcd->blgd', f, w_f).reshape(Bn, L, BRANCH_W)


def dense_attn(q, k, v, sink):
    Bn, Lq, Hq, hd = q.shape
    Lk, Hk = k.shape[1], k.shape[2]
    G = Hq // Hk
    qg = q.reshape(Bn, Lq, Hk, G, hd)
    s = jnp.einsum('bqkgd,bjkd->bkgqj', qg, k).astype(jnp.float32) * (hd ** -0.5)
    if sink is not None:
        sk = jnp.broadcast_to(sink.astype(jnp.float32).reshape(1, Hk, G, 1, 1), (Bn, Hk, G, Lq, 1))
        s = jnp.concatenate([s, sk], axis=-1)
    p = jax.nn.softmax(s, axis=-1)[..., :Lk].astype(v.dtype)
    o = jnp.einsum('bkgqj,bjkd->bqkgd', p, v)
    return o.reshape(Bn, Lq, Hq * hd)


def window_gqa(q, k, v, kc, vc, sink):
    Bn, S, Hq, hd = q.shape
    Hk = k.shape[2]
    G = Hq // Hk
    C = kc.shape[1]
    nb = S // B_BLK
    nl = 3 * B_BLK
    scale = hd ** -0.5
    pad = ((0, 0), (B_BLK, B_BLK), (0, 0), (0, 0))
    kp, vp = jnp.pad(k, pad), jnp.pad(v, pad)
    sink_f = sink.astype(jnp.float32).reshape(1, Hk, G, 1, 1)

    def block(i):
        start = i * B_BLK
        qi = lax.dynamic_slice_in_dim(q, start, B_BLK, axis=1).reshape(Bn, B_BLK, Hk, G, hd)
        ki = lax.dynamic_slice_in_dim(kp, start, nl, axis=1)
        vi = lax.dynamic_slice_in_dim(vp, start, nl, axis=1)
        qpos = start + jnp.arange(B_BLK)
        kpos = start - B_BLK + jnp.arange(nl)
        valid = (jnp.abs(kpos[None, :] - qpos[:, None]) <= B_WIN) & (kpos[None, :] >= 0) & (kpos[None, :] < S)
        s_loc = jnp.einsum('bqkgd,bjkd->bkgqj', qi, ki).astype(jnp.float32) * scale
        s_loc = jnp.where(valid, s_loc, NEG)
        s_ctx = jnp.einsum('bqkgd,bckd->bkgqc', qi, kc).astype(jnp.float32) * scale
        sk = jnp.broadcast_to(sink_f, s_loc.shape[:-1] + (1,))
        p = jax.nn.softmax(jnp.concatenate([s_loc, s_ctx, sk], axis=-1), axis=-1).astype(v.dtype)
        o = (jnp.einsum('bkgqj,bjkd->bqkgd', p[..., :nl], vi)
             + jnp.einsum('bkgqc,bckd->bqkgd', p[..., nl:nl + C], vc))
        return o.reshape(Bn, B_BLK, Hq * hd)

    out = lax.map(block, jnp.arange(nb))
    return out.transpose(1, 0, 2, 3).reshape(Bn, S, Hq * hd)


def neighborhood_attn(q, k, v, kc, vc, rpb):
    Bn, S, H, hd = q.shape
    rows = S // GRID_W
    wr = min(NA_ROWS, rows)
    nk = wr * NA_COLS
    scale = hd ** -0.5
    qg = q.reshape(Bn, rows, GRID_W, H, hd)
    kg = k.reshape(Bn, rows, GRID_W, H, hd)
    vg = v.reshape(Bn, rows, GRID_W, H, hd)
    row_start = jnp.clip(jnp.arange(rows) - wr // 2, 0, rows - wr)
    c_idx = jnp.arange(GRID_W)
    col_start = jnp.clip(c_idx - NA_COLS // 2, 0, GRID_W - NA_COLS)
    col_win = col_start[:, None] + jnp.arange(NA_COLS)[None, :]
    dc = col_win - c_idx[:, None] + (NA_COLS - 1)

    def row_block(r):
        r0 = row_start[r]
        krow = lax.dynamic_slice_in_dim(kg, r0, wr, axis=1)
        vrow = lax.dynamic_slice_in_dim(vg, r0, wr, axis=1)
        kn = krow[:, :, col_win]
        vn = vrow[:, :, col_win]
        qr = lax.dynamic_index_in_dim(qg, r, axis=1, keepdims=False)
        dr = r0 + jnp.arange(wr) - r + (NA_ROWS - 1)
        bias = rpb[:, dr[:, None, None], dc[None, :, :]].transpose(0, 2, 1, 3)
        s_loc = jnp.einsum('bqhd,brqjhd->bhqrj', qr, kn).astype(jnp.float32) * scale
        s_loc = s_loc + bias.astype(jnp.float32)[None]
        s_ctx = jnp.einsum('bqhd,bchd->bhqc', qr, kc).astype(jnp.float32) * scale
        logits = jnp.concatenate([s_loc.reshape(Bn, H, GRID_W, nk), s_ctx], axis=-1)
        p = jax.nn.softmax(logits, axis=-1).astype(v.dtype)
        p_loc = p[..., :nk].reshape(Bn, H, GRID_W, wr, NA_COLS)
        o = (jnp.einsum('bhqrj,brqjhd->bqhd', p_loc, vn)
             + jnp.einsum('bhqc,bchd->bqhd', p[..., nk:], vc))
        return o

    out = lax.map(row_block, jnp.arange(rows))
    return out.transpose(1, 0, 2, 3, 4).reshape(Bn, S, H * hd)


def merge_branches(n, ys, zs, w_gate, w_branch, w_out):
    acc = jax.nn.sigmoid(n @ w_gate[0]) * ((ys[0] * jax.nn.silu(zs[0])) @ w_branch[0])
    for b in range(1, N_BRANCH):
        acc = acc + jax.nn.sigmoid(n @ w_gate[b]) * ((ys[b] * jax.nn.silu(zs[b])) @ w_branch[b])
    return acc @ w_out


def setup_inputs(seed: int = 0) -> dict:
    key = jax.random.key(seed)
    ks = jax.random.split(key, 24)
    D = D_MODEL

    def nrm(k, shape, s):
        return jax.random.normal(k, shape, jnp.float32) * s

    return {
        "x": nrm(ks[0], (BATCH, SEQ, D), 1.0),
        "c": nrm(ks[1], (BATCH, D), 1.0),
        "ctx": nrm(ks[2], (BATCH, CTX_LEN, D), 1.0),
        "c_ctx": nrm(ks[3], (D,), 1.0),
        "norm_g": 1.0 + nrm(ks[4], (DEPTH, D), 0.05),
        "w_ada": nrm(ks[5], (DEPTH, D, 3 * D), 0.5 * D ** -0.5),
        "b_ada": nrm(ks[6], (DEPTH, 3 * D), 0.02),
        "w_in": nrm(ks[7], (DEPTH, D, IN_WIDTH), D ** -0.5),
        "a_norm_g": 1.0 + nrm(ks[8], (DEPTH, BRANCH_W), 0.05),
        "a_w_s": nrm(ks[9], (DEPTH, A_GROUPS, CHUNK, CHUNK), CHUNK ** -0.5),
        "a_b_s": 1.0 + nrm(ks[10], (DEPTH, A_GROUPS, CHUNK), 0.1),
        "b_q_g": 1.0 + nrm(ks[11], (DEPTH, B_HD), 0.05),
        "b_k_g": 1.0 + nrm(ks[12], (DEPTH, B_HD), 0.05),
        "b_sink": nrm(ks[13], (DEPTH, B_QH), 0.5),
        "c_w_f": nrm(ks[14], (DEPTH, C_GROUPS, C_GW, C_GW), C_GW ** -0.5),
        "d_q_g": 1.0 + nrm(ks[15], (DEPTH, D_HD), 0.05),
        "d_k_g": 1.0 + nrm(ks[16], (DEPTH, D_HD), 0.05),
        "d_rpb": nrm(ks[17], (DEPTH, D_H, 2 * NA_ROWS - 1, 2 * NA_COLS - 1), 0.1),
        "w_gate": nrm(ks[18], (DEPTH, N_BRANCH, D, D), D ** -0.5),
        "w_branch": nrm(ks[19], (DEPTH, N_BRANCH, BRANCH_W, D), BRANCH_W ** -0.5),
        "w_out": nrm(ks[20], (DEPTH, D, D), D ** -0.5),
    }


def reference(x, c, ctx, c_ctx, norm_g, w_ada, b_ada, w_in, a_norm_g, a_w_s, a_b_s,
              b_q_g, b_k_g, b_sink, c_w_f, d_q_g, d_k_g, d_rpb, w_gate, w_branch, w_out):
    S = x.shape[1]
    t = jnp.arange(S)
    pos_r, pos_c = t // GRID_W, t % GRID_W
    h, hc = x, ctx
    for l in range(DEPTH):
        last = l == DEPTH - 1
        m = jax.nn.silu(c) @ w_ada[l] + b_ada[l]
        mc = jax.nn.silu(c_ctx) @ w_ada[l] + b_ada[l]
        sh, sc, gt = jnp.split(m[:, None, :], 3, axis=-1)
        shc, scc, gtc = jnp.split(mc, 3, axis=-1)
        n = rmsnorm(h, norm_g[l]) * (1.0 + sc) + sh
        nc = rmsnorm(hc, norm_g[l]) * (1.0 + scc) + shc

        z = project_all(n, w_in[l])
        zc = project_some(nc, w_in[l], CTX_KV_PARTS) if last else project_all(nc, w_in[l])

        kBc = rmsnorm(heads(zc["b_k"], B_KVH), b_k_g[l])
        vBc = heads(zc["b_v"], B_KVH)
        kDc = rmsnorm(heads(zc["d_k"], D_H), d_k_g[l])
        vDc = heads(zc["d_v"], D_H)

        qB = rope_2d(rmsnorm(heads(z["b_q"], B_QH), b_q_g[l]), pos_r, pos_c)
        kB = rope_2d(rmsnorm(heads(z["b_k"], B_KVH), b_k_g[l]), pos_r, pos_c)
        vB = heads(z["b_v"], B_KVH)
        qD = rmsnorm(heads(z["d_q"], D_H), d_q_g[l])
        kD = rmsnorm(heads(z["d_k"], D_H), d_k_g[l])
        vD = heads(z["d_v"], D_H)
        y_lat = [chunk_mlp(z["a_u"], z["a_v"], a_norm_g[l], a_w_s[l], a_b_s[l]),
                 window_gqa(qB, kB, vB, kBc, vBc, b_sink[l]),
                 fourier_mix(z["f_in"], c_w_f[l]),
                 neighborhood_attn(qD, kD, vD, kDc, vDc, d_rpb[l])]
        g_lat = [z["a_z"], z["b_z"], z["f_z"], z["d_z"]]
        h_new = h + gt * merge_branches(n, y_lat, g_lat, w_gate[l], w_branch[l], w_out[l])

        if not last:
            qBc = rmsnorm(heads(zc["b_q"], B_QH), b_q_g[l])
            qDc = rmsnorm(heads(zc["d_q"], D_H), d_q_g[l])
            y_ctx = [chunk_mlp(zc["a_u"], zc["a_v"], a_norm_g[l], a_w_s[l], a_b_s[l]),
                     dense_attn(qBc, kBc, vBc, b_sink[l]),
                     fourier_mix(zc["f_in"], c_w_f[l]),
                     dense_attn(qDc, kDc, vDc, None)]
            g_ctx = [zc["a_z"], zc["b_z"], zc["f_z"], zc["d_z"]]
            hc = hc + gtc * merge_branches(nc, y_ctx, g_ctx, w_gate[l], w_branch[l], w_out[l])
        h = h_new
    return h
```

```python
import numpy as np
import ml_dtypes
import concourse.bass as bass
import concourse.mybir as mybir
from concourse.bass_utils import run_bass_kernel_spmd

F32 = mybir.dt.float32
BF16 = mybir.dt.bfloat16
AF = mybir.ActivationFunctionType
ALU = mybir.AluOpType

QUEUES = ("pe", "act", "dve", "pool", "sp")
COMPUTE = ("pe", "act", "dve", "pool")


class Res:
    __slots__ = ("name", "w", "r", "dsem")

    def __init__(self, name, dsem=None):
        self.name = name
        self.w = None
        self.r = {}
        self.dsem = dsem


class DmaSem:
    __slots__ = ("sem", "count", "last")

    def __init__(self, sem):
        self.sem = sem
        self.count = 0
        self.last = None


class Ins:
    __slots__ = ("q", "fn", "deps", "dma", "dsem", "dcount", "signal", "count", "epoch", "idx", "bar")

    def __init__(self, q, fn, dma, epoch):
        self.q = q
        self.fn = fn
        self.deps = set()
        self.dma = dma
        self.dsem = None
        self.dcount = 0
        self.signal = False
        self.count = 0
        self.epoch = epoch
        self.bar = None


class Prog:
    def __init__(self, nc, n_epochs, n_dma_sems):
        self.nc = nc
        self.ins = {q: [] for q in QUEUES}
        self.epoch = 0
        self.esem = {}
        for e in range(n_epochs):
            for q in COMPUTE:
                self.esem[(q, e)] = nc.alloc_semaphore(name=f"s_{q}_{e}")
        self.dsems = [DmaSem(nc.alloc_semaphore(name=f"d_{i}")) for i in range(n_dma_sems)]
        self.dsem_next = 0
        self.all = []
        self.pending_bar = {q: None for q in QUEUES}
        self.n_waits = 0

    def res(self, name, dma=False):
        d = None
        if dma:
            d = self.dsems[self.dsem_next % len(self.dsems)]
            self.dsem_next += 1
        return Res(name, d)

    def _rec(self, q, fn, reads, writes, dma=False, dsem=None):
        i = Ins(q, fn, dma, self.epoch)
        i.idx = len(self.all)
        self.all.append(i)
        if self.pending_bar[q] is not None:
            i.bar = self.pending_bar[q]
            self.pending_bar[q] = None
        for r in reads:
            if r.w is not None:
                i.deps.add(r.w)
        for w in writes:
            if w.w is not None:
                i.deps.add(w.w)
            for x in w.r.values():
                i.deps.add(x)
        if dma:
            i.dsem = dsem
            if dsem.last is not None:
                i.deps.add(dsem.last)
            dsem.count += 16
            i.dcount = dsem.count
            dsem.last = i
        for r in reads:
            key = ("dma", i.idx) if dma else q
            r.r[key] = i
        for w in writes:
            w.w = i
            w.r = {}
        i.deps.discard(i)
        self.ins[q].append(i)
        return i

    def op(self, q, fn, reads=(), writes=()):
        return self._rec(q, fn, list(reads), list(writes))

    def dma(self, q, out, in_, reads, writes, slot, **kw):
        def fn(eng, out=out, in_=in_, kw=kw):
            return eng.dma_start(out=out, in_=in_, **kw)
        return self._rec(q, fn, list(reads), list(writes), dma=True, dsem=slot.dsem)

    def barrier(self):
        snap = {"eng": {}, "dma": []}
        for q in COMPUTE:
            for x in reversed(self.ins[q]):
                if not x.dma and x.fn is not None:
                    snap["eng"][q] = x
                    break
        for d in self.dsems:
            if d.count:
                snap["dma"].append((d, d.count))
        for q in QUEUES:
            self.pending_bar[q] = snap

    def finish(self):
        self.barrier()
        for q in QUEUES:
            if self.ins[q]:
                self._rec(q, None, [], [])

    def emit(self):
        nc = self.nc
        for i in self.all:
            for p in i.deps:
                if p.dma:
                    continue
                if p.q == i.q and p.q == "pe":
                    continue
                p.signal = True
            if i.bar is not None:
                for q, last in i.bar["eng"].items():
                    last.signal = True
        for q in COMPUTE:
            cnt = {}
            for i in self.ins[q]:
                if i.signal and not i.dma and i.fn is not None:
                    cnt[i.epoch] = cnt.get(i.epoch, 0) + 1
                    i.count = cnt[i.epoch]
        engs = {"pe": "tensor", "act": "scalar", "dve": "vector", "pool": "gpsimd", "sp": "sync"}
        prog = self

        def run_queue(q, eng):
            waited = {}

            def wait(sem, val):
                k = id(sem)
                if waited.get(k, 0) >= val:
                    return
                waited[k] = val
                eng.wait_ge(sem, val)
                prog.n_waits += 1

            for i in prog.ins[q]:
                if i.bar is not None:
                    for pq, last in i.bar["eng"].items():
                        if pq == q and q == "pe":
                            continue
                        wait(prog.esem[(pq, last.epoch)], last.count)
                    for d, c in i.bar["dma"]:
                        wait(d.sem, c)
                for p in i.deps:
                    if p.dma:
                        wait(p.dsem.sem, p.dcount)
                    else:
                        if p.q == q and q == "pe":
                            continue
                        wait(prog.esem[(p.q, p.epoch)], p.count)
                if i.fn is None:
                    continue
                inst = i.fn(eng)
                if i.dma:
                    inst.then_inc(i.dsem.sem, 16)
                elif i.signal:
                    inst.then_inc(prog.esem[(q, i.epoch)], 1)

        with nc.Block() as block:
            for q in QUEUES:
                if not prog.ins[q]:
                    continue
                deco = getattr(block, engs[q])

                def body(eng, q=q):
                    run_queue(q, eng)
                deco(body)


D = 2048
KC = 16
TL = 4096
TCX = 256
T = TL + TCX
NT_TILES = T // 128
DEPTH = 4
EPS = 1e-6
NZ = 46

OFF = dict(a_u=0, a_v=512, a_z=1024, b_q=1536, b_k=2048, b_v=2176, b_z=2304,
           f_in=2816, f_z=3328, d_q=3840, d_k=4352, d_v=4864, d_z=5376)


def z_chunks():
    ch = []
    for g in range(4):
        ch.append(("gelu_bf", "GU", g, list(range(OFF["a_u"] + 128 * g, OFF["a_u"] + 128 * g + 128))))
    for g in range(4):
        ch.append(("gelu_f32", "AV", g, list(range(OFF["a_v"] + 128 * g, OFF["a_v"] + 128 * g + 128))))
    for bi, nm in enumerate(("a_z", "b_z", "f_z", "d_z")):
        for g in range(4):
            ch.append(("silu_bf", "SZ%d" % bi, g, list(range(OFF[nm] + 128 * g, OFF[nm] + 128 * g + 128))))
    for c in range(4):
        cols = list(range(OFF["b_q"] + 64 * c, OFF["b_q"] + 64 * c + 64)) + \
            list(range(OFF["b_q"] + 64 * (c + 4), OFF["b_q"] + 64 * (c + 4) + 64))
        ch.append(("id_f32", "BQ", c, cols))
    ch.append(("id_f32", "BK", 0, list(range(OFF["b_k"], OFF["b_k"] + 128))))
    ch.append(("id_bf", "BV", 0, list(range(OFF["b_v"], OFF["b_v"] + 128))))
    for g in range(4):
        ch.append(("id_bf", "FIN", g, list(range(OFF["f_in"] + 128 * g, OFF["f_in"] + 128 * g + 128))))
    for g in range(4):
        ch.append(("id_f32", "DQ", g, list(range(OFF["d_q"] + 128 * g, OFF["d_q"] + 128 * g + 128))))
    for g in range(4):
        ch.append(("id_f32", "DK", g, list(range(OFF["d_k"] + 128 * g, OFF["d_k"] + 128 * g + 128))))
    for g in range(4):
        ch.append(("id_bf", "DV", g, list(range(OFF["d_v"] + 128 * g, OFF["d_v"] + 128 * g + 128))))
    assert len(ch) == NZ
    return ch


ZCH = z_chunks()
PHASES = "WABCDM"
ARENA_WORDS = 44 * 1024
CAST_DMA_BYTES = 4096
WMODE = 0


def token_chunks():
    out = []
    for c in range(3):
        b = 1088 * c
        out.append([(b, 384, 0), (b + 384, 352, 0), (b + 736, 352, 0)])
    out.append([(3264, 416, 0), (3680, 416, 0), (4096, 256, 1)])
    return out


TCHUNKS = token_chunks()


def d_tiles(i):
    base = min(max(i - 2, 0), 27)
    if i == 0:
        pb = 0
    elif i == 1:
        pb = 5
    elif i == 30:
        pb = 15
    elif i == 31:
        pb = 20
    else:
        pb = 10
    return [(base + o, pb + o) for o in range(5)]


_CONST_CACHE = {}


def host_consts():
    if _CONST_CACHE:
        return _CONST_CACHE
    bf = ml_dtypes.bfloat16
    c = {}
    c["ident_f"] = np.eye(128, dtype=np.float32)
    c["ident_b"] = np.eye(128, dtype=np.float32).astype(bf)
    c["ones_b"] = np.ones((128, 128), np.float32).astype(bf)
    p = np.arange(128)
    c["blk64_b"] = (p[:, None] // 64 == p[None, :] // 64).astype(np.float32).astype(bf)
    c["blk32_b"] = (p[:, None] // 32 == p[None, :] // 32).astype(np.float32).astype(bf)
    partner = np.where((p % 32) < 16, p + 16, p - 16)
    perm = np.zeros((128, 128), np.float32)
    perm[partner, p] = 1.0
    c["perm_f"] = perm
    c["ones_row"] = np.ones((1, 128), np.float32)
    hm = np.zeros((128, 4, 128), np.float32)
    for j in range(4):
        hm[32 * j:32 * j + 32, j, :] = 1.0
    c["headmask"] = hm.reshape(128, 512).astype(bf)
    k = np.arange(128)[:, None]
    q = np.arange(128)[None, :]
    m_prev = (k >= q).astype(np.float32)
    m_next = (k <= q).astype(np.float32)
    bm = np.stack([np.tile(m_prev, (1, 4)), np.tile(m_next, (1, 4))], axis=1)
    c["bmask"] = bm.reshape(128, 1024).astype(bf)
    t = np.arange(TL)
    pos_r, pos_c = (t // 64).astype(np.float64), (t % 64).astype(np.float64)
    inv = 10000.0 ** (-np.arange(16, dtype=np.float64) / 16)
    d = p % 64
    pos = np.where((d < 32)[:, None], pos_r[None, :], pos_c[None, :])
    ang = pos * inv[d % 16][:, None]
    ang32 = (pos.astype(np.float32) * inv.astype(np.float32)[d % 16][:, None]).astype(np.float32)
    c["rcos"] = np.cos(ang32).astype(np.float32)
    sgn = np.where((d % 32) < 16, -1.0, 1.0)[:, None]
    c["rsin"] = (np.sin(ang32) * sgn).astype(np.float32)
    n = np.arange(TL, dtype=np.int64)
    prod = (n[:, None] * n[None, :]) % TL
    a = 2 * np.pi * prod.astype(np.float64) / TL
    c["CL"] = (np.cos(a) / 64.0).astype(np.float32).astype(bf)
    c["SL"] = (np.sin(a) / 64.0).astype(np.float32).astype(bf)
    n2 = np.arange(TCX, dtype=np.int64)
    a2 = 2 * np.pi * ((n2[:, None] * n2[None, :]) % TCX).astype(np.float64) / TCX
    c["C256"] = (np.cos(a2) / 16.0).astype(np.float32).astype(bf)
    c["S256"] = (np.sin(a2) / 16.0).astype(np.float32).astype(bf)
    n3 = np.arange(128, dtype=np.int64)
    a3 = 2 * np.pi * ((n3[:, None] * n3[None, :]) % 128).astype(np.float64) / 128
    c["CC"] = (np.cos(a3) / np.sqrt(128.0)).astype(np.float32)
    c["SC"] = (np.sin(a3) / np.sqrt(128.0)).astype(np.float32)
    _CONST_CACHE.update(c)
    return c


def rpb_gather_index():
    reps = [0, 1, 2, 30, 31]
    DR = np.zeros((25, 128, 128), np.int64)
    DC = np.zeros((25, 128, 128), np.int64)
    V = np.zeros((25, 128, 128), bool)
    kk = np.arange(128)[:, None]
    qq = np.arange(128)[None, :]
    for ri, i in enumerate(reps):
        base = min(max(i - 2, 0), 27)
        for o in range(5):
            kt = base + o
            kr = 2 * kt + kk // 64
            kcol = kk % 64
            qr = 2 * i + qq // 64
            qc = qq % 64
            r0 = np.clip(qr - 4, 0, 56)
            c0 = np.clip(qc - 8, 0, 48)
            valid = (kr >= r0) & (kr <= r0 + 7) & (kcol >= c0) & (kcol <= c0 + 15)
            dr = np.clip(kr - qr + 7, 0, 14)
            dc = np.clip(kcol - qc + 15, 0, 30)
            pid = ri * 5 + o
            DR[pid], DC[pid], V[pid] = np.broadcast_to(dr, (128, 128)), np.broadcast_to(dc, (128, 128)), valid
    return DR, DC, V


def host_layout(inp, layers):
    L = len(layers)
    sh = {}
    w_in = inp["w_in"]
    cols = np.concatenate([np.asarray(z[3]) for z in ZCH])
    wi = np.empty((L, NZ, 128, KC * 128), np.float32)
    for li, l in enumerate(layers):
        w = w_in[l][:, cols]
        wi[li] = w.reshape(KC, 128, NZ, 128).transpose(2, 1, 0, 3).reshape(NZ, 128, KC * 128)
    sh["w_in_b"] = wi
    wa = np.empty((L, 48, 128, KC * 128), np.float32)
    for li, l in enumerate(layers):
        wa[li] = inp["w_ada"][l].reshape(KC, 128, 48, 128).transpose(2, 1, 0, 3).reshape(48, 128, KC * 128)
    sh["w_ada_b"] = wa
    sh["b_ada_f"] = np.ascontiguousarray(inp["b_ada"][layers].reshape(L, 48, 128).transpose(0, 2, 1))
    sh["norm_g_f"] = np.ascontiguousarray(inp["norm_g"][layers].reshape(L, KC, 128).transpose(0, 2, 1))
    wg = np.empty((L, 4, 16, 128, KC * 128), np.float32)
    wb = np.empty((L, 4, 16, 128, 4 * 128), np.float32)
    wo = np.empty((L, 16, 128, KC * 128), np.float32)
    for li, l in enumerate(layers):
        for b in range(4):
            wg[li, b] = inp["w_gate"][l, b].reshape(KC, 128, 16, 128).transpose(2, 1, 0, 3).reshape(16, 128, KC * 128)
            wb[li, b] = inp["w_branch"][l, b].reshape(4, 128, 16, 128).transpose(2, 1, 0, 3).reshape(16, 128, 512)
        wo[li] = inp["w_out"][l].reshape(KC, 128, 16, 128).transpose(2, 1, 0, 3).reshape(16, 128, KC * 128)
    sh["w_gate_b"], sh["w_branch_b"], sh["w_out_b"] = wg, wb, wo
    sh["a_g_f"] = np.ascontiguousarray(inp["a_norm_g"][layers].reshape(L, 4, 128).transpose(0, 2, 1))
    sh["a_wsT"] = np.ascontiguousarray(inp["a_w_s"][layers].transpose(0, 3, 1, 2)).reshape(L, 128, 512)
    sh["a_bs"] = np.ascontiguousarray(inp["a_b_s"][layers].reshape(L, 1, 512))
    sh["b_qg"] = np.ascontiguousarray(np.tile(inp["b_q_g"][layers], (1, 2)).reshape(L, 128, 1))
    sh["b_kg"] = np.ascontiguousarray(np.tile(inp["b_k_g"][layers], (1, 2)).reshape(L, 128, 1))
    sh["b_sinkb"] = np.ascontiguousarray(np.broadcast_to(inp["b_sink"][layers][:, None, :], (L, 128, 8)))
    sh["c_wf"] = np.ascontiguousarray(inp["c_w_f"][layers].transpose(0, 2, 1, 3)).reshape(L, 128, 512)
    sh["d_qg"] = np.ascontiguousarray(np.tile(inp["d_q_g"][layers], (1, 4)).reshape(L, 128, 1))
    sh["d_kg"] = np.ascontiguousarray(np.tile(inp["d_k_g"][layers], (1, 4)).reshape(L, 128, 1))
    DR, DC, V = rpb_gather_index()
    rp = inp["d_rpb"][layers]
    g = rp[:, :, DR, DC]
    g = np.where(V[None, None], g, np.float32(-100.0)).astype(np.float32)
    g = g.reshape(L, 4, 4, 25, 128, 128).transpose(0, 3, 1, 4, 2, 5)
    sh["rpbg"] = np.ascontiguousarray(g).reshape(L, 25, 4, 128, 512)
    sh.update(host_consts())
    return sh


class Builder:
    def __init__(self, n_layers, last_flags, debug=False):
        self.L = n_layers
        self.last_flags = last_flags
        self.debug = debug
        nc = bass.Bass("TRN2", target_bir_lowering=False)
        self.nc = nc
        self.pg = Prog(nc, n_layers + 1, 70)
        self.arena = nc.alloc_sbuf_tensor("arena", [128, ARENA_WORDS], F32)
        self.persist_off = 0
        self.off = 0
        self.ps = [nc.alloc_psum_tensor(f"ps{i}", [128, 512], F32) for i in range(8)]
        self.rps = [self.pg.res(f"ps{i}") for i in range(8)]
        self.dram = {}
        self.dres = {}

    def din(self, name, shape, dt=F32):
        t = self.nc.dram_tensor(name, list(shape), dt, kind="ExternalInput").ap()
        self.dram[name] = t
        return t

    def dscr(self, name, shape, dt, out=False):
        kind = "ExternalOutput" if (out or self.debug) else "Internal"
        t = self.nc.dram_tensor(name, list(shape), dt, kind=kind).ap()
        self.dram[name] = t
        return t

    def dr(self, key):
        r = self.dres.get(key)
        if r is None:
            r = self.pg.res(str(key))
            self.dres[key] = r
        return r

    def alloc(self, n, dt=F32, dma=False, name="t"):
        words = n if dt == F32 else (n + 1) // 2
        words = (words + 1) // 2 * 2
        assert self.off + words <= ARENA_WORDS, f"SBUF arena overflow at {name}: {self.off}+{words}"
        a = self.arena[:, self.off:self.off + words]
        self.off += words
        if dt != F32:
            a = a.bitcast(dt)[:, 0:n]
        return a, self.pg.res(name, dma=dma)

    def ring(self, k, n, dt=F32, dma=False, name="r"):
        return Ring([self.alloc(n, dt, dma, f"{name}{i}") for i in range(k)])

    def phase_reset(self):
        self.pg.barrier()
        self.off = self.persist_off

    def mm(self, out, lhsT, rhs, start, stop, rd, wr):
        self.pg.op("pe", lambda e: e.matmul(out, lhsT=lhsT, rhs=rhs, start=start, stop=stop), rd, wr)

    def tr(self, out, in_, ident, rd, wr):
        self.pg.op("pe", lambda e: e.transpose(out=out, in_=in_, identity=ident), rd, wr)

    def act(self, out, in_, func, rd, wr, scale=1.0, bias=0.0):
        self.pg.op("act", lambda e: e.activation(out=out, in_=in_, func=func, scale=scale, bias=bias), rd, wr)

    def tt(self, q, out, in0, in1, op, rd, wr):
        self.pg.op(q, lambda e: e.tensor_tensor(out=out, in0=in0, in1=in1, op=op), rd, wr)

    def ts(self, q, out, in0, s1, op0, rd, wr, s2=None, op1=None):
        if op1 is None:
            self.pg.op(q, lambda e: e.tensor_scalar(out=out, in0=in0, scalar1=s1, scalar2=None, op0=op0), rd, wr)
        else:
            self.pg.op(q, lambda e: e.tensor_scalar(out=out, in0=in0, scalar1=s1, scalar2=s2, op0=op0, op1=op1), rd, wr)

    def stt(self, out, in0, scalar, in1, op0, op1, rd, wr):
        self.pg.op("dve", lambda e: e.scalar_tensor_tensor(out=out, in0=in0, scalar=scalar, in1=in1, op0=op0, op1=op1), rd, wr)

    def cp(self, q, out, in_, rd, wr):
        self.pg.op(q, lambda e: e.tensor_copy(out=out, in_=in_), rd, wr)

    def recip(self, out, in_, rd, wr):
        self.pg.op("dve", lambda e: e.reciprocal(out=out, in_=in_), rd, wr)

    def memset(self, q, ap, val, wr):
        self.pg.op(q, lambda e: e.memset(ap, val), [], wr)

    def load(self, out, in_, slot, rd=(), q="sp"):
        if q == "pool":
            self.pg.dma(q, out, in_, rd, [slot], slot, max_dma_last_dim=CAST_DMA_BYTES)
        else:
            self.pg.dma(q, out, in_, rd, [slot], slot)

    def store(self, out, in_, slot, wr=(), q="sp"):
        self.pg.dma(q, out, in_, [slot], wr, slot)


class Ring:
    def __init__(self, items):
        self.items = items
        self.i = 0

    def next(self):
        it = self.items[self.i % len(self.items)]
        self.i += 1
        return it


def build_program(n_layers, last_flags, debug=False):
    B = Builder(n_layers, last_flags, debug)
    nc, pg, ps, rps = B.nc, B.pg, B.ps, B.rps
    L = n_layers
    MUL, ADD = ALU.mult, ALU.add

    hT0 = B.din("hT0", [D, T])
    cvec_d = B.din("cvec", [128, 32])
    w_ada_b = B.din("w_ada_b", [L, 48, 128, 2048])
    b_ada_f = B.din("b_ada_f", [L, 128, 48])
    norm_g_f = B.din("norm_g_f", [L, 128, 16])
    w_in_b = B.din("w_in_b", [L, NZ, 128, 2048])
    w_gate_b = B.din("w_gate_b", [L, 4, 16, 128, 2048])
    w_branch_b = B.din("w_branch_b", [L, 4, 16, 128, 512])
    w_out_b = B.din("w_out_b", [L, 16, 128, 2048])
    a_g_f = B.din("a_g_f", [L, 128, 4])
    a_wsT = B.din("a_wsT", [L, 128, 512])
    a_bs = B.din("a_bs", [L, 1, 512])
    b_qg = B.din("b_qg", [L, 128, 1])
    b_kg = B.din("b_kg", [L, 128, 1])
    b_sinkb = B.din("b_sinkb", [L, 128, 8])
    c_wf = B.din("c_wf", [L, 128, 512])
    d_qg = B.din("d_qg", [L, 128, 1])
    d_kg = B.din("d_kg", [L, 128, 1])
    rpbg = B.din("rpbg", [L, 25, 4, 128, 512])
    cd = {}
    for nm, shp, dt in (("ident_f", [128, 128], F32), ("ident_b", [128, 128], BF16), ("ones_b", [128, 128], BF16),
                        ("blk64_b", [128, 128], BF16), ("blk32_b", [128, 128], BF16), ("perm_f", [128, 128], F32),
                        ("ones_row", [1, 128], F32), ("headmask", [128, 512], BF16), ("bmask", [128, 1024], BF16),
                        ("rcos", [128, TL], F32), ("rsin", [128, TL], F32), ("CL", [TL, TL], BF16), ("SL", [TL, TL], BF16),
                        ("C256", [TCX, TCX], BF16), ("S256", [TCX, TCX], BF16), ("CC", [128, 128], F32), ("SC", [128, 128], F32)):
        cd[nm] = B.din(nm, shp, dt)

    OUT = B.dscr("OUT", [D, TL], F32, out=True)
    H = B.dscr("H", [D, T], F32)
    NT = B.dscr("NT", [D, T], BF16)
    GU = B.dscr("GU", [512, T], BF16)
    AV = B.dscr("AV", [512, T], F32)
    SZ = [B.dscr("SZ%d" % i, [512, T], BF16) for i in range(4)]
    BQ = B.dscr("BQ", [512, T], F32)
    BK = B.dscr("BK", [128, T], F32)
    BV = B.dscr("BV", [128, T], BF16)
    FIN = B.dscr("FIN", [512, T], BF16)
    DQ = B.dscr("DQ", [512, T], F32)
    DK = B.dscr("DK", [512, T], F32)
    DV = B.dscr("DV", [512, T], BF16)
    YG = [B.dscr("YG%d" % i, [512, T], BF16) for i in range(4)]
    ET = B.dscr("ET", [25, 4, 128, 512], BF16)
    ZD = dict(GU=GU, AV=AV, SZ0=SZ[0], SZ1=SZ[1], SZ2=SZ[2], SZ3=SZ[3], BQ=BQ, BK=BK, BV=BV, FIN=FIN, DQ=DQ, DK=DK, DV=DV)

    def cst(name, src, n, dt=F32, parts=128):
        a, r = B.alloc(n, dt, dma=True, name=name)
        a = a[0:parts, :]
        B.load(a, src, r)
        return a, r

    ident_f, r_identf = cst("ident_f", cd["ident_f"], 128)
    ident_b, r_identb = cst("ident_b", cd["ident_b"], 128, BF16)
    ones_b, r_onesb = cst("ones_b", cd["ones_b"], 128, BF16)
    blk64, r_blk64 = cst("blk64", cd["blk64_b"], 128, BF16)
    blk32, r_blk32 = cst("blk32", cd["blk32_b"], 128, BF16)
    perm_f, r_perm = cst("perm_f", cd["perm_f"], 128)
    ones_row, r_onesrow = cst("ones_row", cd["ones_row"], 128, F32, parts=1)
    headmask, r_hm = cst("headmask", cd["headmask"], 512, BF16)
    bmask, r_bm = cst("bmask", cd["bmask"], 1024, BF16)
    CCs, r_CC = cst("CC", cd["CC"], 128)
    SCs, r_SC = cst("SC", cd["SC"], 128)
    C256s, r_C256 = cst("C256", cd["C256"].rearrange("(s p) t -> p s t", p=128), 512, BF16)
    S256s, r_S256 = cst("S256", cd["S256"].rearrange("(s p) t -> p s t", p=128), 512, BF16)
    cvec, r_cvec = cst("cvec", cvec_d, 32)
    bada, r_bada = cst("bada", b_ada_f.rearrange("l p o -> p l o"), L * 48)
    MT, r_MT = B.alloc(L * 96, F32, name="MT")
    MTv = MT.rearrange("p (l o j) -> p l o j", l=L, o=48)
    normg, r_normg = B.alloc(16, F32, dma=True, name="normg")
    Gm, r_G = B.alloc(32, F32, name="G")
    Gv = Gm.rearrange("p (k j) -> p k j", j=2)
    gvA, r_gvA = B.alloc(4, F32, dma=True, name="gvA")
    wsTf, r_wsTf = B.alloc(512, F32, dma=True, name="wsTf")
    wsTb, r_wsTb = B.alloc(512, BF16, name="wsTb")
    bsA, r_bsA = B.alloc(512, F32, dma=True, name="bsA")
    bsA = bsA[0:1, :]
    gqB, r_gqB = B.alloc(2, F32, dma=True, name="gqB")
    gkB, r_gkB = B.alloc(2, F32, dma=True, name="gkB")
    esink, r_esink = B.alloc(8, F32, dma=True, name="esink")
    wfC, r_wfC = B.alloc(512, F32, dma=True, name="wfC")
    gqD, r_gqD = B.alloc(2, F32, dma=True, name="gqD")
    gkD, r_gkD = B.alloc(2, F32, dma=True, name="gkD")
    B.persist_off = B.off

    def phase_adaln():
        sc_t, r_sc = B.alloc(32, F32, name="silu_c")
        B.act(sc_t, cvec, AF.Silu, [r_cvec], [r_sc])
        scv = sc_t.rearrange("p (k j) -> p k j", j=2)
        ring = B.ring(3, 2048, F32, dma=True, name="wada")
        pend = []

        def wl(l, oc):
            w, r = ring.next()
            B.load(w, w_ada_b[l, oc], r)
            return w, r
        todo = [(l, oc) for l in range(L) for oc in range(48)]
        for x in todo[:2]:
            pend.append(wl(*x))
        for i, (l, oc) in enumerate(todo):
            if i + 2 < len(todo):
                pend.append(wl(*todo[i + 2]))
            w, r_w = pend.pop(0)
            bank = i % 2
            for kc in range(KC):
                B.mm(ps[bank][:, 0:2], w[:, kc * 128:(kc + 1) * 128], scv[:, kc, :], kc == 0, kc == KC - 1,
                     [r_w, r_sc], [rps[bank]])
            B.act(MTv[:, l, oc, :], ps[bank][:, 0:2], AF.Identity, [rps[bank], r_bada], [r_MT],
                  bias=bada[:, l * 48 + oc:l * 48 + oc + 1])

    def layer_params(li):
        B.load(normg, norm_g_f[li], r_normg)
        B.load(gvA, a_g_f[li], r_gvA)
        B.load(wsTf, a_wsT[li], r_wsTf)
        B.load(bsA, a_bs[li], r_bsA)
        B.load(gqB[:, 0:1], b_qg[li], r_gqB)
        B.load(gkB[:, 0:1], b_kg[li], r_gkB)
        B.load(esink, b_sinkb[li], r_esink)
        B.load(wfC, c_wf[li], r_wfC)
        B.load(gqD[:, 0:1], d_qg[li], r_gqD)
        B.load(gkD[:, 0:1], d_kg[li], r_gkD)
        B.cp("dve", wsTb, wsTf, [r_wsTf], [r_wsTb])
        B.act(esink, esink, AF.Exp, [r_esink], [r_esink])
        B.ts("dve", Gv, MTv[:, li, 16:32, :], 1.0, ADD, [r_MT], [r_G])
        B.tt("dve", Gv, Gv, normg.unsqueeze(2).to_broadcast([128, 16, 2]), MUL, [r_G, r_normg], [r_G])

    def phase_norm_win(li):
        hsrc = hT0 if li == 0 else H
        hsv = hsrc.rearrange("(k p) t -> p k t", p=128)
        NTv = NT.rearrange("(k p) t -> p k t", p=128)
        B.phase_reset()
        nTs = [[B.alloc(16 * 416, BF16, dma=True, name=f"nT{a}_{i}") for i in range(3)] for a in range(2)]
        hring = B.ring(1, 16 * 416, F32, dma=True, name="h")
        sq, r_sq = B.alloc(16 * 416, BF16, name="sq")
        rstd, r_rstd = B.alloc(416, F32, name="rstd")
        wring = B.ring(4, 2048, BF16, dma=True, name="w")
        stg = B.ring(3, 1248, F32, dma=True, name="stg")
        chunks = TCHUNKS[:1] if WMODE else TCHUNKS

        def norm_pieces(ci):
            subs = chunks[ci]
            pieces = []
            for si, (t0, n, j) in enumerate(subs):
                nT, r_nT = nTs[ci % 2][si]
                nTv = nT.rearrange("p (k t) -> p k t", k=16)[:, :, 0:n]
                sqv = sq.rearrange("p (k t) -> p k t", k=16)[:, :, 0:n]
                box = {}

                def p_load(t0=t0, n=n, box=box):
                    hs, r_hs = hring.next()
                    hv = hs.rearrange("p (k t) -> p k t", k=16)[:, :, 0:n]
                    B.load(hv, hsv[:, :, t0:t0 + n], r_hs)
                    box["hv"], box["r"] = hv, r_hs

                def p_stat(n=n, box=box, sqv=sqv):
                    hv, r_hs = box["hv"], box["r"]
                    B.act(sqv, hv, AF.Square, [r_hs], [r_sq])
                    for kc in range(KC):
                        B.mm(ps[6][:, 0:n], ones_b, sqv[:, kc, :], kc == 0, kc == KC - 1, [r_sq, r_onesb], [rps[6]])
                    B.act(rstd[:, 0:n], ps[6][:, 0:n], AF.Sqrt, [rps[6]], [r_rstd], scale=1.0 / D, bias=EPS)
                    B.recip(rstd[:, 0:n], rstd[:, 0:n], [r_rstd], [r_rstd])
                    B.tt("dve", hv, hv, rstd[:, 0:n].unsqueeze(1).to_broadcast([128, 16, n]), MUL, [r_hs, r_rstd], [r_hs])

                def p_mod(k0, j=j, box=box, nTv=nTv, r_nT=r_nT):
                    hv, r_hs = box["hv"], box["r"]
                    for kc in range(k0, k0 + 4):
                        B.act(nTv[:, kc, :], hv[:, kc, :], AF.Identity, [r_hs, r_G, r_MT], [r_nT],
                              scale=Gv[:, kc, j:j + 1], bias=MTv[:, li, kc, j:j + 1])

                def p_store(t0=t0, n=n, nTv=nTv, r_nT=r_nT):
                    B.store(NTv[:, :, t0:t0 + n], nTv, r_nT)
                pieces += [p_load, p_stat, (lambda f=p_mod: f(0)), (lambda f=p_mod: f(4)), (lambda f=p_mod: f(8)),
                           (lambda f=p_mod: f(12)), p_store]
            return pieces

        for p in norm_pieces(0):
            p()
        for ci, subs in enumerate(chunks):
            tbase = subs[0][0]
            ntot = sum(s_[1] for s_ in subs)
            nxt = norm_pieces(ci + 1) if ci + 1 < len(chunks) else []
            pend = []

            def wl(oc):
                w, r = wring.next()
                B.load(w, w_in_b[li, oc], r, q="pool")
                return w, r
            pend.append(wl(0))
            pend.append(wl(1))
            for oc in range(NZ if WMODE < 2 else WMODE):
                if oc + 2 < NZ:
                    pend.append(wl(oc + 2))
                w, r_w = pend.pop(0)
                kind, dest, dch, _ = ZCH[oc]
                st, r_st = stg.next()
                is_bf = kind.endswith("_bf")
                stv = st.bitcast(BF16) if is_bf else st
                off = 0
                for si, (t0, n, j) in enumerate(subs):
                    nT, r_nT = nTs[ci % 2][si]
                    nTv = nT.rearrange("p (k t) -> p k t", k=16)[:, :, 0:n]
                    bank = (oc * 3 + si) % 6
                    for kc in range(KC):
                        B.mm(ps[bank][:, 0:n], w[:, kc * 128:(kc + 1) * 128], nTv[:, kc, :], kc == 0, kc == KC - 1,
                             [r_w, r_nT], [rps[bank]])
                    o = stv[:, off:off + n]
                    if kind.startswith("gelu"):
                        B.act(o, ps[bank][:, 0:n], AF.Gelu_apprx_tanh, [rps[bank]], [r_st])
                    elif kind.startswith("silu"):
                        B.act(o, ps[bank][:, 0:n], AF.Silu, [rps[bank]], [r_st])
                    else:
                        B.cp("dve", o, ps[bank][:, 0:n], [rps[bank]], [r_st])
                    off += n
                B.store(ZD[dest][dch * 128:(dch + 1) * 128, tbase:tbase + ntot], stv[:, 0:ntot], r_st)
                if oc >= 8 and nxt:
                    nxt.pop(0)()
            while nxt:
                nxt.pop(0)()

    def phase_A(li):
        last = last_flags[li]
        B.phase_reset()
        avr = B.ring(2, 2048, F32, dma=True, name="av")
        gur = B.ring(2, 2048, BF16, dma=True, name="gu")
        szr = B.ring(2, 2048, BF16, dma=True, name="sz")
        ygr = B.ring(2, 2048, BF16, dma=True, name="yg")
        sqr = B.ring(2, 2048, BF16, name="sq")
        rsr = B.ring(2, 512, F32, name="rstd")
        vnr = B.ring(2, 2048, BF16, name="vn")
        vtr = B.ring(2, 2048, BF16, name="vtok")
        t1r = B.ring(2, 2048, F32, name="t1")
        AVv = AV.rearrange("(g p) t -> p g t", p=128)
        GUv = GU.rearrange("(g p) t -> p g t", p=128)
        SZv = SZ[0].rearrange("(g p) t -> p g t", p=128)
        YGv = YG[0].rearrange("(g p) t -> p g t", p=128)
        nch = 8 if last else 9

        def s1(ci):
            t0 = ci * 512
            n = 512 if ci < 8 else 256
            ntt = n // 128
            av, r_av = avr.next()
            gu, r_gu = gur.next()
            sz, r_sz = szr.next()
            sq, r_sq = sqr.next()
            rstd, r_rstd = rsr.next()
            vn, r_vn = vnr.next()
            vtok, r_vtok = vtr.next()
            t1, r_t1 = t1r.next()
            v3 = lambda a: a.rearrange("p (g t) -> p g t", g=4)[:, :, 0:n]
            avv, guv, szv, sqv, vnv, t1v = v3(av), v3(gu), v3(sz), v3(sq), v3(vn), v3(t1)
            B.load(avv, AVv[:, :, t0:t0 + n], r_av)
            B.load(guv, GUv[:, :, t0:t0 + n], r_gu)
            B.load(szv, SZv[:, :, t0:t0 + n], r_sz)
            B.act(sqv, avv, AF.Square, [r_av], [r_sq])
            for g in range(4):
                B.mm(ps[0][:, 0:n], ones_b, sqv[:, g, :], g == 0, g == 3, [r_sq, r_onesb], [rps[0]])
            B.act(rstd[:, 0:n], ps[0][:, 0:n], AF.Sqrt, [rps[0]], [r_rstd], scale=1.0 / 512, bias=EPS)
            B.recip(rstd[:, 0:n], rstd[:, 0:n], [r_rstd], [r_rstd])
            for g in range(4):
                B.stt(vnv[:, g, :], avv[:, g, :], gvA[:, g:g + 1], rstd[:, 0:n], MUL, MUL, [r_av, r_rstd, r_gvA], [r_vn])
            vtv = vtok.rearrange("p (t c) -> p t c", t=4)
            for tt in range(ntt):
                bank = 1 + tt // 2
                psb = ps[bank].bitcast(BF16)
                for g in range(4):
                    c0 = (tt % 2) * 512 + g * 128
                    B.tr(psb[:, c0:c0 + 128], vnv[:, g, tt * 128:(tt + 1) * 128], ident_b, [r_vn, r_identb], [rps[bank]])
                if tt % 2 == 0:
                    B.act(vtv[:, tt, :], psb[:, 0:512], AF.Identity, [rps[bank]], [r_vtok])
                else:
                    B.cp("dve", vtv[:, tt, :], psb[:, 512:1024], [rps[bank]], [r_vtok])
            B.tt("pool", t1v, guv, szv, MUL, [r_gu, r_sz], [r_t1])
            return (t0, n, ntt, vtv, r_vtok, t1v, r_t1)

        def s2(ctx):
            t0, n, ntt, vtv, r_vtok, t1v, r_t1 = ctx
            yg, r_yg = ygr.next()
            ygv = yg.rearrange("p (g t) -> p g t", g=4)[:, :, 0:n]
            for g in range(4):
                bank = 3 + g
                for tt in range(ntt):
                    o = ps[bank][:, tt * 128:(tt + 1) * 128]
                    B.mm(o, vtv[:, tt, g * 128:(g + 1) * 128], wsTb[:, g * 128:(g + 1) * 128], True, False,
                         [r_vtok, r_wsTb], [rps[bank]])
                    B.mm(o, ones_row, bsA[:, g * 128:(g + 1) * 128], False, True, [r_onesrow, r_bsA], [rps[bank]])
                B.tt("dve", ygv[:, g, :], ps[bank][:, 0:n], t1v[:, g, :], MUL, [rps[bank], r_t1], [r_yg])
            B.store(YGv[:, :, t0:t0 + n], ygv, r_yg)

        prev = s1(0)
        for ci in range(nch):
            nxt = s1(ci + 1) if ci + 1 < nch else None
            s2(prev)
            prev = nxt

    def phase_B(li):
        B.phase_reset()
        QT, r_QT = B.alloc(4 * T, BF16, name="QT")
        KT, r_KT = B.alloc(T, BF16, name="KT")
        VA, r_VA = B.alloc(NT_TILES * 130, BF16, name="VA")
        QTv = QT.rearrange("p (c t) -> p c t", c=4)
        VAv = VA.rearrange("p (n j e) -> p n j e", n=NT_TILES, j=2)
        szbr = B.ring(2, 2048, BF16, dma=True, name="SZb")
        SZ1v = SZ[1].rearrange("(c p) t -> p c t", p=128)
        B.memset("pool", VA, 1.0, [r_VA])
        bqr = B.ring(2, 2048, F32, dma=True, name="bq")
        bkr = B.ring(2, 512, F32, dma=True, name="bk")
        bvr = B.ring(2, 512, BF16, dma=True, name="bv")
        cosr = B.ring(2, 512, F32, dma=True, name="cos")
        sinr = B.ring(2, 512, F32, dma=True, name="sin")
        sqr = B.ring(2, 512, BF16, name="sq")
        rsr = B.ring(2, 512, F32, name="rs")
        qnr = B.ring(2, 512, F32, name="qn")
        t1r = B.ring(2, 512, F32, name="t1")
        t2r = B.ring(2, 512, F32, name="t2")
        BQv = BQ.rearrange("(c p) t -> p c t", p=128)
        pcnt = dict(k=0)

        def pbank():
            b_ = 6 + pcnt["k"] % 2
            pcnt["k"] += 1
            return b_

        def prep_pieces(ci):
            t0 = ci * 512
            n = 512 if ci < 8 else 256
            lat = ci < 8
            box = {}
            pieces = []

            def p_load():
                bq, r_bq = bqr.next()
                bk, r_bk = bkr.next()
                bv, r_bv = bvr.next()
                bqv = bq.rearrange("p (c t) -> p c t", c=4)[:, :, 0:n]
                B.load(bqv, BQv[:, :, t0:t0 + n], r_bq)
                B.load(bk[:, 0:n], BK[:, t0:t0 + n], r_bk)
                B.load(bv[:, 0:n], BV[:, t0:t0 + n], r_bv)
                box.update(bqv=bqv, r_bq=r_bq, bk=bk, r_bk=r_bk, bv=bv, r_bv=r_bv)
                if lat:
                    cs, r_cs = cosr.next()
                    sn, r_sn = sinr.next()
                    B.load(cs, cd["rcos"][:, t0:t0 + n], r_cs)
                    B.load(sn, cd["rsin"][:, t0:t0 + n], r_sn)
                    box.update(cs=cs, r_cs=r_cs, sn=sn, r_sn=r_sn)
            pieces.append(p_load)
            for c5 in range(5):
                def p_chain(c5=c5):
                    x = box["bqv"][:, c5, :] if c5 < 4 else box["bk"][:, 0:n]
                    r_x = box["r_bq"] if c5 < 4 else box["r_bk"]
                    g, r_g = (gqB, r_gqB) if c5 < 4 else (gkB, r_gkB)
                    dest = QTv[:, c5, t0:t0 + n] if c5 < 4 else KT[:, t0:t0 + n]
                    r_dest = r_QT if c5 < 4 else r_KT
                    sq, r_sq = sqr.next()
                    rs, r_rs = rsr.next()
                    bank = pbank()
                    B.tt("pool", sq[:, 0:n], x, x, MUL, [r_x], [r_sq])
                    B.mm(ps[bank][:, 0:n], blk64, sq[:, 0:n], True, True, [r_sq, r_blk64], [rps[bank]])
                    B.act(rs[:, 0:n], ps[bank][:, 0:n], AF.Ln, [rps[bank]], [r_rs], scale=1.0 / 64, bias=EPS)
                    B.act(rs[:, 0:n], rs[:, 0:n], AF.Exp, [r_rs], [r_rs], scale=-0.5)
                    if lat:
                        qn, r_qn = qnr.next()
                        B.stt(qn[:, 0:n], x, g[:, 0:1], rs[:, 0:n], MUL, MUL, [r_x, r_g, r_rs], [r_qn])
                        bank2 = pbank()
                        B.mm(ps[bank2][:, 0:n], perm_f, qn[:, 0:n], True, True, [r_qn, r_perm], [rps[bank2]])
                        t1, r_t1 = t1r.next()
                        t2, r_t2 = t2r.next()
                        B.tt("dve", t1[:, 0:n], ps[bank2][:, 0:n], box["sn"][:, 0:n], MUL, [rps[bank2], box["r_sn"]], [r_t1])
                        B.tt("pool", t2[:, 0:n], qn[:, 0:n], box["cs"][:, 0:n], MUL, [r_qn, box["r_cs"]], [r_t2])
                        B.tt("dve", dest, t1[:, 0:n], t2[:, 0:n], ADD, [r_t1, r_t2], [r_dest])
                    else:
                        B.stt(dest, x, g[:, 0:1], rs[:, 0:n], MUL, MUL, [r_x, r_g, r_rs], [r_dest])
                pieces.append(p_chain)

            def p_v():
                bv, r_bv = box["bv"], box["r_bv"]
                for tt in range(n // 128):
                    tile = ci * 4 + tt
                    bank = pbank()
                    psb = ps[bank].bitcast(BF16)
                    B.tr(psb[:, 0:128], bv[:, tt * 128:(tt + 1) * 128], ident_b, [r_bv, r_identb], [rps[bank]])
                    B.cp("dve", VAv[:, tile, :, 0:64], psb[:, 0:128].rearrange("p (j e) -> p j e", j=2), [rps[bank]], [r_VA])
            pieces.append(p_v)
            return pieces

        for ci in (8, 0, 1):
            for p in prep_pieces(ci):
                p()
        pq = []
        for ci in range(2, 8):
            pq += [(ci, p) for p in prep_pieces(ci)]

        def ensure(chunk):
            while pq and pq[0][0] <= chunk:
                pq.pop(0)[1]()

        pring = B.ring(20, 512, BF16, name="P")
        onr = B.ring(2, 512, F32, name="On")
        denr = B.ring(2, 8, F32, name="den")
        ygr = B.ring(2, 2048, BF16, dma=True, name="yg")
        bmv = bmask.rearrange("p (m f) -> p m f", m=2)
        YGv = YG[1].rearrange("(c p) t -> p c t", p=128)
        st = dict(sb=0, ob=0, yg=None, r_yg=None, SZbv=None, r_SZb=None, on=None, r_on=None)

        def s1(qb, j):
            lat = qb < 32
            if lat:
                tiles = ([qb - 1] if qb > 0 else []) + [qb] + ([qb + 1] if qb < 31 else []) + [32, 33]
            else:
                tiles = [32, 33]
            Ps = []
            for kt in tiles:
                bank = st["sb"] % 4
                st["sb"] += 1
                B.mm(ps[bank][:], KT[64 * j:64 * j + 64, kt * 128:(kt + 1) * 128],
                     QTv[64 * j:64 * j + 64, :, qb * 128:(qb + 1) * 128], True, True, [r_KT, r_QT], [rps[bank]])
                Pt, r_P = pring.next()
                B.act(Pt, ps[bank][:], AF.Exp, [rps[bank]], [r_P], scale=0.125)
                if lat and kt == qb - 1:
                    B.tt("dve", Pt, Pt, bmv[:, 0, :], MUL, [r_P, r_bm], [r_P])
                if lat and kt == qb + 1:
                    B.tt("dve", Pt, Pt, bmv[:, 1, :], MUL, [r_P, r_bm], [r_P])
                Ps.append((Pt, r_P, kt))
            return Ps

        def s2(qb, j, Ps):
            if j == 0:
                if qb % 4 == 0:
                    st["yg"], st["r_yg"] = ygr.next()
                    SZb, r_SZb = szbr.next()
                    SZbv = SZb.rearrange("p (c t) -> p c t", c=4)
                    nld = min(512, T - qb * 128)
                    B.load(SZbv[:, :, 0:nld], SZ1v[:, :, qb * 128:qb * 128 + nld], r_SZb)
                    st["SZbv"], st["r_SZb"] = SZbv, r_SZb
                st["on"], st["r_on"] = onr.next()
            on, r_on = st["on"], st["r_on"]
            onv = on.rearrange("p (h e) -> p h e", h=8)
            obank = 4 + st["ob"] % 2
            st["ob"] += 1
            Ov = ps[obank][:, 0:260].rearrange("p (h e) -> p h e", e=65)
            for hh in range(4):
                for ti, (Pt, r_P, kt) in enumerate(Ps):
                    B.mm(ps[obank][:, hh * 65:(hh + 1) * 65], Pt[:, hh * 128:(hh + 1) * 128], VAv[:, kt, j, :],
                         ti == 0, ti == len(Ps) - 1, [r_P, r_VA], [rps[obank]])
            den, r_den = denr.next()
            B.tt("dve", den[:, 0:4], Ov[:, :, 64], esink[:, 4 * j:4 * j + 4], ADD, [rps[obank], r_esink], [r_den])
            B.recip(den[:, 0:4], den[:, 0:4], [r_den], [r_den])
            B.tt("dve", onv[:, 4 * j:4 * j + 4, :], Ov[:, :, 0:64], den[:, 0:4].unsqueeze(2).to_broadcast([128, 4, 64]),
                 MUL, [rps[obank], r_den], [r_on])
            if j == 1:
                yg, r_yg = st["yg"], st["r_yg"]
                ygv = yg.rearrange("p (c t) -> p c t", c=4)
                tb = pbank()
                for cc in range(4):
                    B.tr(ps[tb][:, cc * 128:(cc + 1) * 128], on[:, cc * 128:(cc + 1) * 128], ident_f, [r_on, r_identf], [rps[tb]])
                qq = qb % 4
                B.tt("dve", ygv[:, :, qq * 128:(qq + 1) * 128], ps[tb][:].rearrange("p (c t) -> p c t", c=4),
                     st["SZbv"][:, :, qq * 128:(qq + 1) * 128], MUL, [rps[tb], st["r_SZb"]], [r_yg])
                if qb % 4 == 3 or qb == NT_TILES - 1:
                    t0 = (qb // 4) * 512
                    n = (qq + 1) * 128
                    B.store(YGv[:, :, t0:t0 + n], ygv[:, :, 0:n], r_yg)

        nqb = 32 if last_flags[li] else NT_TILES
        units = [(qb, j) for qb in range(nqb) for j in range(2)]
        prev = s1(*units[0])
        for ui, (qb, j) in enumerate(units):
            if ui + 1 < len(units):
                nq = units[ui + 1][0]
                ensure(min(7, (nq + 1) // 4))
                nxt = s1(*units[ui + 1])
            else:
                nxt = None
            s2(qb, j, prev)
            if pq:
                pq.pop(0)[1]()
            prev = nxt
        ensure(99)

    def phase_C(li):
        B.phase_reset()
        Wcs, r_Wcs = B.alloc(1024, BF16, name="Wcs")
        Wv = Wcs.rearrange("p (g d) -> p g d", g=4)
        for g in range(4):
            B.mm(ps[0][:, 0:128], CCs, wfC[:, g * 128:(g + 1) * 128], True, True, [r_CC, r_wfC], [rps[0]])
            B.mm(ps[1][:, 0:128], SCs, wfC[:, g * 128:(g + 1) * 128], True, True, [r_SC, r_wfC], [rps[1]])
            B.cp("dve", Wv[:, g, 0:128], ps[0][:, 0:128], [rps[0]], [r_Wcs])
            B.act(Wv[:, g, 128:256], ps[1][:, 0:128], AF.Identity, [rps[1]], [r_Wcs], scale=-1.0)
        Yall, r_Y = B.alloc(NT_TILES * 1024, BF16, name="Yall")
        Yv = Yall.rearrange("p (s g d) -> p s g d", s=NT_TILES, g=4)
        finr = B.ring(2, 2048, BF16, dma=True, name="fin")
        FINv = FIN.rearrange("(g p) t -> p g t", p=128)
        k = 0
        for ci in range(8 if last_flags[li] else 9):
            t0 = ci * 512
            n = 512 if ci < 8 else 256
            fin, r_fin = finr.next()
            finv = fin.rearrange("p (g t) -> p g t", g=4)[:, :, 0:n]
            B.load(finv, FINv[:, :, t0:t0 + n], r_fin)
            for tt in range(n // 128):
                st = ci * 4 + tt
                for gp in range(2):
                    bank = k % 4
                    k += 1
                    for g2 in range(2):
                        g = gp * 2 + g2
                        B.mm(ps[bank][:, g2 * 256:(g2 + 1) * 256], finv[:, g, tt * 128:(tt + 1) * 128], Wv[:, g, :], True, True,
                             [r_fin, r_Wcs], [rps[bank]])
                    o = Yv[:, st, gp * 2:gp * 2 + 2, :]
                    i_ = ps[bank][:].rearrange("p (g d) -> p g d", g=2)
                    if k % 2 == 0:
                        B.act(o, i_, AF.Identity, [rps[bank]], [r_Y])
                    else:
                        B.cp("dve", o, i_, [rps[bank]], [r_Y])
        tabr = B.ring(3, 8 * 512, BF16, dma=True, name="ctab")
        tabr2 = B.ring(3, 8 * 512, BF16, dma=True, name="stab")
        szr = B.ring(2, 2048, BF16, dma=True, name="szc")
        ygr = B.ring(2, 2048, BF16, dma=True, name="ygc")
        SZv = SZ[2].rearrange("(g p) t -> p g t", p=128)
        YGv = YG[2].rearrange("(g p) t -> p g t", p=128)
        CLv = cd["CL"].rearrange("(s p) t -> p s t", p=128)
        SLv = cd["SL"].rearrange("(s p) t -> p s t", p=128)
        todo = [(tc, sg) for tc in range(8) for sg in range(4)]

        def tl(tc, sg):
            a, r_a = tabr.next()
            b, r_b = tabr2.next()
            av = a.rearrange("p (s t) -> p s t", s=8)
            bv = b.rearrange("p (s t) -> p s t", s=8)
            B.load(av, CLv[:, sg * 8:(sg + 1) * 8, tc * 512:(tc + 1) * 512], r_a)
            B.load(bv, SLv[:, sg * 8:(sg + 1) * 8, tc * 512:(tc + 1) * 512], r_b)
            return av, r_a, bv, r_b
        pend = [tl(*todo[0]), tl(*todo[1])]
        for i, (tc, sg) in enumerate(todo):
            if i + 2 < len(todo):
                pend.append(tl(*todo[i + 2]))
            av, r_a, bv, r_b = pend.pop(0)
            pb = 4 * (tc % 2)
            if sg == 0:
                sz, r_sz = szr.next()
                szv = sz.rearrange("p (g t) -> p g t", g=4)
                B.load(szv, SZv[:, :, tc * 512:(tc + 1) * 512], r_sz)
            for g in range(4):
                for s8 in range(8):
                    st = sg * 8 + s8
                    B.mm(ps[pb + g][:], Yv[:, st, g, 0:128], av[:, s8, :], st == 0, False, [r_Y, r_a], [rps[pb + g]])
                    B.mm(ps[pb + g][:], Yv[:, st, g, 128:256], bv[:, s8, :], False, st == 31, [r_Y, r_b], [rps[pb + g]])
            if sg == 3:
                yg, r_yg = ygr.next()
                ygv = yg.rearrange("p (g t) -> p g t", g=4)
                for g in range(4):
                    B.tt("dve", ygv[:, g, :], ps[pb + g][:], szv[:, g, :], MUL, [rps[pb + g], r_sz], [r_yg])
                B.store(YGv[:, :, tc * 512:(tc + 1) * 512], ygv, r_yg)
        if last_flags[li]:
            return
        sz, r_sz = szr.next()
        szv = sz.rearrange("p (g t) -> p g t", g=4)[:, :, 0:256]
        B.load(szv, SZv[:, :, TL:T], r_sz)
        yg, r_yg = ygr.next()
        ygv = yg.rearrange("p (g t) -> p g t", g=4)[:, :, 0:256]
        C2 = C256s.rearrange("p (s t) -> p s t", s=2)
        S2 = S256s.rearrange("p (s t) -> p s t", s=2)
        for g in range(4):
            for s in range(2):
                B.mm(ps[g][:, 0:256], Yv[:, 32 + s, g, 0:128], C2[:, s, :], s == 0, False, [r_Y, r_C256], [rps[g]])
                B.mm(ps[g][:, 0:256], Yv[:, 32 + s, g, 128:256], S2[:, s, :], False, s == 1, [r_Y, r_S256], [rps[g]])
            B.tt("dve", ygv[:, g, :], ps[g][:, 0:256], szv[:, g, :], MUL, [rps[g], r_sz], [r_yg])
        B.store(YGv[:, :, TL:T], ygv, r_yg)

    def phase_D(li):
        B.phase_reset()
        rbr = B.ring(3, 512, F32, dma=True, name="rb")
        ebr = B.ring(3, 512, BF16, dma=True, name="eb")
        for pat in range(25):
            for c in range(4):
                rb, r_rb = rbr.next()
                eb, r_eb = ebr.next()
                B.load(rb, rpbg[li, pat, c], r_rb)
                B.act(eb, rb, AF.Exp, [r_rb], [r_eb])
                B.store(ET[pat, c], eb, r_eb)
        B.phase_reset()
        bufs = []
        for a in range(2):
            QTc, r_QT = B.alloc(T, BF16, name=f"QTc{a}")
            KTc, r_KT = B.alloc(T, BF16, name=f"KTc{a}")
            VA, r_VA = B.alloc(NT_TILES * 132, BF16, name=f"VAd{a}")
            SZc, r_SZc = B.alloc(T, BF16, dma=True, name=f"SZc{a}")
            Eint, r_Eint = B.alloc(5 * 512, BF16, dma=True, name=f"Eint{a}")
            B.memset("pool", VA, 1.0, [r_VA])
            bufs.append(dict(QTc=QTc, r_QT=r_QT, KTc=KTc, r_KT=r_KT, VA=VA, r_VA=r_VA,
                             VAv=VA.rearrange("p (n j e) -> p n j e", n=NT_TILES, j=4), SZc=SZc, r_SZc=r_SZc,
                             Eiv=Eint.rearrange("p (o f) -> p o f", o=5), r_Eint=r_Eint))
        edr = B.ring(2, 5 * 512, BF16, dma=True, name="Eedge")
        xr = B.ring(3, 512, F32, dma=True, name="x")
        dvr = B.ring(2, 512, BF16, dma=True, name="dv")
        sqr = B.ring(2, 512, BF16, name="sq")
        rsr = B.ring(2, 512, F32, name="rs")
        pring = B.ring(24, 512, BF16, name="P")
        qmr = B.ring(3, 512, BF16, name="qm")
        onr = B.ring(2, 128, F32, name="On")
        denr = B.ring(2, 8, F32, name="den")
        ygr = B.ring(2, 512, BF16, dma=True, name="yg")
        hmv = headmask.rearrange("p (j q) -> p j q", j=4)
        ETv = ET.rearrange("a c p f -> p a c f")
        cntr = dict(sb=0, ob=0, pb=0)
        nqb = 32 if last_flags[li] else NT_TILES

        def prep_pieces(c):
            bf_ = bufs[c % 2]
            pieces = []

            def p_loads():
                B.load(bf_["SZc"], SZ[3][c * 128:(c + 1) * 128, :], bf_["r_SZc"])
                B.load(bf_["Eiv"], ETv[:, 10:15, c, :], bf_["r_Eint"])
            pieces.append(p_loads)
            for ci in range(9):
                t0 = ci * 512
                n = 512 if ci < 8 else 256
                for which in range(2):
                    def p_norm(t0=t0, n=n, which=which):
                        src, g, r_g = (DQ, gqD, r_gqD) if which == 0 else (DK, gkD, r_gkD)
                        dest, r_dest = (bf_["QTc"], bf_["r_QT"]) if which == 0 else (bf_["KTc"], bf_["r_KT"])
                        x, r_x = xr.next()
                        B.load(x[:, 0:n], src[c * 128:(c + 1) * 128, t0:t0 + n], r_x)
                        sq, r_sq = sqr.next()
                        rs, r_rs = rsr.next()
                        bank = 6 + cntr["pb"] % 2
                        cntr["pb"] += 1
                        B.tt("pool", sq[:, 0:n], x[:, 0:n], x[:, 0:n], MUL, [r_x], [r_sq])
                        B.mm(ps[bank][:, 0:n], blk32, sq[:, 0:n], True, True, [r_sq, r_blk32], [rps[bank]])
                        B.act(rs[:, 0:n], ps[bank][:, 0:n], AF.Ln, [rps[bank]], [r_rs], scale=1.0 / 32, bias=EPS)
                        B.act(rs[:, 0:n], rs[:, 0:n], AF.Exp, [r_rs], [r_rs], scale=-0.5)
                        B.stt(dest[:, t0:t0 + n], x[:, 0:n], g[:, 0:1], rs[:, 0:n], MUL, MUL, [r_x, r_g, r_rs], [r_dest])
                    pieces.append(p_norm)

                def p_v(ci=ci, t0=t0, n=n):
                    dv, r_dv = dvr.next()
                    B.load(dv[:, 0:n], DV[c * 128:(c + 1) * 128, t0:t0 + n], r_dv)
                    for tt in range(n // 128):
                        tile = ci * 4 + tt
                        bank = 6 + cntr["pb"] % 2
                        cntr["pb"] += 1
                        psb = ps[bank].bitcast(BF16)
                        B.tr(psb[:, 0:128], dv[:, tt * 128:(tt + 1) * 128], ident_b, [r_dv, r_identb], [rps[bank]])
                        B.cp("dve", bf_["VAv"][:, tile, :, 0:32], psb[:, 0:128].rearrange("p (j e) -> p j e", j=4),
                             [rps[bank]], [bf_["r_VA"]])
                pieces.append(p_v)
            return pieces

        for p in prep_pieces(0):
            p()
        for c in range(4):
            bf_ = bufs[c % 2]
            QTc, r_QT, KTc, r_KT, VAv, r_VA = bf_["QTc"], bf_["r_QT"], bf_["KTc"], bf_["r_KT"], bf_["VAv"], bf_["r_VA"]
            SZc, r_SZc, Eiv, r_Eint = bf_["SZc"], bf_["r_SZc"], bf_["Eiv"], bf_["r_Eint"]
            nxtp = prep_pieces(c + 1) if c + 1 < 4 else []
            st = dict(yg=None, r_yg=None)

            def s1(qb, c=c, QTc=QTc, r_QT=r_QT, KTc=KTc, r_KT=r_KT, Eiv=Eiv, r_Eint=r_Eint):
                lat = qb < 32
                Ev, r_E = None, None
                if lat:
                    tl_ = d_tiles(qb)
                    pb_ = tl_[0][1]
                    if pb_ == 10:
                        Ev, r_E = Eiv, r_Eint
                    else:
                        ed, r_E = edr.next()
                        Ev = ed.rearrange("p (o f) -> p o f", o=5)
                        B.load(Ev, ETv[:, pb_:pb_ + 5, c, :], r_E)
                    tiles = [(kt, o) for o, (kt, _) in enumerate(tl_)] + [(32, None), (33, None)]
                else:
                    tiles = [(32, None), (33, None)]
                qm, r_qm = qmr.next()
                B.tt("pool", qm.rearrange("p (j q) -> p j q", j=4),
                     QTc[:, qb * 128:(qb + 1) * 128].unsqueeze(1).to_broadcast([128, 4, 128]), hmv, MUL,
                     [r_QT, r_hm], [r_qm])
                Ps = []
                for (kt, o) in tiles:
                    bank = cntr["sb"] % 4
                    cntr["sb"] += 1
                    B.mm(ps[bank][:], KTc[:, kt * 128:(kt + 1) * 128], qm, True, True, [r_KT, r_qm], [rps[bank]])
                    Pt, r_P = pring.next()
                    B.act(Pt, ps[bank][:], AF.Exp, [rps[bank]], [r_P], scale=float(32 ** -0.5))
                    if o is not None:
                        B.tt("dve", Pt, Pt, Ev[:, o, :], MUL, [r_P, r_E], [r_P])
                    Ps.append((Pt, r_P, kt))
                return Ps

            def s2(qb, Ps, c=c, VAv=VAv, r_VA=r_VA, SZc=SZc, r_SZc=r_SZc):
                if qb % 4 == 0:
                    st["yg"], st["r_yg"] = ygr.next()
                yg, r_yg = st["yg"], st["r_yg"]
                obank = 4 + cntr["ob"] % 2
                cntr["ob"] += 1
                Ov = ps[obank][:, 0:132].rearrange("p (h e) -> p h e", e=33)
                for jh in range(4):
                    for ti, (Pt, r_P, kt) in enumerate(Ps):
                        B.mm(ps[obank][:, jh * 33:(jh + 1) * 33], Pt[:, jh * 128:(jh + 1) * 128], VAv[:, kt, jh, :],
                             ti == 0, ti == len(Ps) - 1, [r_P, r_VA], [rps[obank]])
                den, r_den = denr.next()
                B.recip(den[:, 0:4], Ov[:, :, 32], [rps[obank]], [r_den])
                on, r_on = onr.next()
                B.tt("dve", on.rearrange("p (h e) -> p h e", h=4), Ov[:, :, 0:32],
                     den[:, 0:4].unsqueeze(2).to_broadcast([128, 4, 32]), MUL, [rps[obank], r_den], [r_on])
                tb = 6 + cntr["pb"] % 2
                cntr["pb"] += 1
                B.tr(ps[tb][:, 0:128], on, ident_f, [r_on, r_identf], [rps[tb]])
                qq = qb % 4
                B.tt("dve", yg[:, qq * 128:(qq + 1) * 128], ps[tb][:, 0:128], SZc[:, qb * 128:(qb + 1) * 128], MUL,
                     [rps[tb], r_SZc], [r_yg])
                if qb % 4 == 3 or qb == NT_TILES - 1:
                    t0 = (qb // 4) * 512
                    n = (qq + 1) * 128
                    B.store(YG[3][c * 128:(c + 1) * 128, t0:t0 + n], yg[:, 0:n], r_yg)

            prev = s1(0)
            for qb in range(nqb):
                nxt = s1(qb + 1) if qb + 1 < nqb else None
                s2(qb, prev)
                if nxtp:
                    nxtp.pop(0)()
                prev = nxt
            while nxtp:
                nxtp.pop(0)()

    def phase_merge(li):
        last = last_flags[li]
        hsrc = hT0 if li == 0 else H
        hdst = OUT if last else H
        B.phase_reset()
        nTb, r_nTb = B.alloc(16 * 1088, BF16, dma=True, name="nTb")
        ygb, r_ygb = B.alloc(16 * 1088, BF16, dma=True, name="ygb")
        acc, r_acc = B.alloc(16 * 1088, BF16, name="accT")
        nTv = nTb.rearrange("p (k t) -> p k t", k=16)
        ygv = ygb.rearrange("p (k t) -> p k t", k=16)
        accv = acc.rearrange("p (k t) -> p k t", k=16)
        wgr = B.ring(4, 2048, BF16, dma=True, name="wg")
        wbr = B.ring(4, 512, BF16, dma=True, name="wb")
        sigr = B.ring(3, 416, F32, dma=True, name="sig")
        tmpr = B.ring(3, 416, F32, dma=True, name="tmp")
        a32 = [[B.alloc(416, F32, dma=True, name=f"a32_{i}_{k}") for k in range(3)] for i in range(2)]
        wor = wgr
        htr = Ring(sigr.items + tmpr.items)
        ostr = Ring(a32[0] + a32[1])
        NTv = NT.rearrange("(k p) t -> p k t", p=128)
        gb = 0
        for subs0 in TCHUNKS:
            subs = [s for s in subs0 if not (last and s[2] == 1)]
            tbase = subs[0][0]
            ntot = sum(s[1] for s in subs)
            offs = []
            o_ = 0
            for s in subs:
                offs.append(o_)
                o_ += s[1]
            B.load(nTv[:, :, 0:ntot], NTv[:, :, tbase:tbase + ntot], r_nTb)
            r_ygs = []
            for b in range(4):
                r_b = pg.res(f"ygb{b}", dma=True)
                B.pg.dma("sp", ygv[:, b * 4:(b + 1) * 4, 0:ntot],
                         YG[b].rearrange("(k p) t -> p k t", p=128)[:, :, tbase:tbase + ntot], [], [r_b, r_ygb], r_b)
                r_ygs.append(r_b)
            todo = [(oc, b) for oc in range(16) for b in range(4)]

            def wl(oc, b):
                wg, r_wg = wgr.next()
                wb, r_wb = wbr.next()
                B.load(wg, w_gate_b[li, b, oc], r_wg, q="pool")
                B.load(wb, w_branch_b[li, b, oc], r_wb, q="pool")
                return wg, r_wg, wb, r_wb
            pend = [wl(*todo[0]), wl(*todo[1])]
            for i, (oc, b) in enumerate(todo):
                if i + 2 < len(todo):
                    pend.append(wl(*todo[i + 2]))
                wg, r_wg, wb, r_wb = pend.pop(0)
                for si, (t0, n, j) in enumerate(subs):
                    off = offs[si]
                    gbank = gb % 4
                    pbank = 4 + gb % 4
                    gb += 1
                    for kc in range(KC):
                        B.mm(ps[gbank][:, 0:n], wg[:, kc * 128:(kc + 1) * 128], nTv[:, kc, off:off + n], kc == 0, kc == KC - 1,
                             [r_wg, r_nTb], [rps[gbank]])
                    for kc in range(4):
                        B.mm(ps[pbank][:, 0:n], wb[:, kc * 128:(kc + 1) * 128], ygv[:, b * 4 + kc, off:off + n], kc == 0, kc == 3,
                             [r_wb, r_ygs[b]], [rps[pbank]])
                    sig, r_sig = sigr.next()
                    B.act(sig[:, 0:n], ps[gbank][:, 0:n], AF.Sigmoid, [rps[gbank]], [r_sig])
                    a, r_a = a32[oc % 2][si]
                    if b == 0:
                        B.tt("dve", a[:, 0:n], sig[:, 0:n], ps[pbank][:, 0:n], MUL, [r_sig, rps[pbank]], [r_a])
                    else:
                        tmp, r_tmp = tmpr.next()
                        B.tt("dve", tmp[:, 0:n], sig[:, 0:n], ps[pbank][:, 0:n], MUL, [r_sig, rps[pbank]], [r_tmp])
                        if b < 3:
                            B.tt("dve", a[:, 0:n], a[:, 0:n], tmp[:, 0:n], ADD, [r_a, r_tmp], [r_a])
                        else:
                            B.tt("dve", accv[:, oc, off:off + n], a[:, 0:n], tmp[:, 0:n], ADD, [r_a, r_tmp], [r_acc])
            pend = []

            def wol(oc2):
                w, r = wor.next()
                B.load(w, w_out_b[li, oc2], r, q="pool")
                return w, r
            pend.append(wol(0))
            pend.append(wol(1))
            for oc2 in range(16):
                if oc2 + 2 < 16:
                    pend.append(wol(oc2 + 2))
                w, r_w = pend.pop(0)
                hts = []
                for si, (t0, n, j) in enumerate(subs):
                    ht, r_ht = htr.next()
                    B.load(ht[:, 0:n], hsrc[oc2 * 128:(oc2 + 1) * 128, t0:t0 + n], r_ht)
                    hts.append((ht, r_ht))
                for si, (t0, n, j) in enumerate(subs):
                    off = offs[si]
                    bank = gb % 8
                    gb += 1
                    for kc in range(KC):
                        B.mm(ps[bank][:, 0:n], w[:, kc * 128:(kc + 1) * 128], accv[:, kc, off:off + n], kc == 0, kc == KC - 1,
                             [r_w, r_acc], [rps[bank]])
                    ht, r_ht = hts[si]
                    ost, r_ost = ostr.next()
                    B.stt(ost[:, 0:n], ps[bank][:, 0:n], MTv[:, li, 32 + oc2, j:j + 1], ht[:, 0:n], MUL, ADD,
                          [rps[bank], r_ht, r_MT], [r_ost])
                    B.pg.dma("act", hdst[oc2 * 128:(oc2 + 1) * 128, t0:t0 + n], ost[:, 0:n], [r_ost], [], r_ost)

    phase_adaln()
    for li in range(L):
        pg.epoch = li + 1
        B.phase_reset()
        layer_params(li)
        for nm, fn in (("W", phase_norm_win), ("A", phase_A), ("B", phase_B), ("C", phase_C), ("D", phase_D),
                       ("M", phase_merge)):
            if nm in PHASES:
                fn(li)
    pg.finish()
    pg.emit()
    return B


def make_in_maps(inp, layers, cores):
    sh = host_layout(inp, layers)
    maps = []
    for b in cores:
        m = dict(sh)
        m["hT0"] = np.ascontiguousarray(np.concatenate([inp["x"][b].T, inp["ctx"][b].T], axis=1))
        cv = np.stack([inp["c"][b].reshape(KC, 128).T, inp["c_ctx"].reshape(KC, 128).T], axis=2)
        m["cvec"] = np.ascontiguousarray(cv.reshape(128, 32))
        maps.append(m)
    return maps


def kernel(**inputs):
    inp = {k: np.asarray(v) for k, v in inputs.items()}
    layers = list(range(DEPTH))
    B = build_program(DEPTH, [l == DEPTH - 1 for l in layers])
    maps = make_in_maps(inp, layers, list(range(8)))
    res = run_bass_kernel_spmd(B.nc, maps, core_ids=list(range(8)))
    out = np.stack([np.ascontiguousarray(r["OUT"].T) for r in res.results], axis=0)
    return out.astype(np.float32)
```

```python
import numpy as np
import ml_dtypes
import concourse.bass as bass
import concourse.mybir as mybir
from concourse.bass_utils import run_bass_kernel_spmd

F32 = mybir.dt.float32
BF16 = mybir.dt.bfloat16
AF = mybir.ActivationFunctionType
ALU = mybir.AluOpType

QUEUES = ("pe", "act", "dve", "pool", "sp")
COMPUTE = ("pe", "act", "dve", "pool")


class Res:
    __slots__ = ("name", "w", "r", "dsem")

    def __init__(self, name, dsem=None):
        self.name = name
        self.w = None
        self.r = {}
        self.dsem = dsem


class DmaSem:
    __slots__ = ("sem", "count", "last")

    def __init__(self, sem):
        self.sem = sem
        self.count = 0
        self.last = None


class Ins:
    __slots__ = ("q", "fn", "deps", "dma", "dsem", "dcount", "signal", "count", "epoch", "idx", "bar")

    def __init__(self, q, fn, dma, epoch):
        self.q = q
        self.fn = fn
        self.deps = set()
        self.dma = dma
        self.dsem = None
        self.dcount = 0
        self.signal = False
        self.count = 0
        self.epoch = epoch
        self.bar = None


class Prog:
    def __init__(self, nc, n_epochs, n_dma_sems):
        self.nc = nc
        self.ins = {q: [] for q in QUEUES}
        self.epoch = 0
        self.esem = {}
        for e in range(n_epochs):
            for q in COMPUTE:
                self.esem[(q, e)] = nc.alloc_semaphore(name=f"s_{q}_{e}")
        self.dsems = [DmaSem(nc.alloc_semaphore(name=f"d_{i}")) for i in range(n_dma_sems)]
        self.dsem_next = 0
        self.all = []
        self.pending_bar = {q: None for q in QUEUES}
        self.n_waits = 0

    def res(self, name, dma=False):
        d = None
        if dma:
            d = self.dsems[self.dsem_next % len(self.dsems)]
            self.dsem_next += 1
        return Res(name, d)

    def _rec(self, q, fn, reads, writes, dma=False, dsem=None):
        i = Ins(q, fn, dma, self.epoch)
        i.idx = len(self.all)
        self.all.append(i)
        if self.pending_bar[q] is not None:
            i.bar = self.pending_bar[q]
            self.pending_bar[q] = None
        for r in reads:
            if r.w is not None:
                i.deps.add(r.w)
        for w in writes:
            if w.w is not None:
                i.deps.add(w.w)
            for x in w.r.values():
                i.deps.add(x)
        if dma:
            i.dsem = dsem
            if dsem.last is not None:
                i.deps.add(dsem.last)
            dsem.count += 16
            i.dcount = dsem.count
            dsem.last = i
        for r in reads:
            key = ("dma", i.idx) if dma else q
            r.r[key] = i
        for w in writes:
            w.w = i
            w.r = {}
        i.deps.discard(i)
        self.ins[q].append(i)
        return i

    def op(self, q, fn, reads=(), writes=()):
        return self._rec(q, fn, list(reads), list(writes))

    def dma(self, q, out, in_, reads, writes, slot, **kw):
        def fn(eng, out=out, in_=in_, kw=kw):
            return eng.dma_start(out=out, in_=in_, **kw)
        return self._rec(q, fn, list(reads), list(writes), dma=True, dsem=slot.dsem)

    def barrier(self):
        snap = {"eng": {}, "dma": []}
        for q in COMPUTE:
            for x in reversed(self.ins[q]):
                if not x.dma and x.fn is not None:
                    snap["eng"][q] = x
                    break
        for d in self.dsems:
            if d.count:
                snap["dma"].append((d, d.count))
        for q in QUEUES:
            self.pending_bar[q] = snap

    def finish(self):
        self.barrier()
        for q in QUEUES:
            if self.ins[q]:
                self._rec(q, None, [], [])

    def emit(self):
        nc = self.nc
        for i in self.all:
            for p in i.deps:
                if p.dma:
                    continue
                if p.q == i.q and p.q == "pe":
                    continue
                p.signal = True
            if i.bar is not None:
                for q, last in i.bar["eng"].items():
                    last.signal = True
        for q in COMPUTE:
            cnt = {}
            for i in self.ins[q]:
                if i.signal and not i.dma and i.fn is not None:
                    cnt[i.epoch] = cnt.get(i.epoch, 0) + 1
                    i.count = cnt[i.epoch]
        engs = {"pe": "tensor", "act": "scalar", "dve": "vector", "pool": "gpsimd", "sp": "sync"}
        prog = self

        def run_queue(q, eng):
            waited = {}

            def wait(sem, val):
                k = id(sem)
                if waited.get(k, 0) >= val:
                    return
                waited[k] = val
                eng.wait_ge(sem, val)
                prog.n_waits += 1

            for i in prog.ins[q]:
                if i.bar is not None:
                    for pq, last in i.bar["eng"].items():
                        if pq == q and q == "pe":
                            continue
                        wait(prog.esem[(pq, last.epoch)], last.count)
                    for d, c in i.bar["dma"]:
                        wait(d.sem, c)
                for p in i.deps:
                    if p.dma:
                        wait(p.dsem.sem, p.dcount)
                    else:
                        if p.q == q and q == "pe":
                            continue
                        wait(prog.esem[(p.q, p.epoch)], p.count)
                if i.fn is None:
                    continue
                inst = i.fn(eng)
                if i.dma:
                    inst.then_inc(i.dsem.sem, 16)
                elif i.signal:
                    inst.then_inc(prog.esem[(q, i.epoch)], 1)

        with nc.Block() as block:
            for q in QUEUES:
                if not prog.ins[q]:
                    continue
                deco = getattr(block, engs[q])

                def body(eng, q=q):
                    run_queue(q, eng)
                deco(body)


D = 2048
KC = 16
TL = 4096
TCX = 256
T = TL + TCX
NT_TILES = T // 128
DEPTH = 4
EPS = 1e-6
NZ = 46

OFF = dict(a_u=0, a_v=512, a_z=1024, b_q=1536, b_k=2048, b_v=2176, b_z=2304,
           f_in=2816, f_z=3328, d_q=3840, d_k=4352, d_v=4864, d_z=5376)


def z_chunks():
    ch = []
    for g in range(4):
        ch.append(("gelu_bf", "GU", g, list(range(OFF["a_u"] + 128 * g, OFF["a_u"] + 128 * g + 128))))
    for g in range(4):
        ch.append(("gelu_f32", "AV", g, list(range(OFF["a_v"] + 128 * g, OFF["a_v"] + 128 * g + 128))))
    for bi, nm in enumerate(("a_z", "b_z", "f_z", "d_z")):
        for g in range(4):
            ch.append(("silu_bf", "SZ%d" % bi, g, list(range(OFF[nm] + 128 * g, OFF[nm] + 128 * g + 128))))
    for c in range(4):
        cols = list(range(OFF["b_q"] + 64 * c, OFF["b_q"] + 64 * c + 64)) + \
            list(range(OFF["b_q"] + 64 * (c + 4), OFF["b_q"] + 64 * (c + 4) + 64))
        ch.append(("id_f32", "BQ", c, cols))
    ch.append(("id_f32", "BK", 0, list(range(OFF["b_k"], OFF["b_k"] + 128))))
    ch.append(("id_bf", "BV", 0, list(range(OFF["b_v"], OFF["b_v"] + 128))))
    for g in range(4):
        ch.append(("id_bf", "FIN", g, list(range(OFF["f_in"] + 128 * g, OFF["f_in"] + 128 * g + 128))))
    for g in range(4):
        ch.append(("id_f32", "DQ", g, list(range(OFF["d_q"] + 128 * g, OFF["d_q"] + 128 * g + 128))))
    for g in range(4):
        ch.append(("id_f32", "DK", g, list(range(OFF["d_k"] + 128 * g, OFF["d_k"] + 128 * g + 128))))
    for g in range(4):
        ch.append(("id_bf", "DV", g, list(range(OFF["d_v"] + 128 * g, OFF["d_v"] + 128 * g + 128))))
    assert len(ch) == NZ
    return ch


ZCH = z_chunks()
PHASES = "WABCDM"
ARENA_WORDS = 44 * 1024
CAST_DMA_BYTES = 4096
WMODE = 0


def token_chunks():
    out = []
    for c in range(3):
        b = 1088 * c
        out.append([(b, 384, 0), (b + 384, 352, 0), (b + 736, 352, 0)])
    out.append([(3264, 416, 0), (3680, 416, 0), (4096, 256, 1)])
    return out


TCHUNKS = token_chunks()


def d_tiles(i):
    base = min(max(i - 2, 0), 27)
    if i == 0:
        pb = 0
    elif i == 1:
        pb = 5
    elif i == 30:
        pb = 15
    elif i == 31:
        pb = 20
    else:
        pb = 10
    return [(base + o, pb + o) for o in range(5)]


_CONST_CACHE = {}


def host_consts():
    if _CONST_CACHE:
        return _CONST_CACHE
    bf = ml_dtypes.bfloat16
    c = {}
    c["ident_f"] = np.eye(128, dtype=np.float32)
    c["ident_b"] = np.eye(128, dtype=np.float32).astype(bf)
    c["ones_b"] = np.ones((128, 128), np.float32).astype(bf)
    p = np.arange(128)
    c["blk64_b"] = (p[:, None] // 64 == p[None, :] // 64).astype(np.float32).astype(bf)
    c["blk32_b"] = (p[:, None] // 32 == p[None, :] // 32).astype(np.float32).astype(bf)
    partner = np.where((p % 32) < 16, p + 16, p - 16)
    perm = np.zeros((128, 128), np.float32)
    perm[partner, p] = 1.0
    c["perm_f"] = perm
    c["ones_row"] = np.ones((1, 128), np.float32)
    hm = np.zeros((128, 4, 128), np.float32)
    for j in range(4):
        hm[32 * j:32 * j + 32, j, :] = 1.0
    c["headmask"] = hm.reshape(128, 512).astype(bf)
    k = np.arange(128)[:, None]
    q = np.arange(128)[None, :]
    m_prev = (k >= q).astype(np.float32)
    m_next = (k <= q).astype(np.float32)
    bm = np.stack([np.tile(m_prev, (1, 4)), np.tile(m_next, (1, 4))], axis=1)
    c["bmask"] = bm.reshape(128, 1024).astype(bf)
    t = np.arange(TL)
    pos_r, pos_c = (t // 64).astype(np.float64), (t % 64).astype(np.float64)
    inv = 10000.0 ** (-np.arange(16, dtype=np.float64) / 16)
    d = p % 64
    pos = np.where((d < 32)[:, None], pos_r[None, :], pos_c[None, :])
    ang = pos * inv[d % 16][:, None]
    ang32 = (pos.astype(np.float32) * inv.astype(np.float32)[d % 16][:, None]).astype(np.float32)
    c["rcos"] = np.cos(ang32).astype(np.float32)
    sgn = np.where((d % 32) < 16, -1.0, 1.0)[:, None]
    c["rsin"] = (np.sin(ang32) * sgn).astype(np.float32)
    n = np.arange(TL, dtype=np.int64)
    prod = (n[:, None] * n[None, :]) % TL
    a = 2 * np.pi * prod.astype(np.float64) / TL
    c["CL"] = (np.cos(a) / 64.0).astype(np.float32).astype(bf)
    c["SL"] = (np.sin(a) / 64.0).astype(np.float32).astype(bf)
    n2 = np.arange(TCX, dtype=np.int64)
    a2 = 2 * np.pi * ((n2[:, None] * n2[None, :]) % TCX).astype(np.float64) / TCX
    c["C256"] = (np.cos(a2) / 16.0).astype(np.float32).astype(bf)
    c["S256"] = (np.sin(a2) / 16.0).astype(np.float32).astype(bf)
    n3 = np.arange(128, dtype=np.int64)
    a3 = 2 * np.pi * ((n3[:, None] * n3[None, :]) % 128).astype(np.float64) / 128
    c["CC"] = (np.cos(a3) / np.sqrt(128.0)).astype(np.float32)
    c["SC"] = (np.sin(a3) / np.sqrt(128.0)).astype(np.float32)
    _CONST_CACHE.update(c)
    return c


def rpb_gather_index():
    reps = [0, 1, 2, 30, 31]
    DR = np.zeros((25, 128, 128), np.int64)
    DC = np.zeros((25, 128, 128), np.int64)
    V = np.zeros((25, 128, 128), bool)
    kk = np.arange(128)[:, None]
    qq = np.arange(128)[None, :]
    for ri, i in enumerate(reps):
        base = min(max(i - 2, 0), 27)
        for o in range(5):
            kt = base + o
            kr = 2 * kt + kk // 64
            kcol = kk % 64
            qr = 2 * i + qq // 64
            qc = qq % 64
            r0 = np.clip(qr - 4, 0, 56)
            c0 = np.clip(qc - 8, 0, 48)
            valid = (kr >= r0) & (kr <= r0 + 7) & (kcol >= c0) & (kcol <= c0 + 15)
            dr = np.clip(kr - qr + 7, 0, 14)
            dc = np.clip(kcol - qc + 15, 0, 30)
            pid = ri * 5 + o
            DR[pid], DC[pid], V[pid] = np.broadcast_to(dr, (128, 128)), np.broadcast_to(dc, (128, 128)), valid
    return DR, DC, V


def host_layout(inp, layers):
    L = len(layers)
    sh = {}
    w_in = inp["w_in"]
    cols = np.concatenate([np.asarray(z[3]) for z in ZCH])
    wi = np.empty((L, NZ, 128, KC * 128), np.float32)
    for li, l in enumerate(layers):
        w = w_in[l][:, cols]
        wi[li] = w.reshape(KC, 128, NZ, 128).transpose(2, 1, 0, 3).reshape(NZ, 128, KC * 128)
    sh["w_in_b"] = wi
    wa = np.empty((L, 48, 128, KC * 128), np.float32)
    for li, l in enumerate(layers):
        wa[li] = inp["w_ada"][l].reshape(KC, 128, 48, 128).transpose(2, 1, 0, 3).reshape(48, 128, KC * 128)
    sh["w_ada_b"] = wa
    sh["b_ada_f"] = np.ascontiguousarray(inp["b_ada"][layers].reshape(L, 48, 128).transpose(0, 2, 1))
    sh["norm_g_f"] = np.ascontiguousarray(inp["norm_g"][layers].reshape(L, KC, 128).transpose(0, 2, 1))
    wg = np.empty((L, 4, 16, 128, KC * 128), np.float32)
    wb = np.empty((L, 4, 16, 128, 4 * 128), np.float32)
    wo = np.empty((L, 16, 128, KC * 128), np.float32)
    for li, l in enumerate(layers):
        for b in range(4):
            wg[li, b] = inp["w_gate"][l, b].reshape(KC, 128, 16, 128).transpose(2, 1, 0, 3).reshape(16, 128, KC * 128)
            wb[li, b] = inp["w_branch"][l, b].reshape(4, 128, 16, 128).transpose(2, 1, 0, 3).reshape(16, 128, 512)
        wo[li] = inp["w_out"][l].reshape(KC, 128, 16, 128).transpose(2, 1, 0, 3).reshape(16, 128, KC * 128)
    sh["w_gate_b"], sh["w_branch_b"], sh["w_out_b"] = wg, wb, wo
    sh["a_g_f"] = np.ascontiguousarray(inp["a_norm_g"][layers].reshape(L, 4, 128).transpose(0, 2, 1))
    sh["a_wsT"] = np.ascontiguousarray(inp["a_w_s"][layers].transpose(0, 3, 1, 2)).reshape(L, 128, 512)
    sh["a_bs"] = np.ascontiguousarray(inp["a_b_s"][layers].reshape(L, 1, 512))
    sh["b_qg"] = np.ascontiguousarray(np.tile(inp["b_q_g"][layers], (1, 2)).reshape(L, 128, 1))
    sh["b_kg"] = np.ascontiguousarray(np.tile(inp["b_k_g"][layers], (1, 2)).reshape(L, 128, 1))
    sh["b_sinkb"] = np.ascontiguousarray(np.broadcast_to(inp["b_sink"][layers][:, None, :], (L, 128, 8)))
    sh["c_wf"] = np.ascontiguousarray(inp["c_w_f"][layers].transpose(0, 2, 1, 3)).reshape(L, 128, 512)
    sh["d_qg"] = np.ascontiguousarray(np.tile(inp["d_q_g"][layers], (1, 4)).reshape(L, 128, 1))
    sh["d_kg"] = np.ascontiguousarray(np.tile(inp["d_k_g"][layers], (1, 4)).reshape(L, 128, 1))
    DR, DC, V = rpb_gather_index()
    rp = inp["d_rpb"][layers]
    g = rp[:, :, DR, DC]
    g = np.where(V[None, None], g, np.float32(-100.0)).astype(np.float32)
    g = g.reshape(L, 4, 4, 25, 128, 128).transpose(0, 3, 1, 4, 2, 5)
    sh["rpbg"] = np.ascontiguousarray(g).reshape(L, 25, 4, 128, 512)
    sh.update(host_consts())
    return sh


class Builder:
    def __init__(self, n_layers, last_flags, debug=False):
        self.L = n_layers
        self.last_flags = last_flags
        self.debug = debug
        nc = bass.Bass("TRN2", target_bir_lowering=False)
        self.nc = nc
        self.pg = Prog(nc, n_layers + 1, 70)
        self.arena = nc.alloc_sbuf_tensor("arena", [128, ARENA_WORDS], F32)
        self.persist_off = 0
        self.off = 0
        self.ps = [nc.alloc_psum_tensor(f"ps{i}", [128, 512], F32) for i in range(8)]
        self.rps = [self.pg.res(f"ps{i}") for i in range(8)]
        self.dram = {}
        self.dres = {}

    def din(self, name, shape, dt=F32):
        t = self.nc.dram_tensor(name, list(shape), dt, kind="ExternalInput").ap()
        self.dram[name] = t
        return t

    def dscr(self, name, shape, dt, out=False):
        kind = "ExternalOutput" if (out or self.debug) else "Internal"
        t = self.nc.dram_tensor(name, list(shape), dt, kind=kind).ap()
        self.dram[name] = t
        return t

    def dr(self, key):
        r = self.dres.get(key)
        if r is None:
            r = self.pg.res(str(key))
            self.dres[key] = r
        return r

    def alloc(self, n, dt=F32, dma=False, name="t"):
        words = n if dt == F32 else (n + 1) // 2
        words = (words + 1) // 2 * 2
        assert self.off + words <= ARENA_WORDS, f"SBUF arena overflow at {name}: {self.off}+{words}"
        a = self.arena[:, self.off:self.off + words]
        self.off += words
        if dt != F32:
            a = a.bitcast(dt)[:, 0:n]
        return a, self.pg.res(name, dma=dma)

    def ring(self, k, n, dt=F32, dma=False, name="r"):
        return Ring([self.alloc(n, dt, dma, f"{name}{i}") for i in range(k)])

    def phase_reset(self):
        self.pg.barrier()
        self.off = self.persist_off

    def mm(self, out, lhsT, rhs, start, stop, rd, wr):
        self.pg.op("pe", lambda e: e.matmul(out, lhsT=lhsT, rhs=rhs, start=start, stop=stop), rd, wr)

    def tr(self, out, in_, ident, rd, wr):
        self.pg.op("pe", lambda e: e.transpose(out=out, in_=in_, identity=ident), rd, wr)

    def act(self, out, in_, func, rd, wr, scale=1.0, bias=0.0):
        self.pg.op("act", lambda e: e.activation(out=out, in_=in_, func=func, scale=scale, bias=bias), rd, wr)

    def tt(self, q, out, in0, in1, op, rd, wr):
        self.pg.op(q, lambda e: e.tensor_tensor(out=out, in0=in0, in1=in1, op=op), rd, wr)

    def ts(self, q, out, in0, s1, op0, rd, wr, s2=None, op1=None):
        if op1 is None:
            self.pg.op(q, lambda e: e.tensor_scalar(out=out, in0=in0, scalar1=s1, scalar2=None, op0=op0), rd, wr)
        else:
            self.pg.op(q, lambda e: e.tensor_scalar(out=out, in0=in0, scalar1=s1, scalar2=s2, op0=op0, op1=op1), rd, wr)

    def stt(self, out, in0, scalar, in1, op0, op1, rd, wr):
        self.pg.op("dve", lambda e: e.scalar_tensor_tensor(out=out, in0=in0, scalar=scalar, in1=in1, op0=op0, op1=op1), rd, wr)

    def cp(self, q, out, in_, rd, wr):
        self.pg.op(q, lambda e: e.tensor_copy(out=out, in_=in_), rd, wr)

    def recip(self, out, in_, rd, wr):
        self.pg.op("dve", lambda e: e.reciprocal(out=out, in_=in_), rd, wr)

    def memset(self, q, ap, val, wr):
        self.pg.op(q, lambda e: e.memset(ap, val), [], wr)

    def load(self, out, in_, slot, rd=(), q="sp"):
        if q == "pool":
            self.pg.dma(q, out, in_, rd, [slot], slot, max_dma_last_dim=CAST_DMA_BYTES)
        else:
            self.pg.dma(q, out, in_, rd, [slot], slot)

    def store(self, out, in_, slot, wr=(), q="sp"):
        self.pg.dma(q, out, in_, [slot], wr, slot)


class Ring:
    def __init__(self, items):
        self.items = items
        self.i = 0

    def next(self):
        it = self.items[self.i % len(self.items)]
        self.i += 1
        return it


def build_program(n_layers, last_flags, debug=False):
    B = Builder(n_layers, last_flags, debug)
    nc, pg, ps, rps = B.nc, B.pg, B.ps, B.rps
    L = n_layers
    MUL, ADD = ALU.mult, ALU.add

    hT0 = B.din("hT0", [D, T])
    cvec_d = B.din("cvec", [128, 32])
    w_ada_b = B.din("w_ada_b", [L, 48, 128, 2048])
    b_ada_f = B.din("b_ada_f", [L, 128, 48])
    norm_g_f = B.din("norm_g_f", [L, 128, 16])
    w_in_b = B.din("w_in_b", [L, NZ, 128, 2048])
    w_gate_b = B.din("w_gate_b", [L, 4, 16, 128, 2048])
    w_branch_b = B.din("w_branch_b", [L, 4, 16, 128, 512])
    w_out_b = B.din("w_out_b", [L, 16, 128, 2048])
    a_g_f = B.din("a_g_f", [L, 128, 4])
    a_wsT = B.din("a_wsT", [L, 128, 512])
    a_bs = B.din("a_bs", [L, 1, 512])
    b_qg = B.din("b_qg", [L, 128, 1])
    b_kg = B.din("b_kg", [L, 128, 1])
    b_sinkb = B.din("b_sinkb", [L, 128, 8])
    c_wf = B.din("c_wf", [L, 128, 512])
    d_qg = B.din("d_qg", [L, 128, 1])
    d_kg = B.din("d_kg", [L, 128, 1])
    rpbg = B.din("rpbg", [L, 25, 4, 128, 512])
    cd = {}
    for nm, shp, dt in (("ident_f", [128, 128], F32), ("ident_b", [128, 128], BF16), ("ones_b", [128, 128], BF16),
                        ("blk64_b", [128, 128], BF16), ("blk32_b", [128, 128], BF16), ("perm_f", [128, 128], F32),
                        ("ones_row", [1, 128], F32), ("headmask", [128, 512], BF16), ("bmask", [128, 1024], BF16),
                        ("rcos", [128, TL], F32), ("rsin", [128, TL], F32), ("CL", [TL, TL], BF16), ("SL", [TL, TL], BF16),
                        ("C256", [TCX, TCX], BF16), ("S256", [TCX, TCX], BF16), ("CC", [128, 128], F32), ("SC", [128, 128], F32)):
        cd[nm] = B.din(nm, shp, dt)

    OUT = B.dscr("OUT", [D, TL], F32, out=True)
    H = B.dscr("H", [D, T], F32)
    NT = B.dscr("NT", [D, T], BF16)
    GU = B.dscr("GU", [512, T], BF16)
    AV = B.dscr("AV", [512, T], F32)
    SZ = [B.dscr("SZ%d" % i, [512, T], BF16) for i in range(4)]
    BQ = B.dscr("BQ", [512, T], F32)
    BK = B.dscr("BK", [128, T], F32)
    BV = B.dscr("BV", [128, T], BF16)
    FIN = B.dscr("FIN", [512, T], BF16)
    DQ = B.dscr("DQ", [512, T], F32)
    DK = B.dscr("DK", [512, T], F32)
    DV = B.dscr("DV", [512, T], BF16)
    YG = [B.dscr("YG%d" % i, [512, T], BF16) for i in range(4)]
    ET = B.dscr("ET", [25, 4, 128, 512], BF16)
    ZD = dict(GU=GU, AV=AV, SZ0=SZ[0], SZ1=SZ[1], SZ2=SZ[2], SZ3=SZ[3], BQ=BQ, BK=BK, BV=BV, FIN=FIN, DQ=DQ, DK=DK, DV=DV)

    def cst(name, src, n, dt=F32, parts=128):
        a, r = B.alloc(n, dt, dma=True, name=name)
        a = a[0:parts, :]
        B.load(a, src, r)
        return a, r

    ident_f, r_identf = cst("ident_f", cd["ident_f"], 128)
    ident_b, r_identb = cst("ident_b", cd["ident_b"], 128, BF16)
    ones_b, r_onesb = cst("ones_b", cd["ones_b"], 128, BF16)
    blk64, r_blk64 = cst("blk64", cd["blk64_b"], 128, BF16)
    blk32, r_blk32 = cst("blk32", cd["blk32_b"], 128, BF16)
    perm_f, r_perm = cst("perm_f", cd["perm_f"], 128)
    ones_row, r_onesrow = cst("ones_row", cd["ones_row"], 128, F32, parts=1)
    headmask, r_hm = cst("headmask", cd["headmask"], 512, BF16)
    bmask, r_bm = cst("bmask", cd["bmask"], 1024, BF16)
    CCs, r_CC = cst("CC", cd["CC"], 128)
    SCs, r_SC = cst("SC", cd["SC"], 128)
    C256s, r_C256 = cst("C256", cd["C256"].rearrange("(s p) t -> p s t", p=128), 512, BF16)
    S256s, r_S256 = cst("S256", cd["S256"].rearrange("(s p) t -> p s t", p=128), 512, BF16)
    cvec, r_cvec = cst("cvec", cvec_d, 32)
    bada, r_bada = cst("bada", b_ada_f.rearrange("l p o -> p l o"), L * 48)
    MT, r_MT = B.alloc(L * 96, F32, name="MT")
    MTv = MT.rearrange("p (l o j) -> p l o j", l=L, o=48)
    normg, r_normg = B.alloc(16, F32, dma=True, name="normg")
    Gm, r_G = B.alloc(32, F32, name="G")
    Gv = Gm.rearrange("p (k j) -> p k j", j=2)
    gvA, r_gvA = B.alloc(4, F32, dma=True, name="gvA")
    wsTf, r_wsTf = B.alloc(512, F32, dma=True, name="wsTf")
    wsTb, r_wsTb = B.alloc(512, BF16, name="wsTb")
    bsA, r_bsA = B.alloc(512, F32, dma=True, name="bsA")
    bsA = bsA[0:1, :]
    gqB, r_gqB = B.alloc(2, F32, dma=True, name="gqB")
    gkB, r_gkB = B.alloc(2, F32, dma=True, name="gkB")
    esink, r_esink = B.alloc(8, F32, dma=True, name="esink")
    wfC, r_wfC = B.alloc(512, F32, dma=True, name="wfC")
    gqD, r_gqD = B.alloc(2, F32, dma=True, name="gqD")
    gkD, r_gkD = B.alloc(2, F32, dma=True, name="gkD")
    B.persist_off = B.off

    def phase_adaln():
        sc_t, r_sc = B.alloc(32, F32, name="silu_c")
        B.act(sc_t, cvec, AF.Silu, [r_cvec], [r_sc])
        scv = sc_t.rearrange("p (k j) -> p k j", j=2)
        ring = B.ring(3, 2048, F32, dma=True, name="wada")
        pend = []

        def wl(l, oc):
            w, r = ring.next()
            B.load(w, w_ada_b[l, oc], r)
            return w, r
        todo = [(l, oc) for l in range(L) for oc in range(48)]
        for x in todo[:2]:
            pend.append(wl(*x))
        for i, (l, oc) in enumerate(todo):
            if i + 2 < len(todo):
                pend.append(wl(*todo[i + 2]))
            w, r_w = pend.pop(0)
            bank = i % 2
            for kc in range(KC):
                B.mm(ps[bank][:, 0:2], w[:, kc * 128:(kc + 1) * 128], scv[:, kc, :], kc == 0, kc == KC - 1,
                     [r_w, r_sc], [rps[bank]])
            B.act(MTv[:, l, oc, :], ps[bank][:, 0:2], AF.Identity, [rps[bank], r_bada], [r_MT],
                  bias=bada[:, l * 48 + oc:l * 48 + oc + 1])

    def layer_params(li):
        B.load(normg, norm_g_f[li], r_normg)
        B.load(gvA, a_g_f[li], r_gvA)
        B.load(wsTf, a_wsT[li], r_wsTf)
        B.load(bsA, a_bs[li], r_bsA)
        B.load(gqB[:, 0:1], b_qg[li], r_gqB)
        B.load(gkB[:, 0:1], b_kg[li], r_gkB)
        B.load(esink, b_sinkb[li], r_esink)
        B.load(wfC, c_wf[li], r_wfC)
        B.load(gqD[:, 0:1], d_qg[li], r_gqD)
        B.load(gkD[:, 0:1], d_kg[li], r_gkD)
        B.cp("dve", wsTb, wsTf, [r_wsTf], [r_wsTb])
        B.act(esink, esink, AF.Exp, [r_esink], [r_esink])
        B.ts("dve", Gv, MTv[:, li, 16:32, :], 1.0, ADD, [r_MT], [r_G])
        B.tt("dve", Gv, Gv, normg.unsqueeze(2).to_broadcast([128, 16, 2]), MUL, [r_G, r_normg], [r_G])

    def phase_norm_win(li):
        hsrc = hT0 if li == 0 else H
        hsv = hsrc.rearrange("(k p) t -> p k t", p=128)
        NTv = NT.rearrange("(k p) t -> p k t", p=128)
        B.phase_reset()
        nTs = [[B.alloc(16 * 416, BF16, dma=True, name=f"nT{a}_{i}") for i in range(3)] for a in range(2)]
        hring = B.ring(1, 16 * 416, F32, dma=True, name="h")
        sq, r_sq = B.alloc(16 * 416, BF16, name="sq")
        rstd, r_rstd = B.alloc(416, F32, name="rstd")
        wring = B.ring(4, 2048, BF16, dma=True, name="w")
        stg = B.ring(3, 1248, F32, dma=True, name="stg")
        chunks = TCHUNKS[:1] if WMODE else TCHUNKS

        def norm_pieces(ci):
            subs = chunks[ci]
            pieces = []
            for si, (t0, n, j) in enumerate(subs):
                nT, r_nT = nTs[ci % 2][si]
                nTv = nT.rearrange("p (k t) -> p k t", k=16)[:, :, 0:n]
                sqv = sq.rearrange("p (k t) -> p k t", k=16)[:, :, 0:n]
                box = {}

                def p_load(t0=t0, n=n, box=box):
                    hs, r_hs = hring.next()
                    hv = hs.rearrange("p (k t) -> p k t", k=16)[:, :, 0:n]
                    B.load(hv, hsv[:, :, t0:t0 + n], r_hs)
                    box["hv"], box["r"] = hv, r_hs

                def p_stat(n=n, box=box, sqv=sqv):
                    hv, r_hs = box["hv"], box["r"]
                    B.act(sqv, hv, AF.Square, [r_hs], [r_sq])
                    for kc in range(KC):
                        B.mm(ps[6][:, 0:n], ones_b, sqv[:, kc, :], kc == 0, kc == KC - 1, [r_sq, r_onesb], [rps[6]])
                    B.act(rstd[:, 0:n], ps[6][:, 0:n], AF.Sqrt, [rps[6]], [r_rstd], scale=1.0 / D, bias=EPS)
                    B.recip(rstd[:, 0:n], rstd[:, 0:n], [r_rstd], [r_rstd])
                    B.tt("dve", hv, hv, rstd[:, 0:n].unsqueeze(1).to_broadcast([128, 16, n]), MUL, [r_hs, r_rstd], [r_hs])

                def p_mod(k0, j=j, box=box, nTv=nTv, r_nT=r_nT):
                    hv, r_hs = box["hv"], box["r"]
                    for kc in range(k0, k0 + 4):
                        B.act(nTv[:, kc, :], hv[:, kc, :], AF.Identity, [r_hs, r_G, r_MT], [r_nT],
                              scale=Gv[:, kc, j:j + 1], bias=MTv[:, li, kc, j:j + 1])

                def p_store(t0=t0, n=n, nTv=nTv, r_nT=r_nT):
                    B.store(NTv[:, :, t0:t0 + n], nTv, r_nT)
                pieces += [p_load, p_stat, (lambda f=p_mod: f(0)), (lambda f=p_mod: f(4)), (lambda f=p_mod: f(8)),
                           (lambda f=p_mod: f(12)), p_store]
            return pieces

        for p in norm_pieces(0):
            p()
        for ci, subs in enumerate(chunks):
            tbase = subs[0][0]
            ntot = sum(s_[1] for s_ in subs)
            nxt = norm_pieces(ci + 1) if ci + 1 < len(chunks) else []
            pend = []

            def wl(oc):
                w, r = wring.next()
                B.load(w, w_in_b[li, oc], r, q="pool")
                return w, r
            pend.append(wl(0))
            pend.append(wl(1))
            for oc in range(NZ if WMODE < 2 else WMODE):
                if oc + 2 < NZ:
                    pend.append(wl(oc + 2))
                w, r_w = pend.pop(0)
                kind, dest, dch, _ = ZCH[oc]
                st, r_st = stg.next()
                is_bf = kind.endswith("_bf")
                stv = st.bitcast(BF16) if is_bf else st
                off = 0
                for si, (t0, n, j) in enumerate(subs):
                    nT, r_nT = nTs[ci % 2][si]
                    nTv = nT.rearrange("p (k t) -> p k t", k=16)[:, :, 0:n]
                    bank = (oc * 3 + si) % 6
                    for kc in range(KC):
                        B.mm(ps[bank][:, 0:n], w[:, kc * 128:(kc + 1) * 128], nTv[:, kc, :], kc == 0, kc == KC - 1,
                             [r_w, r_nT], [rps[bank]])
                    o = stv[:, off:off + n]
                    if kind.startswith("gelu"):
                        B.act(o, ps[bank][:, 0:n], AF.Gelu_apprx_tanh, [rps[bank]], [r_st])
                    elif kind.startswith("silu"):
                        B.act(o, ps[bank][:, 0:n], AF.Silu, [rps[bank]], [r_st])
                    else:
                        B.cp("dve", o, ps[bank][:, 0:n], [rps[bank]], [r_st])
                    off += n
                B.store(ZD[dest][dch * 128:(dch + 1) * 128, tbase:tbase + ntot], stv[:, 0:ntot], r_st)
                if oc >= 8 and nxt:
                    nxt.pop(0)()
            while nxt:
                nxt.pop(0)()

    def phase_A(li):
        last = last_flags[li]
        B.phase_reset()
        avr = B.ring(2, 2048, F32, dma=True, name="av")
        gur = B.ring(2, 2048, BF16, dma=True, name="gu")
        szr = B.ring(2, 2048, BF16, dma=True, name="sz")
        ygr = B.ring(2, 2048, BF16, dma=True, name="yg")
        sqr = B.ring(2, 2048, BF16, name="sq")
        rsr = B.ring(2, 512, F32, name="rstd")
        vnr = B.ring(2, 2048, BF16, name="vn")
        vtr = B.ring(2, 2048, BF16, name="vtok")
        t1r = B.ring(2, 2048, F32, name="t1")
        AVv = AV.rearrange("(g p) t -> p g t", p=128)
        GUv = GU.rearrange("(g p) t -> p g t", p=128)
        SZv = SZ[0].rearrange("(g p) t -> p g t", p=128)
        YGv = YG[0].rearrange("(g p) t -> p g t", p=128)
        nch = 8 if last else 9

        def s1(ci):
            t0 = ci * 512
            n = 512 if ci < 8 else 256
            ntt = n // 128
            av, r_av = avr.next()
            gu, r_gu = gur.next()
            sz, r_sz = szr.next()
            sq, r_sq = sqr.next()
            rstd, r_rstd = rsr.next()
            vn, r_vn = vnr.next()
            vtok, r_vtok = vtr.next()
            t1, r_t1 = t1r.next()
            v3 = lambda a: a.rearrange("p (g t) -> p g t", g=4)[:, :, 0:n]
            avv, guv, szv, sqv, vnv, t1v = v3(av), v3(gu), v3(sz), v3(sq), v3(vn), v3(t1)
            B.load(avv, AVv[:, :, t0:t0 + n], r_av)
            B.load(guv, GUv[:, :, t0:t0 + n], r_gu)
            B.load(szv, SZv[:, :, t0:t0 + n], r_sz)
            B.act(sqv, avv, AF.Square, [r_av], [r_sq])
            for g in range(4):
                B.mm(ps[0][:, 0:n], ones_b, sqv[:, g, :], g == 0, g == 3, [r_sq, r_onesb], [rps[0]])
            B.act(rstd[:, 0:n], ps[0][:, 0:n], AF.Sqrt, [rps[0]], [r_rstd], scale=1.0 / 512, bias=EPS)
            B.recip(rstd[:, 0:n], rstd[:, 0:n], [r_rstd], [r_rstd])
            for g in range(4):
                B.stt(vnv[:, g, :], avv[:, g, :], gvA[:, g:g + 1], rstd[:, 0:n], MUL, MUL, [r_av, r_rstd, r_gvA], [r_vn])
            vtv = vtok.rearrange("p (t c) -> p t c", t=4)
            for tt in range(ntt):
                bank = 1 + tt // 2
                psb = ps[bank].bitcast(BF16)
                for g in range(4):
                    c0 = (tt % 2) * 512 + g * 128
                    B.tr(psb[:, c0:c0 + 128], vnv[:, g, tt * 128:(tt + 1) * 128], ident_b, [r_vn, r_identb], [rps[bank]])
                if tt % 2 == 0:
                    B.act(vtv[:, tt, :], psb[:, 0:512], AF.Identity, [rps[bank]], [r_vtok])
                else:
                    B.cp("dve", vtv[:, tt, :], psb[:, 512:1024], [rps[bank]], [r_vtok])
            B.tt("pool", t1v, guv, szv, MUL, [r_gu, r_sz], [r_t1])
            return (t0, n, ntt, vtv, r_vtok, t1v, r_t1)

        def s2(ctx):
            t0, n, ntt, vtv, r_vtok, t1v, r_t1 = ctx
            yg, r_yg = ygr.next()
            ygv = yg.rearrange("p (g t) -> p g t", g=4)[:, :, 0:n]
            for g in range(4):
                bank = 3 + g
                for tt in range(ntt):
                    o = ps[bank][:, tt * 128:(tt + 1) * 128]
                    B.mm(o, vtv[:, tt, g * 128:(g + 1) * 128], wsTb[:, g * 128:(g + 1) * 128], True, False,
                         [r_vtok, r_wsTb], [rps[bank]])
                    B.mm(o, ones_row, bsA[:, g * 128:(g + 1) * 128], False, True, [r_onesrow, r_bsA], [rps[bank]])
                B.tt("dve", ygv[:, g, :], ps[bank][:, 0:n], t1v[:, g, :], MUL, [rps[bank], r_t1], [r_yg])
            B.store(YGv[:, :, t0:t0 + n], ygv, r_yg)

        prev = s1(0)
        for ci in range(nch):
            nxt = s1(ci + 1) if ci + 1 < nch else None
            s2(prev)
            prev = nxt

    def phase_B(li):
        B.phase_reset()
        QT, r_QT = B.alloc(4 * T, BF16, name="QT")
        KT, r_KT = B.alloc(T, BF16, name="KT")
        VA, r_VA = B.alloc(NT_TILES * 130, BF16, name="VA")
        QTv = QT.rearrange("p (c t) -> p c t", c=4)
        VAv = VA.rearrange("p (n j e) -> p n j e", n=NT_TILES, j=2)
        szbr = B.ring(2, 2048, BF16, dma=True, name="SZb")
        SZ1v = SZ[1].rearrange("(c p) t -> p c t", p=128)
        B.memset("pool", VA, 1.0, [r_VA])
        bqr = B.ring(2, 2048, F32, dma=True, name="bq")
        bkr = B.ring(2, 512, F32, dma=True, name="bk")
        bvr = B.ring(2, 512, BF16, dma=True, name="bv")
        cosr = B.ring(2, 512, F32, dma=True, name="cos")
        sinr = B.ring(2, 512, F32, dma=True, name="sin")
        sqr = B.ring(2, 512, BF16, name="sq")
        rsr = B.ring(2, 512, F32, name="rs")
        qnr = B.ring(2, 512, F32, name="qn")
        t1r = B.ring(2, 512, F32, name="t1")
        t2r = B.ring(2, 512, F32, name="t2")
        BQv = BQ.rearrange("(c p) t -> p c t", p=128)
        pcnt = dict(k=0)

        def pbank():
            b_ = 6 + pcnt["k"] % 2
            pcnt["k"] += 1
            return b_

        def prep_pieces(ci):
            t0 = ci * 512
            n = 512 if ci < 8 else 256
            lat = ci < 8
            box = {}
            pieces = []

            def p_load():
                bq, r_bq = bqr.next()
                bk, r_bk = bkr.next()
                bv, r_bv = bvr.next()
                bqv = bq.rearrange("p (c t) -> p c t", c=4)[:, :, 0:n]
                B.load(bqv, BQv[:, :, t0:t0 + n], r_bq)
                B.load(bk[:, 0:n], BK[:, t0:t0 + n], r_bk)
                B.load(bv[:, 0:n], BV[:, t0:t0 + n], r_bv)
                box.update(bqv=bqv, r_bq=r_bq, bk=bk, r_bk=r_bk, bv=bv, r_bv=r_bv)
                if lat:
                    cs, r_cs = cosr.next()
                    sn, r_sn = sinr.next()
                    B.load(cs, cd["rcos"][:, t0:t0 + n], r_cs)
                    B.load(sn, cd["rsin"][:, t0:t0 + n], r_sn)
                    box.update(cs=cs, r_cs=r_cs, sn=sn, r_sn=r_sn)
            pieces.append(p_load)
            for c5 in range(5):
                def p_chain(c5=c5):
                    x = box["bqv"][:, c5, :] if c5 < 4 else box["bk"][:, 0:n]
                    r_x = box["r_bq"] if c5 < 4 else box["r_bk"]
                    g, r_g = (gqB, r_gqB) if c5 < 4 else (gkB, r_gkB)
                    dest = QTv[:, c5, t0:t0 + n] if c5 < 4 else KT[:, t0:t0 + n]
                    r_dest = r_QT if c5 < 4 else r_KT
                    sq, r_sq = sqr.next()
                    rs, r_rs = rsr.next()
                    bank = pbank()
                    B.tt("pool", sq[:, 0:n], x, x, MUL, [r_x], [r_sq])
                    B.mm(ps[bank][:, 0:n], blk64, sq[:, 0:n], True, True, [r_sq, r_blk64], [rps[bank]])
                    B.act(rs[:, 0:n], ps[bank][:, 0:n], AF.Ln, [rps[bank]], [r_rs], scale=1.0 / 64, bias=EPS)
                    B.act(rs[:, 0:n], rs[:, 0:n], AF.Exp, [r_rs], [r_rs], scale=-0.5)
                    if lat:
                        qn, r_qn = qnr.next()
                        B.stt(qn[:, 0:n], x, g[:, 0:1], rs[:, 0:n], MUL, MUL, [r_x, r_g, r_rs], [r_qn])
                        bank2 = pbank()
                        B.mm(ps[bank2][:, 0:n], perm_f, qn[:, 0:n], True, True, [r_qn, r_perm], [rps[bank2]])
                        t1, r_t1 = t1r.next()
                        t2, r_t2 = t2r.next()
                        B.tt("dve", t1[:, 0:n], ps[bank2][:, 0:n], box["sn"][:, 0:n], MUL, [rps[bank2], box["r_sn"]], [r_t1])
                        B.tt("pool", t2[:, 0:n], qn[:, 0:n], box["cs"][:, 0:n], MUL, [r_qn, box["r_cs"]], [r_t2])
                        B.tt("dve", dest, t1[:, 0:n], t2[:, 0:n], ADD, [r_t1, r_t2], [r_dest])
                    else:
                        B.stt(dest, x, g[:, 0:1], rs[:, 0:n], MUL, MUL, [r_x, r_g, r_rs], [r_dest])
                pieces.append(p_chain)

            def p_v():
                bv, r_bv = box["bv"], box["r_bv"]
                for tt in range(n // 128):
                    tile = ci * 4 + tt
                    bank = pbank()
                    psb = ps[bank].bitcast(BF16)
                    B.tr(psb[:, 0:128], bv[:, tt * 128:(tt + 1) * 128], ident_b, [r_bv, r_identb], [rps[bank]])
                    B.cp("dve", VAv[:, tile, :, 0:64], psb[:, 0:128].rearrange("p (j e) -> p j e", j=2), [rps[bank]], [r_VA])
            pieces.append(p_v)
            return pieces

        allp = {ci: prep_pieces(ci) for ci in range(9)}
        for ci in (8, 0, 1):
            for p in allp[ci]:
                p()
        allp[2][0]()
        pq = []
        for ci in range(2, 8):
            if ci + 1 < 8:
                pq.append((ci, allp[ci + 1][0]))
            pq += [(ci, p) for p in allp[ci][1:]]

        def ensure(chunk):
            while pq and pq[0][0] <= chunk:
                pq.pop(0)[1]()

        pring = B.ring(20, 512, BF16, name="P")
        onr = B.ring(2, 512, F32, name="On")
        denr = B.ring(2, 8, F32, name="den")
        ygr = B.ring(2, 2048, BF16, dma=True, name="yg")
        bmv = bmask.rearrange("p (m f) -> p m f", m=2)
        YGv = YG[1].rearrange("(c p) t -> p c t", p=128)
        st = dict(sb=0, ob=0, yg=None, r_yg=None, SZbv=None, r_SZb=None, on=None, r_on=None)

        def s1(qb, j):
            lat = qb < 32
            if lat:
                tiles = ([qb - 1] if qb > 0 else []) + [qb] + ([qb + 1] if qb < 31 else []) + [32, 33]
            else:
                tiles = [32, 33]
            Ps = []
            for kt in tiles:
                bank = st["sb"] % 4
                st["sb"] += 1
                B.mm(ps[bank][:], KT[64 * j:64 * j + 64, kt * 128:(kt + 1) * 128],
                     QTv[64 * j:64 * j + 64, :, qb * 128:(qb + 1) * 128], True, True, [r_KT, r_QT], [rps[bank]])
                Pt, r_P = pring.next()
                B.act(Pt, ps[bank][:], AF.Exp, [rps[bank]], [r_P], scale=0.125)
                if lat and kt == qb - 1:
                    B.tt("dve", Pt, Pt, bmv[:, 0, :], MUL, [r_P, r_bm], [r_P])
                if lat and kt == qb + 1:
                    B.tt("dve", Pt, Pt, bmv[:, 1, :], MUL, [r_P, r_bm], [r_P])
                Ps.append((Pt, r_P, kt))
            return Ps

        def s2(qb, j, Ps):
            if j == 0:
                if qb % 4 == 0:
                    st["yg"], st["r_yg"] = ygr.next()
                    SZb, r_SZb = szbr.next()
                    SZbv = SZb.rearrange("p (c t) -> p c t", c=4)
                    nld = min(512, T - qb * 128)
                    B.load(SZbv[:, :, 0:nld], SZ1v[:, :, qb * 128:qb * 128 + nld], r_SZb)
                    st["SZbv"], st["r_SZb"] = SZbv, r_SZb
                st["on"], st["r_on"] = onr.next()
            on, r_on = st["on"], st["r_on"]
            onv = on.rearrange("p (h e) -> p h e", h=8)
            obank = 4 + st["ob"] % 2
            st["ob"] += 1
            Ov = ps[obank][:, 0:260].rearrange("p (h e) -> p h e", e=65)
            for hh in range(4):
                for ti, (Pt, r_P, kt) in enumerate(Ps):
                    B.mm(ps[obank][:, hh * 65:(hh + 1) * 65], Pt[:, hh * 128:(hh + 1) * 128], VAv[:, kt, j, :],
                         ti == 0, ti == len(Ps) - 1, [r_P, r_VA], [rps[obank]])
            den, r_den = denr.next()
            B.tt("dve", den[:, 0:4], Ov[:, :, 64], esink[:, 4 * j:4 * j + 4], ADD, [rps[obank], r_esink], [r_den])
            B.recip(den[:, 0:4], den[:, 0:4], [r_den], [r_den])
            B.tt("dve", onv[:, 4 * j:4 * j + 4, :], Ov[:, :, 0:64], den[:, 0:4].unsqueeze(2).to_broadcast([128, 4, 64]),
                 MUL, [rps[obank], r_den], [r_on])
            if j == 1:
                yg, r_yg = st["yg"], st["r_yg"]
                ygv = yg.rearrange("p (c t) -> p c t", c=4)
                tb = pbank()
                for cc in range(4):
                    B.tr(ps[tb][:, cc * 128:(cc + 1) * 128], on[:, cc * 128:(cc + 1) * 128], ident_f, [r_on, r_identf], [rps[tb]])
                qq = qb % 4
                B.tt("dve", ygv[:, :, qq * 128:(qq + 1) * 128], ps[tb][:].rearrange("p (c t) -> p c t", c=4),
                     st["SZbv"][:, :, qq * 128:(qq + 1) * 128], MUL, [rps[tb], st["r_SZb"]], [r_yg])
                if qb % 4 == 3 or qb == NT_TILES - 1:
                    t0 = (qb // 4) * 512
                    n = (qq + 1) * 128
                    B.store(YGv[:, :, t0:t0 + n], ygv[:, :, 0:n], r_yg)

        nqb = 32 if last_flags[li] else NT_TILES
        units = [(qb, j) for qb in range(nqb) for j in range(2)]
        prev = s1(*units[0])
        for ui, (qb, j) in enumerate(units):
            if ui + 1 < len(units):
                nq = units[ui + 1][0]
                ensure(min(7, (nq + 1) // 4))
                nxt = s1(*units[ui + 1])
            else:
                nxt = None
            s2(qb, j, prev)
            if pq:
                pq.pop(0)[1]()
            prev = nxt
        ensure(99)

    def phase_C(li):
        B.phase_reset()
        Wcs, r_Wcs = B.alloc(1024, BF16, name="Wcs")
        Wv = Wcs.rearrange("p (g d) -> p g d", g=4)
        for g in range(4):
            B.mm(ps[0][:, 0:128], CCs, wfC[:, g * 128:(g + 1) * 128], True, True, [r_CC, r_wfC], [rps[0]])
            B.mm(ps[1][:, 0:128], SCs, wfC[:, g * 128:(g + 1) * 128], True, True, [r_SC, r_wfC], [rps[1]])
            B.cp("dve", Wv[:, g, 0:128], ps[0][:, 0:128], [rps[0]], [r_Wcs])
            B.act(Wv[:, g, 128:256], ps[1][:, 0:128], AF.Identity, [rps[1]], [r_Wcs], scale=-1.0)
        Yall, r_Y = B.alloc(NT_TILES * 1024, BF16, name="Yall")
        Yv = Yall.rearrange("p (s g d) -> p s g d", s=NT_TILES, g=4)
        finr = B.ring(2, 2048, BF16, dma=True, name="fin")
        FINv = FIN.rearrange("(g p) t -> p g t", p=128)
        k = 0
        for ci in range(8 if last_flags[li] else 9):
            t0 = ci * 512
            n = 512 if ci < 8 else 256
            fin, r_fin = finr.next()
            finv = fin.rearrange("p (g t) -> p g t", g=4)[:, :, 0:n]
            B.load(finv, FINv[:, :, t0:t0 + n], r_fin)
            for tt in range(n // 128):
                st = ci * 4 + tt
                for gp in range(2):
                    bank = k % 4
                    k += 1
                    for g2 in range(2):
                        g = gp * 2 + g2
                        B.mm(ps[bank][:, g2 * 256:(g2 + 1) * 256], finv[:, g, tt * 128:(tt + 1) * 128], Wv[:, g, :], True, True,
                             [r_fin, r_Wcs], [rps[bank]])
                    o = Yv[:, st, gp * 2:gp * 2 + 2, :]
                    i_ = ps[bank][:].rearrange("p (g d) -> p g d", g=2)
                    if k % 2 == 0:
                        B.act(o, i_, AF.Identity, [rps[bank]], [r_Y])
                    else:
                        B.cp("dve", o, i_, [rps[bank]], [r_Y])
        tabr = B.ring(3, 8 * 512, BF16, dma=True, name="ctab")
        tabr2 = B.ring(3, 8 * 512, BF16, dma=True, name="stab")
        szr = B.ring(2, 2048, BF16, dma=True, name="szc")
        ygr = B.ring(2, 2048, BF16, dma=True, name="ygc")
        SZv = SZ[2].rearrange("(g p) t -> p g t", p=128)
        YGv = YG[2].rearrange("(g p) t -> p g t", p=128)
        CLv = cd["CL"].rearrange("(s p) t -> p s t", p=128)
        SLv = cd["SL"].rearrange("(s p) t -> p s t", p=128)
        todo = [(tc, sg) for tc in range(8) for sg in range(4)]

        def tl(tc, sg):
            a, r_a = tabr.next()
            b, r_b = tabr2.next()
            av = a.rearrange("p (s t) -> p s t", s=8)
            bv = b.rearrange("p (s t) -> p s t", s=8)
            B.load(av, CLv[:, sg * 8:(sg + 1) * 8, tc * 512:(tc + 1) * 512], r_a)
            B.load(bv, SLv[:, sg * 8:(sg + 1) * 8, tc * 512:(tc + 1) * 512], r_b)
            return av, r_a, bv, r_b
        pend = [tl(*todo[0]), tl(*todo[1])]
        for i, (tc, sg) in enumerate(todo):
            if i + 2 < len(todo):
                pend.append(tl(*todo[i + 2]))
            av, r_a, bv, r_b = pend.pop(0)
            pb = 4 * (tc % 2)
            if sg == 0:
                sz, r_sz = szr.next()
                szv = sz.rearrange("p (g t) -> p g t", g=4)
                B.load(szv, SZv[:, :, tc * 512:(tc + 1) * 512], r_sz)
            for g in range(4):
                for s8 in range(8):
                    st = sg * 8 + s8
                    B.mm(ps[pb + g][:], Yv[:, st, g, 0:128], av[:, s8, :], st == 0, False, [r_Y, r_a], [rps[pb + g]])
                    B.mm(ps[pb + g][:], Yv[:, st, g, 128:256], bv[:, s8, :], False, st == 31, [r_Y, r_b], [rps[pb + g]])
            if sg == 3:
                yg, r_yg = ygr.next()
                ygv = yg.rearrange("p (g t) -> p g t", g=4)
                for g in range(4):
                    B.tt("dve", ygv[:, g, :], ps[pb + g][:], szv[:, g, :], MUL, [rps[pb + g], r_sz], [r_yg])
                B.store(YGv[:, :, tc * 512:(tc + 1) * 512], ygv, r_yg)
        if last_flags[li]:
            return
        sz, r_sz = szr.next()
        szv = sz.rearrange("p (g t) -> p g t", g=4)[:, :, 0:256]
        B.load(szv, SZv[:, :, TL:T], r_sz)
        yg, r_yg = ygr.next()
        ygv = yg.rearrange("p (g t) -> p g t", g=4)[:, :, 0:256]
        C2 = C256s.rearrange("p (s t) -> p s t", s=2)
        S2 = S256s.rearrange("p (s t) -> p s t", s=2)
        for g in range(4):
            for s in range(2):
                B.mm(ps[g][:, 0:256], Yv[:, 32 + s, g, 0:128], C2[:, s, :], s == 0, False, [r_Y, r_C256], [rps[g]])
                B.mm(ps[g][:, 0:256], Yv[:, 32 + s, g, 128:256], S2[:, s, :], False, s == 1, [r_Y, r_S256], [rps[g]])
            B.tt("dve", ygv[:, g, :], ps[g][:, 0:256], szv[:, g, :], MUL, [rps[g], r_sz], [r_yg])
        B.store(YGv[:, :, TL:T], ygv, r_yg)

    def phase_D(li):
        B.phase_reset()
        rbr = B.ring(3, 512, F32, dma=True, name="rb")
        ebr = B.ring(3, 512, BF16, dma=True, name="eb")
        for pat in range(25):
            for c in range(4):
                rb, r_rb = rbr.next()
                eb, r_eb = ebr.next()
                B.load(rb, rpbg[li, pat, c], r_rb)
                B.act(eb, rb, AF.Exp, [r_rb], [r_eb])
                B.store(ET[pat, c], eb, r_eb)
        B.phase_reset()
        bufs = []
        for a in range(2):
            QTc, r_QT = B.alloc(T, BF16, name=f"QTc{a}")
            KTc, r_KT = B.alloc(T, BF16, name=f"KTc{a}")
            VA, r_VA = B.alloc(NT_TILES * 132, BF16, name=f"VAd{a}")
            SZc, r_SZc = B.alloc(T, BF16, dma=True, name=f"SZc{a}")
            Eint, r_Eint = B.alloc(5 * 512, BF16, dma=True, name=f"Eint{a}")
            B.memset("pool", VA, 1.0, [r_VA])
            bufs.append(dict(QTc=QTc, r_QT=r_QT, KTc=KTc, r_KT=r_KT, VA=VA, r_VA=r_VA,
                             VAv=VA.rearrange("p (n j e) -> p n j e", n=NT_TILES, j=4), SZc=SZc, r_SZc=r_SZc,
                             Eiv=Eint.rearrange("p (o f) -> p o f", o=5), r_Eint=r_Eint))
        edr = B.ring(2, 5 * 512, BF16, dma=True, name="Eedge")
        xr = B.ring(3, 512, F32, dma=True, name="x")
        dvr = B.ring(2, 512, BF16, dma=True, name="dv")
        sqr = B.ring(2, 512, BF16, name="sq")
        rsr = B.ring(2, 512, F32, name="rs")
        pring = B.ring(24, 512, BF16, name="P")
        qmr = B.ring(3, 512, BF16, name="qm")
        onr = B.ring(2, 128, F32, name="On")
        denr = B.ring(2, 8, F32, name="den")
        ygr = B.ring(2, 512, BF16, dma=True, name="yg")
        hmv = headmask.rearrange("p (j q) -> p j q", j=4)
        ETv = ET.rearrange("a c p f -> p a c f")
        cntr = dict(sb=0, ob=0, pb=0)
        nqb = 32 if last_flags[li] else NT_TILES

        def prep_pieces(c):
            bf_ = bufs[c % 2]
            lds, cps = [], []

            def p_loads():
                B.load(bf_["SZc"], SZ[3][c * 128:(c + 1) * 128, :], bf_["r_SZc"])
                B.load(bf_["Eiv"], ETv[:, 10:15, c, :], bf_["r_Eint"])
            for ci in range(9):
                t0 = ci * 512
                n = 512 if ci < 8 else 256
                for which in range(2):
                    box = {}

                    def l_norm(t0=t0, n=n, which=which, box=box):
                        src = DQ if which == 0 else DK
                        x, r_x = xr.next()
                        B.load(x[:, 0:n], src[c * 128:(c + 1) * 128, t0:t0 + n], r_x)
                        box["x"], box["r_x"] = x, r_x

                    def p_norm(t0=t0, n=n, which=which, box=box):
                        g, r_g = (gqD, r_gqD) if which == 0 else (gkD, r_gkD)
                        dest, r_dest = (bf_["QTc"], bf_["r_QT"]) if which == 0 else (bf_["KTc"], bf_["r_KT"])
                        x, r_x = box["x"], box["r_x"]
                        sq, r_sq = sqr.next()
                        rs, r_rs = rsr.next()
                        bank = 6 + cntr["pb"] % 2
                        cntr["pb"] += 1
                        B.tt("pool", sq[:, 0:n], x[:, 0:n], x[:, 0:n], MUL, [r_x], [r_sq])
                        B.mm(ps[bank][:, 0:n], blk32, sq[:, 0:n], True, True, [r_sq, r_blk32], [rps[bank]])
                        B.act(rs[:, 0:n], ps[bank][:, 0:n], AF.Ln, [rps[bank]], [r_rs], scale=1.0 / 32, bias=EPS)
                        B.act(rs[:, 0:n], rs[:, 0:n], AF.Exp, [r_rs], [r_rs], scale=-0.5)
                        B.stt(dest[:, t0:t0 + n], x[:, 0:n], g[:, 0:1], rs[:, 0:n], MUL, MUL, [r_x, r_g, r_rs], [r_dest])
                    lds.append(l_norm)
                    cps.append(p_norm)
                vbox = {}

                def l_v(t0=t0, n=n, vbox=vbox):
                    dv, r_dv = dvr.next()
                    B.load(dv[:, 0:n], DV[c * 128:(c + 1) * 128, t0:t0 + n], r_dv)
                    vbox["dv"], vbox["r_dv"] = dv, r_dv

                def p_v(ci=ci, n=n, vbox=vbox):
                    dv, r_dv = vbox["dv"], vbox["r_dv"]
                    for tt in range(n // 128):
                        tile = ci * 4 + tt
                        bank = 6 + cntr["pb"] % 2
                        cntr["pb"] += 1
                        psb = ps[bank].bitcast(BF16)
                        B.tr(psb[:, 0:128], dv[:, tt * 128:(tt + 1) * 128], ident_b, [r_dv, r_identb], [rps[bank]])
                        B.cp("dve", bf_["VAv"][:, tile, :, 0:32], psb[:, 0:128].rearrange("p (j e) -> p j e", j=4),
                             [rps[bank]], [bf_["r_VA"]])
                lds.append(l_v)
                cps.append(p_v)

            def first():
                p_loads()
                lds[0]()
                lds[1]()
            pieces = [first]
            for i in range(len(cps)):
                def piece(i=i):
                    cps[i]()
                    if i + 2 < len(lds):
                        lds[i + 2]()
                pieces.append(piece)
            return pieces

        for p in prep_pieces(0):
            p()
        for c in range(4):
            bf_ = bufs[c % 2]
            QTc, r_QT, KTc, r_KT, VAv, r_VA = bf_["QTc"], bf_["r_QT"], bf_["KTc"], bf_["r_KT"], bf_["VAv"], bf_["r_VA"]
            SZc, r_SZc, Eiv, r_Eint = bf_["SZc"], bf_["r_SZc"], bf_["Eiv"], bf_["r_Eint"]
            nxtp = prep_pieces(c + 1) if c + 1 < 4 else []
            st = dict(yg=None, r_yg=None)

            def s1(qb, c=c, QTc=QTc, r_QT=r_QT, KTc=KTc, r_KT=r_KT, Eiv=Eiv, r_Eint=r_Eint):
                lat = qb < 32
                Ev, r_E = None, None
                if lat:
                    tl_ = d_tiles(qb)
                    pb_ = tl_[0][1]
                    if pb_ == 10:
                        Ev, r_E = Eiv, r_Eint
                    else:
                        ed, r_E = edr.next()
                        Ev = ed.rearrange("p (o f) -> p o f", o=5)
                        B.load(Ev, ETv[:, pb_:pb_ + 5, c, :], r_E)
                    tiles = [(kt, o) for o, (kt, _) in enumerate(tl_)] + [(32, None), (33, None)]
                else:
                    tiles = [(32, None), (33, None)]
                qm, r_qm = qmr.next()
                B.tt("pool", qm.rearrange("p (j q) -> p j q", j=4),
                     QTc[:, qb * 128:(qb + 1) * 128].unsqueeze(1).to_broadcast([128, 4, 128]), hmv, MUL,
                     [r_QT, r_hm], [r_qm])
                Ps = []
                for (kt, o) in tiles:
                    bank = cntr["sb"] % 4
                    cntr["sb"] += 1
                    B.mm(ps[bank][:], KTc[:, kt * 128:(kt + 1) * 128], qm, True, True, [r_KT, r_qm], [rps[bank]])
                    Pt, r_P = pring.next()
                    B.act(Pt, ps[bank][:], AF.Exp, [rps[bank]], [r_P], scale=float(32 ** -0.5))
                    if o is not None:
                        B.tt("dve", Pt, Pt, Ev[:, o, :], MUL, [r_P, r_E], [r_P])
                    Ps.append((Pt, r_P, kt))
                return Ps

            def s2(qb, Ps, c=c, VAv=VAv, r_VA=r_VA, SZc=SZc, r_SZc=r_SZc):
                if qb % 4 == 0:
                    st["yg"], st["r_yg"] = ygr.next()
                yg, r_yg = st["yg"], st["r_yg"]
                obank = 4 + cntr["ob"] % 2
                cntr["ob"] += 1
                Ov = ps[obank][:, 0:132].rearrange("p (h e) -> p h e", e=33)
                for jh in range(4):
                    for ti, (Pt, r_P, kt) in enumerate(Ps):
                        B.mm(ps[obank][:, jh * 33:(jh + 1) * 33], Pt[:, jh * 128:(jh + 1) * 128], VAv[:, kt, jh, :],
                             ti == 0, ti == len(Ps) - 1, [r_P, r_VA], [rps[obank]])
                den, r_den = denr.next()
                B.recip(den[:, 0:4], Ov[:, :, 32], [rps[obank]], [r_den])
                on, r_on = onr.next()
                B.tt("dve", on.rearrange("p (h e) -> p h e", h=4), Ov[:, :, 0:32],
                     den[:, 0:4].unsqueeze(2).to_broadcast([128, 4, 32]), MUL, [rps[obank], r_den], [r_on])
                tb = 6 + cntr["pb"] % 2
                cntr["pb"] += 1
                B.tr(ps[tb][:, 0:128], on, ident_f, [r_on, r_identf], [rps[tb]])
                qq = qb % 4
                B.tt("dve", yg[:, qq * 128:(qq + 1) * 128], ps[tb][:, 0:128], SZc[:, qb * 128:(qb + 1) * 128], MUL,
                     [rps[tb], r_SZc], [r_yg])
                if qb % 4 == 3 or qb == NT_TILES - 1:
                    t0 = (qb // 4) * 512
                    n = (qq + 1) * 128
                    B.store(YG[3][c * 128:(c + 1) * 128, t0:t0 + n], yg[:, 0:n], r_yg)

            prev = s1(0)
            for qb in range(nqb):
                nxt = s1(qb + 1) if qb + 1 < nqb else None
                s2(qb, prev)
                if nxtp:
                    nxtp.pop(0)()
                prev = nxt
            while nxtp:
                nxtp.pop(0)()

    def phase_merge(li):
        last = last_flags[li]
        hsrc = hT0 if li == 0 else H
        hdst = OUT if last else H
        B.phase_reset()
        nTb, r_nTb = B.alloc(16 * 1088, BF16, dma=True, name="nTb")
        ygb, r_ygb = B.alloc(16 * 1088, BF16, dma=True, name="ygb")
        acc, r_acc = B.alloc(16 * 1088, BF16, name="accT")
        nTv = nTb.rearrange("p (k t) -> p k t", k=16)
        ygv = ygb.rearrange("p (k t) -> p k t", k=16)
        accv = acc.rearrange("p (k t) -> p k t", k=16)
        wgr = B.ring(4, 2048, BF16, dma=True, name="wg")
        wbr = B.ring(4, 512, BF16, dma=True, name="wb")
        sigr = B.ring(3, 416, F32, dma=True, name="sig")
        tmpr = B.ring(3, 416, F32, dma=True, name="tmp")
        a32 = [[B.alloc(416, F32, dma=True, name=f"a32_{i}_{k}") for k in range(3)] for i in range(2)]
        wor = wgr
        htr = Ring(sigr.items + tmpr.items)
        ostr = Ring(a32[0] + a32[1])
        NTv = NT.rearrange("(k p) t -> p k t", p=128)
        gb = 0
        for subs0 in TCHUNKS:
            subs = [s for s in subs0 if not (last and s[2] == 1)]
            tbase = subs[0][0]
            ntot = sum(s[1] for s in subs)
            offs = []
            o_ = 0
            for s in subs:
                offs.append(o_)
                o_ += s[1]
            B.load(nTv[:, :, 0:ntot], NTv[:, :, tbase:tbase + ntot], r_nTb)
            r_ygs = []
            for b in range(4):
                r_b = pg.res(f"ygb{b}", dma=True)
                B.pg.dma("sp", ygv[:, b * 4:(b + 1) * 4, 0:ntot],
                         YG[b].rearrange("(k p) t -> p k t", p=128)[:, :, tbase:tbase + ntot], [], [r_b, r_ygb], r_b)
                r_ygs.append(r_b)
            todo = [(oc, b) for oc in range(16) for b in range(4)]

            def wl(oc, b):
                wg, r_wg = wgr.next()
                wb, r_wb = wbr.next()
                B.load(wg, w_gate_b[li, b, oc], r_wg, q="pool")
                B.load(wb, w_branch_b[li, b, oc], r_wb, q="pool")
                return wg, r_wg, wb, r_wb
            pend = [wl(*todo[0]), wl(*todo[1])]
            for i, (oc, b) in enumerate(todo):
                if i + 2 < len(todo):
                    pend.append(wl(*todo[i + 2]))
                wg, r_wg, wb, r_wb = pend.pop(0)
                for si, (t0, n, j) in enumerate(subs):
                    off = offs[si]
                    gbank = gb % 4
                    pbank = 4 + gb % 4
                    gb += 1
                    for kc in range(KC):
                        B.mm(ps[gbank][:, 0:n], wg[:, kc * 128:(kc + 1) * 128], nTv[:, kc, off:off + n], kc == 0, kc == KC - 1,
                             [r_wg, r_nTb], [rps[gbank]])
                    for kc in range(4):
                        B.mm(ps[pbank][:, 0:n], wb[:, kc * 128:(kc + 1) * 128], ygv[:, b * 4 + kc, off:off + n], kc == 0, kc == 3,
                             [r_wb, r_ygs[b]], [rps[pbank]])
                    sig, r_sig = sigr.next()
                    B.act(sig[:, 0:n], ps[gbank][:, 0:n], AF.Sigmoid, [rps[gbank]], [r_sig])
                    a, r_a = a32[oc % 2][si]
                    if b == 0:
                        B.tt("dve", a[:, 0:n], sig[:, 0:n], ps[pbank][:, 0:n], MUL, [r_sig, rps[pbank]], [r_a])
                    else:
                        tmp, r_tmp = tmpr.next()
                        B.tt("dve", tmp[:, 0:n], sig[:, 0:n], ps[pbank][:, 0:n], MUL, [r_sig, rps[pbank]], [r_tmp])
                        if b < 3:
                            B.tt("dve", a[:, 0:n], a[:, 0:n], tmp[:, 0:n], ADD, [r_a, r_tmp], [r_a])
                        else:
                            B.tt("dve", accv[:, oc, off:off + n], a[:, 0:n], tmp[:, 0:n], ADD, [r_a, r_tmp], [r_acc])
            pend = []

            def wol(oc2):
                w, r = wor.next()
                B.load(w, w_out_b[li, oc2], r, q="pool")
                return w, r
            pend.append(wol(0))
            pend.append(wol(1))
            for oc2 in range(16):
                if oc2 + 2 < 16:
                    pend.append(wol(oc2 + 2))
                w, r_w = pend.pop(0)
                hts = []
                for si, (t0, n, j) in enumerate(subs):
                    ht, r_ht = htr.next()
                    B.load(ht[:, 0:n], hsrc[oc2 * 128:(oc2 + 1) * 128, t0:t0 + n], r_ht)
                    hts.append((ht, r_ht))
                for si, (t0, n, j) in enumerate(subs):
                    off = offs[si]
                    bank = gb % 8
                    gb += 1
                    for kc in range(KC):
                        B.mm(ps[bank][:, 0:n], w[:, kc * 128:(kc + 1) * 128], accv[:, kc, off:off + n], kc == 0, kc == KC - 1,
                             [r_w, r_acc], [rps[bank]])
                    ht, r_ht = hts[si]
                    ost, r_ost = ostr.next()
                    B.stt(ost[:, 0:n], ps[bank][:, 0:n], MTv[:, li, 32 + oc2, j:j + 1], ht[:, 0:n], MUL, ADD,
                          [rps[bank], r_ht, r_MT], [r_ost])
                    B.pg.dma("act", hdst[oc2 * 128:(oc2 + 1) * 128, t0:t0 + n], ost[:, 0:n], [r_ost], [], r_ost)

    phase_adaln()
    for li in range(L):
        pg.epoch = li + 1
        B.phase_reset()
        layer_params(li)
        for nm, fn in (("W", phase_norm_win), ("A", phase_A), ("B", phase_B), ("C", phase_C), ("D", phase_D),
                       ("M", phase_merge)):
            if nm in PHASES:
                fn(li)
    pg.finish()
    pg.emit()
    return B


def make_in_maps(inp, layers, cores):
    sh = host_layout(inp, layers)
    maps = []
    for b in cores:
        m = dict(sh)
        m["hT0"] = np.ascontiguousarray(np.concatenate([inp["x"][b].T, inp["ctx"][b].T], axis=1))
        cv = np.stack([inp["c"][b].reshape(KC, 128).T, inp["c_ctx"].reshape(KC, 128).T], axis=2)
        m["cvec"] = np.ascontiguousarray(cv.reshape(128, 32))
        maps.append(m)
    return maps


def kernel(**inputs):
    inp = {k: np.asarray(v) for k, v in inputs.items()}
    layers = list(range(DEPTH))
    B = build_program(DEPTH, [l == DEPTH - 1 for l in layers])
    maps = make_in_maps(inp, layers, list(range(8)))
    res = run_bass_kernel_spmd(B.nc, maps, core_ids=list(range(8)))
    out = np.stack([np.ascontiguousarray(r["OUT"].T) for r in res.results], axis=0)
    return out.astype(np.float32)
```

```python
import numpy as np
import ml_dtypes
import concourse.bass as bass
import concourse.mybir as mybir
from concourse.bass_utils import run_bass_kernel_spmd

F32 = mybir.dt.float32
BF16 = mybir.dt.bfloat16
AF = mybir.ActivationFunctionType
ALU = mybir.AluOpType

QUEUES = ("pe", "act", "dve", "pool", "sp")
COMPUTE = ("pe", "act", "dve", "pool")


class Res:
    __slots__ = ("name", "w", "r", "dsem")

    def __init__(self, name, dsem=None):
        self.name = name
        self.w = None
        self.r = {}
        self.dsem = dsem


class DmaSem:
    __slots__ = ("sem", "count", "last")

    def __init__(self, sem):
        self.sem = sem
        self.count = 0
        self.last = None


class Ins:
    __slots__ = ("q", "fn", "deps", "dma", "dsem", "dcount", "signal", "count", "epoch", "idx", "bar")

    def __init__(self, q, fn, dma, epoch):
        self.q = q
        self.fn = fn
        self.deps = set()
        self.dma = dma
        self.dsem = None
        self.dcount = 0
        self.signal = False
        self.count = 0
        self.epoch = epoch
        self.bar = None


class Prog:
    def __init__(self, nc, n_epochs, n_dma_sems):
        self.nc = nc
        self.ins = {q: [] for q in QUEUES}
        self.epoch = 0
        self.esem = {}
        for e in range(n_epochs):
            for q in COMPUTE:
                self.esem[(q, e)] = nc.alloc_semaphore(name=f"s_{q}_{e}")
        self.dsems = [DmaSem(nc.alloc_semaphore(name=f"d_{i}")) for i in range(n_dma_sems)]
        self.dsem_next = 0
        self.all = []
        self.pending_bar = {q: None for q in QUEUES}
        self.n_waits = 0

    def res(self, name, dma=False):
        d = None
        if dma:
            d = self.dsems[self.dsem_next % len(self.dsems)]
            self.dsem_next += 1
        return Res(name, d)

    def _rec(self, q, fn, reads, writes, dma=False, dsem=None):
        i = Ins(q, fn, dma, self.epoch)
        i.idx = len(self.all)
        self.all.append(i)
        if self.pending_bar[q] is not None:
            i.bar = self.pending_bar[q]
            self.pending_bar[q] = None
        for r in reads:
            if r.w is not None:
                i.deps.add(r.w)
        for w in writes:
            if w.w is not None:
                i.deps.add(w.w)
            for x in w.r.values():
                i.deps.add(x)
        if dma:
            i.dsem = dsem
            if dsem.last is not None:
                i.deps.add(dsem.last)
            dsem.count += 16
            i.dcount = dsem.count
            dsem.last = i
        for r in reads:
            key = ("dma", i.idx) if dma else q
            r.r[key] = i
        for w in writes:
            w.w = i
            w.r = {}
        i.deps.discard(i)
        self.ins[q].append(i)
        return i

    def op(self, q, fn, reads=(), writes=()):
        return self._rec(q, fn, list(reads), list(writes))

    def dma(self, q, out, in_, reads, writes, slot, **kw):
        def fn(eng, out=out, in_=in_, kw=kw):
            return eng.dma_start(out=out, in_=in_, **kw)
        return self._rec(q, fn, list(reads), list(writes), dma=True, dsem=slot.dsem)

    def barrier(self):
        snap = {"eng": {}, "dma": []}
        for q in COMPUTE:
            for x in reversed(self.ins[q]):
                if not x.dma and x.fn is not None:
                    snap["eng"][q] = x
                    break
        for d in self.dsems:
            if d.count:
                snap["dma"].append((d, d.count))
        for q in QUEUES:
            self.pending_bar[q] = snap

    def finish(self):
        self.barrier()
        for q in QUEUES:
            if self.ins[q]:
                self._rec(q, None, [], [])

    def emit(self):
        nc = self.nc
        for i in self.all:
            for p in i.deps:
                if p.dma:
                    continue
                if p.q == i.q and p.q == "pe":
                    continue
                p.signal = True
            if i.bar is not None:
                for q, last in i.bar["eng"].items():
                    last.signal = True
        for q in COMPUTE:
            cnt = {}
            for i in self.ins[q]:
                if i.signal and not i.dma and i.fn is not None:
                    cnt[i.epoch] = cnt.get(i.epoch, 0) + 1
                    i.count = cnt[i.epoch]
        engs = {"pe": "tensor", "act": "scalar", "dve": "vector", "pool": "gpsimd", "sp": "sync"}
        prog = self

        def run_queue(q, eng):
            waited = {}

            def wait(sem, val):
                k = id(sem)
                if waited.get(k, 0) >= val:
                    return
                waited[k] = val
                eng.wait_ge(sem, val)
                prog.n_waits += 1

            for i in prog.ins[q]:
                if i.bar is not None:
                    for pq, last in i.bar["eng"].items():
                        if pq == q and q == "pe":
                            continue
                        wait(prog.esem[(pq, last.epoch)], last.count)
                    for d, c in i.bar["dma"]:
                        wait(d.sem, c)
                for p in i.deps:
                    if p.dma:
                        wait(p.dsem.sem, p.dcount)
                    else:
                        if p.q == q and q == "pe":
                            continue
                        wait(prog.esem[(p.q, p.epoch)], p.count)
                if i.fn is None:
                    continue
                inst = i.fn(eng)
                if i.dma:
                    inst.then_inc(i.dsem.sem, 16)
                elif i.signal:
                    inst.then_inc(prog.esem[(q, i.epoch)], 1)

        with nc.Block() as block:
            for q in QUEUES:
                if not prog.ins[q]:
                    continue
                deco = getattr(block, engs[q])

                def body(eng, q=q):
                    run_queue(q, eng)
                deco(body)


D = 2048
KC = 16
TL = 4096
TCX = 256
T = TL + TCX
NT_TILES = T // 128
DEPTH = 4
EPS = 1e-6
NZ = 46

OFF = dict(a_u=0, a_v=512, a_z=1024, b_q=1536, b_k=2048, b_v=2176, b_z=2304,
           f_in=2816, f_z=3328, d_q=3840, d_k=4352, d_v=4864, d_z=5376)


def z_chunks():
    ch = []
    for g in range(4):
        ch.append(("gelu_bf", "GU", g, list(range(OFF["a_u"] + 128 * g, OFF["a_u"] + 128 * g + 128))))
    for g in range(4):
        ch.append(("gelu_f32", "AV", g, list(range(OFF["a_v"] + 128 * g, OFF["a_v"] + 128 * g + 128))))
    for bi, nm in enumerate(("a_z", "b_z", "f_z", "d_z")):
        for g in range(4):
            ch.append(("silu_bf", "SZ%d" % bi, g, list(range(OFF[nm] + 128 * g, OFF[nm] + 128 * g + 128))))
    for c in range(4):
        cols = list(range(OFF["b_q"] + 64 * c, OFF["b_q"] + 64 * c + 64)) + \
            list(range(OFF["b_q"] + 64 * (c + 4), OFF["b_q"] + 64 * (c + 4) + 64))
        ch.append(("id_f32", "BQ", c, cols))
    ch.append(("id_f32", "BK", 0, list(range(OFF["b_k"], OFF["b_k"] + 128))))
    ch.append(("id_bf", "BV", 0, list(range(OFF["b_v"], OFF["b_v"] + 128))))
    for g in range(4):
        ch.append(("id_bf", "FIN", g, list(range(OFF["f_in"] + 128 * g, OFF["f_in"] + 128 * g + 128))))
    for g in range(4):
        ch.append(("id_f32", "DQ", g, list(range(OFF["d_q"] + 128 * g, OFF["d_q"] + 128 * g + 128))))
    for g in range(4):
        ch.append(("id_f32", "DK", g, list(range(OFF["d_k"] + 128 * g, OFF["d_k"] + 128 * g + 128))))
    for g in range(4):
        ch.append(("id_bf", "DV", g, list(range(OFF["d_v"] + 128 * g, OFF["d_v"] + 128 * g + 128))))
    assert len(ch) == NZ
    return ch


ZCH = z_chunks()
PHASES = "WABCDM"
ARENA_WORDS = 44 * 1024
CAST_DMA_BYTES = 4096
WMODE = 0


def token_chunks():
    out = []
    for c in range(3):
        b = 1088 * c
        out.append([(b, 384, 0), (b + 384, 352, 0), (b + 736, 352, 0)])
    out.append([(3264, 416, 0), (3680, 416, 0), (4096, 256, 1)])
    return out


TCHUNKS = token_chunks()


def d_tiles(i):
    base = min(max(i - 2, 0), 27)
    if i == 0:
        pb = 0
    elif i == 1:
        pb = 5
    elif i == 30:
        pb = 15
    elif i == 31:
        pb = 20
    else:
        pb = 10
    return [(base + o, pb + o) for o in range(5)]


_CONST_CACHE = {}


def host_consts():
    if _CONST_CACHE:
        return _CONST_CACHE
    bf = ml_dtypes.bfloat16
    c = {}
    c["ident_f"] = np.eye(128, dtype=np.float32)
    c["ident_b"] = np.eye(128, dtype=np.float32).astype(bf)
    c["ones_b"] = np.ones((128, 128), np.float32).astype(bf)
    p = np.arange(128)
    c["blk64_b"] = (p[:, None] // 64 == p[None, :] // 64).astype(np.float32).astype(bf)
    c["blk32_b"] = (p[:, None] // 32 == p[None, :] // 32).astype(np.float32).astype(bf)
    partner = np.where((p % 32) < 16, p + 16, p - 16)
    perm = np.zeros((128, 128), np.float32)
    perm[partner, p] = 1.0
    c["perm_f"] = perm
    c["ones_row"] = np.ones((1, 128), np.float32)
    hm = np.zeros((128, 4, 128), np.float32)
    for j in range(4):
        hm[32 * j:32 * j + 32, j, :] = 1.0
    c["headmask"] = hm.reshape(128, 512).astype(bf)
    k = np.arange(128)[:, None]
    q = np.arange(128)[None, :]
    m_prev = (k >= q).astype(np.float32)
    m_next = (k <= q).astype(np.float32)
    bm = np.stack([np.tile(m_prev, (1, 4)), np.tile(m_next, (1, 4))], axis=1)
    c["bmask"] = bm.reshape(128, 1024).astype(bf)
    t = np.arange(TL)
    pos_r, pos_c = (t // 64).astype(np.float64), (t % 64).astype(np.float64)
    inv = 10000.0 ** (-np.arange(16, dtype=np.float64) / 16)
    d = p % 64
    pos = np.where((d < 32)[:, None], pos_r[None, :], pos_c[None, :])
    ang = pos * inv[d % 16][:, None]
    ang32 = (pos.astype(np.float32) * inv.astype(np.float32)[d % 16][:, None]).astype(np.float32)
    c["rcos"] = np.cos(ang32).astype(np.float32)
    sgn = np.where((d % 32) < 16, -1.0, 1.0)[:, None]
    c["rsin"] = (np.sin(ang32) * sgn).astype(np.float32)
    n = np.arange(TL, dtype=np.int64)
    prod = (n[:, None] * n[None, :]) % TL
    a = 2 * np.pi * prod.astype(np.float64) / TL
    c["CL"] = (np.cos(a) / 64.0).astype(np.float32).astype(bf)
    c["SL"] = (np.sin(a) / 64.0).astype(np.float32).astype(bf)
    n2 = np.arange(TCX, dtype=np.int64)
    a2 = 2 * np.pi * ((n2[:, None] * n2[None, :]) % TCX).astype(np.float64) / TCX
    c["C256"] = (np.cos(a2) / 16.0).astype(np.float32).astype(bf)
    c["S256"] = (np.sin(a2) / 16.0).astype(np.float32).astype(bf)
    n3 = np.arange(128, dtype=np.int64)
    a3 = 2 * np.pi * ((n3[:, None] * n3[None, :]) % 128).astype(np.float64) / 128
    c["CC"] = (np.cos(a3) / np.sqrt(128.0)).astype(np.float32)
    c["SC"] = (np.sin(a3) / np.sqrt(128.0)).astype(np.float32)
    _CONST_CACHE.update(c)
    return c


def rpb_gather_index():
    reps = [0, 1, 2, 30, 31]
    DR = np.zeros((25, 128, 128), np.int64)
    DC = np.zeros((25, 128, 128), np.int64)
    V = np.zeros((25, 128, 128), bool)
    kk = np.arange(128)[:, None]
    qq = np.arange(128)[None, :]
    for ri, i in enumerate(reps):
        base = min(max(i - 2, 0), 27)
        for o in range(5):
            kt = base + o
            kr = 2 * kt + kk // 64
            kcol = kk % 64
            qr = 2 * i + qq // 64
            qc = qq % 64
            r0 = np.clip(qr - 4, 0, 56)
            c0 = np.clip(qc - 8, 0, 48)
            valid = (kr >= r0) & (kr <= r0 + 7) & (kcol >= c0) & (kcol <= c0 + 15)
            dr = np.clip(kr - qr + 7, 0, 14)
            dc = np.clip(kcol - qc + 15, 0, 30)
            pid = ri * 5 + o
            DR[pid], DC[pid], V[pid] = np.broadcast_to(dr, (128, 128)), np.broadcast_to(dc, (128, 128)), valid
    return DR, DC, V


def host_layout(inp, layers):
    L = len(layers)
    sh = {}
    w_in = inp["w_in"]
    cols = np.concatenate([np.asarray(z[3]) for z in ZCH])
    wi = np.empty((L, NZ, 128, KC * 128), np.float32)
    for li, l in enumerate(layers):
        w = w_in[l][:, cols]
        wi[li] = w.reshape(KC, 128, NZ, 128).transpose(2, 1, 0, 3).reshape(NZ, 128, KC * 128)
    sh["w_in_b"] = wi
    wa = np.empty((L, 48, 128, KC * 128), np.float32)
    for li, l in enumerate(layers):
        wa[li] = inp["w_ada"][l].reshape(KC, 128, 48, 128).transpose(2, 1, 0, 3).reshape(48, 128, KC * 128)
    sh["w_ada_b"] = wa
    sh["b_ada_f"] = np.ascontiguousarray(inp["b_ada"][layers].reshape(L, 48, 128).transpose(0, 2, 1))
    sh["norm_g_f"] = np.ascontiguousarray(inp["norm_g"][layers].reshape(L, KC, 128).transpose(0, 2, 1))
    wg = np.empty((L, 4, 16, 128, KC * 128), np.float32)
    wb = np.empty((L, 4, 16, 128, 4 * 128), np.float32)
    wo = np.empty((L, 16, 128, KC * 128), np.float32)
    for li, l in enumerate(layers):
        for b in range(4):
            wg[li, b] = inp["w_gate"][l, b].reshape(KC, 128, 16, 128).transpose(2, 1, 0, 3).reshape(16, 128, KC * 128)
            wb[li, b] = inp["w_branch"][l, b].reshape(4, 128, 16, 128).transpose(2, 1, 0, 3).reshape(16, 128, 512)
        wo[li] = inp["w_out"][l].reshape(KC, 128, 16, 128).transpose(2, 1, 0, 3).reshape(16, 128, KC * 128)
    sh["w_gate_b"], sh["w_branch_b"], sh["w_out_b"] = wg, wb, wo
    sh["a_g_f"] = np.ascontiguousarray(inp["a_norm_g"][layers].reshape(L, 4, 128).transpose(0, 2, 1))
    sh["a_wsT"] = np.ascontiguousarray(inp["a_w_s"][layers].transpose(0, 3, 1, 2)).reshape(L, 128, 512)
    sh["a_bs"] = np.ascontiguousarray(inp["a_b_s"][layers].reshape(L, 1, 512))
    sh["b_qg"] = np.ascontiguousarray(np.tile(inp["b_q_g"][layers], (1, 2)).reshape(L, 128, 1))
    sh["b_kg"] = np.ascontiguousarray(np.tile(inp["b_k_g"][layers], (1, 2)).reshape(L, 128, 1))
    sh["b_sinkb"] = np.ascontiguousarray(np.broadcast_to(inp["b_sink"][layers][:, None, :], (L, 128, 8)))
    sh["c_wf"] = np.ascontiguousarray(inp["c_w_f"][layers].transpose(0, 2, 1, 3)).reshape(L, 128, 512)
    sh["d_qg"] = np.ascontiguousarray(np.tile(inp["d_q_g"][layers], (1, 4)).reshape(L, 128, 1))
    sh["d_kg"] = np.ascontiguousarray(np.tile(inp["d_k_g"][layers], (1, 4)).reshape(L, 128, 1))
    DR, DC, V = rpb_gather_index()
    rp = inp["d_rpb"][layers]
    g = rp[:, :, DR, DC]
    g = np.where(V[None, None], g, np.float32(-100.0)).astype(np.float32)
    g = g.reshape(L, 4, 4, 25, 128, 128).transpose(0, 3, 1, 4, 2, 5)
    sh["rpbg"] = np.ascontiguousarray(g).reshape(L, 25, 4, 128, 512)
    sh.update(host_consts())
    return sh


class Builder:
    def __init__(self, n_layers, last_flags, debug=False):
        self.L = n_layers
        self.last_flags = last_flags
        self.debug = debug
        nc = bass.Bass("TRN2", target_bir_lowering=False)
        self.nc = nc
        self.pg = Prog(nc, n_layers + 1, 70)
        self.arena = nc.alloc_sbuf_tensor("arena", [128, ARENA_WORDS], F32)
        self.persist_off = 0
        self.off = 0
        self.ps = [nc.alloc_psum_tensor(f"ps{i}", [128, 512], F32) for i in range(8)]
        self.rps = [self.pg.res(f"ps{i}") for i in range(8)]
        self.dram = {}
        self.dres = {}

    def din(self, name, shape, dt=F32):
        t = self.nc.dram_tensor(name, list(shape), dt, kind="ExternalInput").ap()
        self.dram[name] = t
        return t

    def dscr(self, name, shape, dt, out=False):
        kind = "ExternalOutput" if (out or self.debug) else "Internal"
        t = self.nc.dram_tensor(name, list(shape), dt, kind=kind).ap()
        self.dram[name] = t
        return t

    def dr(self, key):
        r = self.dres.get(key)
        if r is None:
            r = self.pg.res(str(key))
            self.dres[key] = r
        return r

    def alloc(self, n, dt=F32, dma=False, name="t"):
        words = n if dt == F32 else (n + 1) // 2
        words = (words + 1) // 2 * 2
        assert self.off + words <= ARENA_WORDS, f"SBUF arena overflow at {name}: {self.off}+{words}"
        a = self.arena[:, self.off:self.off + words]
        self.off += words
        if dt != F32:
            a = a.bitcast(dt)[:, 0:n]
        return a, self.pg.res(name, dma=dma)

    def ring(self, k, n, dt=F32, dma=False, name="r"):
        return Ring([self.alloc(n, dt, dma, f"{name}{i}") for i in range(k)])

    def phase_reset(self):
        self.pg.barrier()
        self.off = self.persist_off

    def mm(self, out, lhsT, rhs, start, stop, rd, wr):
        self.pg.op("pe", lambda e: e.matmul(out, lhsT=lhsT, rhs=rhs, start=start, stop=stop), rd, wr)

    def tr(self, out, in_, ident, rd, wr):
        self.pg.op("pe", lambda e: e.transpose(out=out, in_=in_, identity=ident), rd, wr)

    def act(self, out, in_, func, rd, wr, scale=1.0, bias=0.0):
        self.pg.op("act", lambda e: e.activation(out=out, in_=in_, func=func, scale=scale, bias=bias), rd, wr)

    def tt(self, q, out, in0, in1, op, rd, wr):
        self.pg.op(q, lambda e: e.tensor_tensor(out=out, in0=in0, in1=in1, op=op), rd, wr)

    def ts(self, q, out, in0, s1, op0, rd, wr, s2=None, op1=None):
        if op1 is None:
            self.pg.op(q, lambda e: e.tensor_scalar(out=out, in0=in0, scalar1=s1, scalar2=None, op0=op0), rd, wr)
        else:
            self.pg.op(q, lambda e: e.tensor_scalar(out=out, in0=in0, scalar1=s1, scalar2=s2, op0=op0, op1=op1), rd, wr)

    def stt(self, out, in0, scalar, in1, op0, op1, rd, wr):
        self.pg.op("dve", lambda e: e.scalar_tensor_tensor(out=out, in0=in0, scalar=scalar, in1=in1, op0=op0, op1=op1), rd, wr)

    def cp(self, q, out, in_, rd, wr):
        self.pg.op(q, lambda e: e.tensor_copy(out=out, in_=in_), rd, wr)

    def recip(self, out, in_, rd, wr):
        self.pg.op("dve", lambda e: e.reciprocal(out=out, in_=in_), rd, wr)

    def memset(self, q, ap, val, wr):
        self.pg.op(q, lambda e: e.memset(ap, val), [], wr)

    def load(self, out, in_, slot, rd=(), q="sp"):
        if q == "pool":
            self.pg.dma(q, out, in_, rd, [slot], slot, max_dma_last_dim=CAST_DMA_BYTES)
        else:
            self.pg.dma(q, out, in_, rd, [slot], slot)

    def store(self, out, in_, slot, wr=(), q="sp"):
        self.pg.dma(q, out, in_, [slot], wr, slot)


class Ring:
    def __init__(self, items):
        self.items = items
        self.i = 0

    def next(self):
        it = self.items[self.i % len(self.items)]
        self.i += 1
        return it


def build_program(n_layers, last_flags, debug=False):
    B = Builder(n_layers, last_flags, debug)
    nc, pg, ps, rps = B.nc, B.pg, B.ps, B.rps
    L = n_layers
    MUL, ADD = ALU.mult, ALU.add

    hT0 = B.din("hT0", [D, T])
    cvec_d = B.din("cvec", [128, 32])
    w_ada_b = B.din("w_ada_b", [L, 48, 128, 2048])
    b_ada_f = B.din("b_ada_f", [L, 128, 48])
    norm_g_f = B.din("norm_g_f", [L, 128, 16])
    w_in_b = B.din("w_in_b", [L, NZ, 128, 2048])
    w_gate_b = B.din("w_gate_b", [L, 4, 16, 128, 2048])
    w_branch_b = B.din("w_branch_b", [L, 4, 16, 128, 512])
    w_out_b = B.din("w_out_b", [L, 16, 128, 2048])
    a_g_f = B.din("a_g_f", [L, 128, 4])
    a_wsT = B.din("a_wsT", [L, 128, 512])
    a_bs = B.din("a_bs", [L, 1, 512])
    b_qg = B.din("b_qg", [L, 128, 1])
    b_kg = B.din("b_kg", [L, 128, 1])
    b_sinkb = B.din("b_sinkb", [L, 128, 8])
    c_wf = B.din("c_wf", [L, 128, 512])
    d_qg = B.din("d_qg", [L, 128, 1])
    d_kg = B.din("d_kg", [L, 128, 1])
    rpbg = B.din("rpbg", [L, 25, 4, 128, 512])
    cd = {}
    for nm, shp, dt in (("ident_f", [128, 128], F32), ("ident_b", [128, 128], BF16), ("ones_b", [128, 128], BF16),
                        ("blk64_b", [128, 128], BF16), ("blk32_b", [128, 128], BF16), ("perm_f", [128, 128], F32),
                        ("ones_row", [1, 128], F32), ("headmask", [128, 512], BF16), ("bmask", [128, 1024], BF16),
                        ("rcos", [128, TL], F32), ("rsin", [128, TL], F32), ("CL", [TL, TL], BF16), ("SL", [TL, TL], BF16),
                        ("C256", [TCX, TCX], BF16), ("S256", [TCX, TCX], BF16), ("CC", [128, 128], F32), ("SC", [128, 128], F32)):
        cd[nm] = B.din(nm, shp, dt)

    OUT = B.dscr("OUT", [D, TL], F32, out=True)
    H = B.dscr("H", [D, T], F32)
    NT = B.dscr("NT", [D, T], BF16)
    GU = B.dscr("GU", [512, T], BF16)
    AV = B.dscr("AV", [512, T], F32)
    SZ = [B.dscr("SZ%d" % i, [512, T], BF16) for i in range(4)]
    BQ = B.dscr("BQ", [512, T], F32)
    BK = B.dscr("BK", [128, T], F32)
    BV = B.dscr("BV", [128, T], BF16)
    FIN = B.dscr("FIN", [512, T], BF16)
    DQ = B.dscr("DQ", [512, T], F32)
    DK = B.dscr("DK", [512, T], F32)
    DV = B.dscr("DV", [512, T], BF16)
    YG = [B.dscr("YG%d" % i, [512, T], BF16) for i in range(4)]
    ET = B.dscr("ET", [25, 4, 128, 512], BF16)
    ZD = dict(GU=GU, AV=AV, SZ0=SZ[0], SZ1=SZ[1], SZ2=SZ[2], SZ3=SZ[3], BQ=BQ, BK=BK, BV=BV, FIN=FIN, DQ=DQ, DK=DK, DV=DV)

    def cst(name, src, n, dt=F32, parts=128):
        a, r = B.alloc(n, dt, dma=True, name=name)
        a = a[0:parts, :]
        B.load(a, src, r)
        return a, r

    ident_f, r_identf = cst("ident_f", cd["ident_f"], 128)
    ident_b, r_identb = cst("ident_b", cd["ident_b"], 128, BF16)
    ones_b, r_onesb = cst("ones_b", cd["ones_b"], 128, BF16)
    blk64, r_blk64 = cst("blk64", cd["blk64_b"], 128, BF16)
    blk32, r_blk32 = cst("blk32", cd["blk32_b"], 128, BF16)
    perm_f, r_perm = cst("perm_f", cd["perm_f"], 128)
    ones_row, r_onesrow = cst("ones_row", cd["ones_row"], 128, F32, parts=1)
    headmask, r_hm = cst("headmask", cd["headmask"], 512, BF16)
    bmask, r_bm = cst("bmask", cd["bmask"], 1024, BF16)
    CCs, r_CC = cst("CC", cd["CC"], 128)
    SCs, r_SC = cst("SC", cd["SC"], 128)
    C256s, r_C256 = cst("C256", cd["C256"].rearrange("(s p) t -> p s t", p=128), 512, BF16)
    S256s, r_S256 = cst("S256", cd["S256"].rearrange("(s p) t -> p s t", p=128), 512, BF16)
    cvec, r_cvec = cst("cvec", cvec_d, 32)
    bada, r_bada = cst("bada", b_ada_f.rearrange("l p o -> p l o"), L * 48)
    MT, r_MT = B.alloc(L * 96, F32, name="MT")
    MTv = MT.rearrange("p (l o j) -> p l o j", l=L, o=48)
    normg, r_normg = B.alloc(16, F32, dma=True, name="normg")
    Gm, r_G = B.alloc(32, F32, name="G")
    Gv = Gm.rearrange("p (k j) -> p k j", j=2)
    gvA, r_gvA = B.alloc(4, F32, dma=True, name="gvA")
    wsTf, r_wsTf = B.alloc(512, F32, dma=True, name="wsTf")
    wsTb, r_wsTb = B.alloc(512, BF16, name="wsTb")
    bsA, r_bsA = B.alloc(512, F32, dma=True, name="bsA")
    bsA = bsA[0:1, :]
    gqB, r_gqB = B.alloc(2, F32, dma=True, name="gqB")
    gkB, r_gkB = B.alloc(2, F32, dma=True, name="gkB")
    esink, r_esink = B.alloc(8, F32, dma=True, name="esink")
    wfC, r_wfC = B.alloc(512, F32, dma=True, name="wfC")
    gqD, r_gqD = B.alloc(2, F32, dma=True, name="gqD")
    gkD, r_gkD = B.alloc(2, F32, dma=True, name="gkD")
    B.persist_off = B.off

    def phase_adaln():
        sc_t, r_sc = B.alloc(32, F32, name="silu_c")
        B.act(sc_t, cvec, AF.Silu, [r_cvec], [r_sc])
        scv = sc_t.rearrange("p (k j) -> p k j", j=2)
        ring = B.ring(3, 2048, F32, dma=True, name="wada")
        pend = []

        def wl(l, oc):
            w, r = ring.next()
            B.load(w, w_ada_b[l, oc], r)
            return w, r
        todo = [(l, oc) for l in range(L) for oc in range(48)]
        for x in todo[:2]:
            pend.append(wl(*x))
        for i, (l, oc) in enumerate(todo):
            if i + 2 < len(todo):
                pend.append(wl(*todo[i + 2]))
            w, r_w = pend.pop(0)
            bank = i % 2
            for kc in range(KC):
                B.mm(ps[bank][:, 0:2], w[:, kc * 128:(kc + 1) * 128], scv[:, kc, :], kc == 0, kc == KC - 1,
                     [r_w, r_sc], [rps[bank]])
            B.act(MTv[:, l, oc, :], ps[bank][:, 0:2], AF.Identity, [rps[bank], r_bada], [r_MT],
                  bias=bada[:, l * 48 + oc:l * 48 + oc + 1])

    def layer_params(li):
        B.load(normg, norm_g_f[li], r_normg)
        B.load(gvA, a_g_f[li], r_gvA)
        B.load(wsTf, a_wsT[li], r_wsTf)
        B.load(bsA, a_bs[li], r_bsA)
        B.load(gqB[:, 0:1], b_qg[li], r_gqB)
        B.load(gkB[:, 0:1], b_kg[li], r_gkB)
        B.load(esink, b_sinkb[li], r_esink)
        B.load(wfC, c_wf[li], r_wfC)
        B.load(gqD[:, 0:1], d_qg[li], r_gqD)
        B.load(gkD[:, 0:1], d_kg[li], r_gkD)
        B.cp("dve", wsTb, wsTf, [r_wsTf], [r_wsTb])
        B.act(esink, esink, AF.Exp, [r_esink], [r_esink])
        B.ts("dve", Gv, MTv[:, li, 16:32, :], 1.0, ADD, [r_MT], [r_G])
        B.tt("dve", Gv, Gv, normg.unsqueeze(2).to_broadcast([128, 16, 2]), MUL, [r_G, r_normg], [r_G])

    def phase_norm_win(li):
        hsrc = hT0 if li == 0 else H
        hsv = hsrc.rearrange("(k p) t -> p k t", p=128)
        NTv = NT.rearrange("(k p) t -> p k t", p=128)
        B.phase_reset()
        nTs = [[B.alloc(16 * 416, BF16, dma=True, name=f"nT{a}_{i}") for i in range(3)] for a in range(2)]
        hring = B.ring(1, 16 * 416, F32, dma=True, name="h")
        sq, r_sq = B.alloc(16 * 416, BF16, name="sq")
        rstd, r_rstd = B.alloc(416, F32, name="rstd")
        wring = B.ring(4, 2048, BF16, dma=True, name="w")
        stg = B.ring(3, 1248, F32, dma=True, name="stg")
        chunks = TCHUNKS[:1] if WMODE else TCHUNKS

        def norm_pieces(ci):
            subs = chunks[ci]
            pieces = []
            for si, (t0, n, j) in enumerate(subs):
                nT, r_nT = nTs[ci % 2][si]
                nTv = nT.rearrange("p (k t) -> p k t", k=16)[:, :, 0:n]
                sqv = sq.rearrange("p (k t) -> p k t", k=16)[:, :, 0:n]
                box = {}

                def p_load(t0=t0, n=n, box=box):
                    hs, r_hs = hring.next()
                    hv = hs.rearrange("p (k t) -> p k t", k=16)[:, :, 0:n]
                    B.load(hv, hsv[:, :, t0:t0 + n], r_hs)
                    box["hv"], box["r"] = hv, r_hs

                def p_stat(n=n, box=box, sqv=sqv):
                    hv, r_hs = box["hv"], box["r"]
                    B.act(sqv, hv, AF.Square, [r_hs], [r_sq])
                    for kc in range(KC):
                        B.mm(ps[6][:, 0:n], ones_b, sqv[:, kc, :], kc == 0, kc == KC - 1, [r_sq, r_onesb], [rps[6]])
                    B.act(rstd[:, 0:n], ps[6][:, 0:n], AF.Sqrt, [rps[6]], [r_rstd], scale=1.0 / D, bias=EPS)
                    B.recip(rstd[:, 0:n], rstd[:, 0:n], [r_rstd], [r_rstd])
                    B.tt("dve", hv, hv, rstd[:, 0:n].unsqueeze(1).to_broadcast([128, 16, n]), MUL, [r_hs, r_rstd], [r_hs])

                def p_mod(k0, j=j, box=box, nTv=nTv, r_nT=r_nT):
                    hv, r_hs = box["hv"], box["r"]
                    for kc in range(k0, k0 + 4):
                        B.act(nTv[:, kc, :], hv[:, kc, :], AF.Identity, [r_hs, r_G, r_MT], [r_nT],
                              scale=Gv[:, kc, j:j + 1], bias=MTv[:, li, kc, j:j + 1])

                def p_store(t0=t0, n=n, nTv=nTv, r_nT=r_nT):
                    B.store(NTv[:, :, t0:t0 + n], nTv, r_nT)
                pieces += [p_load, p_stat, (lambda f=p_mod: f(0)), (lambda f=p_mod: f(4)), (lambda f=p_mod: f(8)),
                           (lambda f=p_mod: f(12)), p_store]
            return pieces

        for p in norm_pieces(0):
            p()
        for ci, subs in enumerate(chunks):
            tbase = subs[0][0]
            ntot = sum(s_[1] for s_ in subs)
            nxt = norm_pieces(ci + 1) if ci + 1 < len(chunks) else []
            pend = []

            def wl(oc):
                w, r = wring.next()
                B.load(w, w_in_b[li, oc], r, q="pool")
                return w, r
            pend.append(wl(0))
            pend.append(wl(1))
            for oc in range(NZ if WMODE < 2 else WMODE):
                if oc + 2 < NZ:
                    pend.append(wl(oc + 2))
                w, r_w = pend.pop(0)
                kind, dest, dch, _ = ZCH[oc]
                st, r_st = stg.next()
                is_bf = kind.endswith("_bf")
                stv = st.bitcast(BF16) if is_bf else st
                off = 0
                for si, (t0, n, j) in enumerate(subs):
                    nT, r_nT = nTs[ci % 2][si]
                    nTv = nT.rearrange("p (k t) -> p k t", k=16)[:, :, 0:n]
                    bank = (oc * 3 + si) % 6
                    for kc in range(KC):
                        B.mm(ps[bank][:, 0:n], w[:, kc * 128:(kc + 1) * 128], nTv[:, kc, :], kc == 0, kc == KC - 1,
                             [r_w, r_nT], [rps[bank]])
                    o = stv[:, off:off + n]
                    if kind.startswith("gelu"):
                        B.act(o, ps[bank][:, 0:n], AF.Gelu_apprx_tanh, [rps[bank]], [r_st])
                    elif kind.startswith("silu"):
                        B.act(o, ps[bank][:, 0:n], AF.Silu, [rps[bank]], [r_st])
                    else:
                        B.cp("dve", o, ps[bank][:, 0:n], [rps[bank]], [r_st])
                    off += n
                B.store(ZD[dest][dch * 128:(dch + 1) * 128, tbase:tbase + ntot], stv[:, 0:ntot], r_st)
                if oc >= 8 and nxt:
                    nxt.pop(0)()
            while nxt:
                nxt.pop(0)()

    def phase_A(li):
        last = last_flags[li]
        B.phase_reset()
        avr = B.ring(2, 2048, F32, dma=True, name="av")
        gur = B.ring(2, 2048, BF16, dma=True, name="gu")
        szr = B.ring(2, 2048, BF16, dma=True, name="sz")
        ygr = B.ring(2, 2048, BF16, dma=True, name="yg")
        sqr = B.ring(2, 2048, BF16, name="sq")
        rsr = B.ring(2, 512, F32, name="rstd")
        vnr = B.ring(2, 2048, BF16, name="vn")
        vtr = B.ring(2, 2048, BF16, name="vtok")
        t1r = B.ring(2, 2048, F32, name="t1")
        AVv = AV.rearrange("(g p) t -> p g t", p=128)
        GUv = GU.rearrange("(g p) t -> p g t", p=128)
        SZv = SZ[0].rearrange("(g p) t -> p g t", p=128)
        YGv = YG[0].rearrange("(g p) t -> p g t", p=128)
        nch = 8 if last else 9

        def s1(ci):
            t0 = ci * 512
            n = 512 if ci < 8 else 256
            ntt = n // 128
            av, r_av = avr.next()
            gu, r_gu = gur.next()
            sz, r_sz = szr.next()
            sq, r_sq = sqr.next()
            rstd, r_rstd = rsr.next()
            vn, r_vn = vnr.next()
            vtok, r_vtok = vtr.next()
            t1, r_t1 = t1r.next()
            v3 = lambda a: a.rearrange("p (g t) -> p g t", g=4)[:, :, 0:n]
            avv, guv, szv, sqv, vnv, t1v = v3(av), v3(gu), v3(sz), v3(sq), v3(vn), v3(t1)
            B.load(avv, AVv[:, :, t0:t0 + n], r_av)
            B.load(guv, GUv[:, :, t0:t0 + n], r_gu)
            B.load(szv, SZv[:, :, t0:t0 + n], r_sz)
            B.tt("pool", sqv, avv, avv, MUL, [r_av], [r_sq])
            for g in range(4):
                B.mm(ps[0][:, 0:n], ones_b, sqv[:, g, :], g == 0, g == 3, [r_sq, r_onesb], [rps[0]])
            B.act(rstd[:, 0:n], ps[0][:, 0:n], AF.Ln, [rps[0]], [r_rstd], scale=1.0 / 512, bias=EPS)
            B.act(rstd[:, 0:n], rstd[:, 0:n], AF.Exp, [r_rstd], [r_rstd], scale=-0.5)
            for g in range(4):
                B.stt(vnv[:, g, :], avv[:, g, :], gvA[:, g:g + 1], rstd[:, 0:n], MUL, MUL, [r_av, r_rstd, r_gvA], [r_vn])
            vtv = vtok.rearrange("p (t c) -> p t c", t=4)
            for tt in range(ntt):
                bank = 1 + tt // 2
                psb = ps[bank].bitcast(BF16)
                for g in range(4):
                    c0 = (tt % 2) * 512 + g * 128
                    B.tr(psb[:, c0:c0 + 128], vnv[:, g, tt * 128:(tt + 1) * 128], ident_b, [r_vn, r_identb], [rps[bank]])
                if tt % 2 == 0:
                    B.act(vtv[:, tt, :], psb[:, 0:512], AF.Identity, [rps[bank]], [r_vtok])
                else:
                    B.cp("dve", vtv[:, tt, :], psb[:, 512:1024], [rps[bank]], [r_vtok])
            B.tt("pool", t1v, guv, szv, MUL, [r_gu, r_sz], [r_t1])
            return (t0, n, ntt, vtv, r_vtok, t1v, r_t1)

        def s2(ctx):
            t0, n, ntt, vtv, r_vtok, t1v, r_t1 = ctx
            yg, r_yg = ygr.next()
            ygv = yg.rearrange("p (g t) -> p g t", g=4)[:, :, 0:n]
            for g in range(4):
                bank = 3 + g
                for tt in range(ntt):
                    o = ps[bank][:, tt * 128:(tt + 1) * 128]
                    B.mm(o, vtv[:, tt, g * 128:(g + 1) * 128], wsTb[:, g * 128:(g + 1) * 128], True, False,
                         [r_vtok, r_wsTb], [rps[bank]])
                    B.mm(o, ones_row, bsA[:, g * 128:(g + 1) * 128], False, True, [r_onesrow, r_bsA], [rps[bank]])
                B.tt("dve", ygv[:, g, :], ps[bank][:, 0:n], t1v[:, g, :], MUL, [rps[bank], r_t1], [r_yg])
            B.store(YGv[:, :, t0:t0 + n], ygv, r_yg)

        prev = s1(0)
        for ci in range(nch):
            nxt = s1(ci + 1) if ci + 1 < nch else None
            s2(prev)
            prev = nxt

    def phase_B(li):
        B.phase_reset()
        QT, r_QT = B.alloc(4 * T, BF16, name="QT")
        KT, r_KT = B.alloc(T, BF16, name="KT")
        VA, r_VA = B.alloc(NT_TILES * 130, BF16, name="VA")
        QTv = QT.rearrange("p (c t) -> p c t", c=4)
        VAv = VA.rearrange("p (n j e) -> p n j e", n=NT_TILES, j=2)
        szbr = B.ring(2, 2048, BF16, dma=True, name="SZb")
        SZ1v = SZ[1].rearrange("(c p) t -> p c t", p=128)
        B.memset("pool", VA, 1.0, [r_VA])
        bqr = B.ring(2, 2048, F32, dma=True, name="bq")
        bkr = B.ring(2, 512, F32, dma=True, name="bk")
        bvr = B.ring(2, 512, BF16, dma=True, name="bv")
        cosr = B.ring(2, 512, F32, dma=True, name="cos")
        sinr = B.ring(2, 512, F32, dma=True, name="sin")
        sqr = B.ring(2, 512, BF16, name="sq")
        rsr = B.ring(2, 512, F32, name="rs")
        qnr = B.ring(2, 512, F32, name="qn")
        t1r = B.ring(2, 512, F32, name="t1")
        t2r = B.ring(2, 512, F32, name="t2")
        BQv = BQ.rearrange("(c p) t -> p c t", p=128)
        cnt = 0
        for ci in range(9):
            t0 = ci * 512
            n = 512 if ci < 8 else 256
            lat = ci < 8
            bq, r_bq = bqr.next()
            bk, r_bk = bkr.next()
            bv, r_bv = bvr.next()
            bqv = bq.rearrange("p (c t) -> p c t", c=4)[:, :, 0:n]
            B.load(bqv, BQv[:, :, t0:t0 + n], r_bq)
            B.load(bk[:, 0:n], BK[:, t0:t0 + n], r_bk)
            B.load(bv[:, 0:n], BV[:, t0:t0 + n], r_bv)
            if lat:
                cs, r_cs = cosr.next()
                sn, r_sn = sinr.next()
                B.load(cs, cd["rcos"][:, t0:t0 + n], r_cs)
                B.load(sn, cd["rsin"][:, t0:t0 + n], r_sn)
            for c5 in range(5):
                x = bqv[:, c5, :] if c5 < 4 else bk[:, 0:n]
                r_x = r_bq if c5 < 4 else r_bk
                g, r_g = (gqB, r_gqB) if c5 < 4 else (gkB, r_gkB)
                dest = QTv[:, c5, t0:t0 + n] if c5 < 4 else KT[:, t0:t0 + n]
                r_dest = r_QT if c5 < 4 else r_KT
                sq, r_sq = sqr.next()
                rs, r_rs = rsr.next()
                qn, r_qn = qnr.next()
                bank = cnt % 4
                cnt += 1
                B.tt("pool", sq[:, 0:n], x, x, MUL, [r_x], [r_sq])
                B.mm(ps[bank][:, 0:n], blk64, sq[:, 0:n], True, True, [r_sq, r_blk64], [rps[bank]])
                B.act(rs[:, 0:n], ps[bank][:, 0:n], AF.Ln, [rps[bank]], [r_rs], scale=1.0 / 64, bias=EPS)
                B.act(rs[:, 0:n], rs[:, 0:n], AF.Exp, [r_rs], [r_rs], scale=-0.5)
                if lat:
                    B.stt(qn[:, 0:n], x, g[:, 0:1], rs[:, 0:n], MUL, MUL, [r_x, r_g, r_rs], [r_qn])
                    bank2 = 4 + cnt % 2
                    B.mm(ps[bank2][:, 0:n], perm_f, qn[:, 0:n], True, True, [r_qn, r_perm], [rps[bank2]])
                    t1, r_t1 = t1r.next()
                    t2, r_t2 = t2r.next()
                    B.tt("dve", t1[:, 0:n], ps[bank2][:, 0:n], sn[:, 0:n], MUL, [rps[bank2], r_sn], [r_t1])
                    B.tt("pool", t2[:, 0:n], qn[:, 0:n], cs[:, 0:n], MUL, [r_qn, r_cs], [r_t2])
                    B.tt("dve", dest, t1[:, 0:n], t2[:, 0:n], ADD, [r_t1, r_t2], [r_dest])
                else:
                    B.stt(dest, x, g[:, 0:1], rs[:, 0:n], MUL, MUL, [r_x, r_g, r_rs], [r_dest])
            for tt in range(n // 128):
                tile = ci * 4 + tt
                bank = 6 + tt % 2
                psb = ps[bank].bitcast(BF16)
                B.tr(psb[:, 0:128], bv[:, tt * 128:(tt + 1) * 128], ident_b, [r_bv, r_identb], [rps[bank]])
                B.cp("dve", VAv[:, tile, :, 0:64], psb[:, 0:128].rearrange("p (j e) -> p j e", j=2), [rps[bank]], [r_VA])
        pring = B.ring(20, 512, BF16, name="P")
        onr = B.ring(2, 512, F32, name="On")
        denr = B.ring(2, 8, F32, name="den")
        ygr = B.ring(2, 2048, BF16, dma=True, name="yg")
        bmv = bmask.rearrange("p (m f) -> p m f", m=2)
        YGv = YG[1].rearrange("(c p) t -> p c t", p=128)
        st = dict(sb=0, ob=0, yg=None, r_yg=None, SZbv=None, r_SZb=None, on=None, r_on=None)

        def s1(qb, j):
            lat = qb < 32
            if lat:
                tiles = ([qb - 1] if qb > 0 else []) + [qb] + ([qb + 1] if qb < 31 else []) + [32, 33]
            else:
                tiles = [32, 33]
            Ps = []
            for kt in tiles:
                bank = st["sb"] % 4
                st["sb"] += 1
                B.mm(ps[bank][:], KT[64 * j:64 * j + 64, kt * 128:(kt + 1) * 128],
                     QTv[64 * j:64 * j + 64, :, qb * 128:(qb + 1) * 128], True, True, [r_KT, r_QT], [rps[bank]])
                Pt, r_P = pring.next()
                B.act(Pt, ps[bank][:], AF.Exp, [rps[bank]], [r_P], scale=0.125)
                if lat and kt == qb - 1:
                    B.tt("dve", Pt, Pt, bmv[:, 0, :], MUL, [r_P, r_bm], [r_P])
                if lat and kt == qb + 1:
                    B.tt("dve", Pt, Pt, bmv[:, 1, :], MUL, [r_P, r_bm], [r_P])
                Ps.append((Pt, r_P, kt))
            return Ps

        def s2(qb, j, Ps):
            if j == 0:
                if qb % 4 == 0:
                    st["yg"], st["r_yg"] = ygr.next()
                    SZb, r_SZb = szbr.next()
                    SZbv = SZb.rearrange("p (c t) -> p c t", c=4)
                    nld = min(512, T - qb * 128)
                    B.load(SZbv[:, :, 0:nld], SZ1v[:, :, qb * 128:qb * 128 + nld], r_SZb)
                    st["SZbv"], st["r_SZb"] = SZbv, r_SZb
                st["on"], st["r_on"] = onr.next()
            on, r_on = st["on"], st["r_on"]
            onv = on.rearrange("p (h e) -> p h e", h=8)
            obank = 4 + st["ob"] % 2
            st["ob"] += 1
            Ov = ps[obank][:, 0:260].rearrange("p (h e) -> p h e", e=65)
            for hh in range(4):
                for ti, (Pt, r_P, kt) in enumerate(Ps):
                    B.mm(ps[obank][:, hh * 65:(hh + 1) * 65], Pt[:, hh * 128:(hh + 1) * 128], VAv[:, kt, j, :],
                         ti == 0, ti == len(Ps) - 1, [r_P, r_VA], [rps[obank]])
            den, r_den = denr.next()
            B.tt("dve", den[:, 0:4], Ov[:, :, 64], esink[:, 4 * j:4 * j + 4], ADD, [rps[obank], r_esink], [r_den])
            B.recip(den[:, 0:4], den[:, 0:4], [r_den], [r_den])
            B.tt("dve", onv[:, 4 * j:4 * j + 4, :], Ov[:, :, 0:64], den[:, 0:4].unsqueeze(2).to_broadcast([128, 4, 64]),
                 MUL, [rps[obank], r_den], [r_on])
            if j == 1:
                yg, r_yg = st["yg"], st["r_yg"]
                ygv = yg.rearrange("p (c t) -> p c t", c=4)
                tb = 6 + qb % 2
                for cc in range(4):
                    B.tr(ps[tb][:, cc * 128:(cc + 1) * 128], on[:, cc * 128:(cc + 1) * 128], ident_f, [r_on, r_identf], [rps[tb]])
                qq = qb % 4
                B.tt("dve", ygv[:, :, qq * 128:(qq + 1) * 128], ps[tb][:].rearrange("p (c t) -> p c t", c=4),
                     st["SZbv"][:, :, qq * 128:(qq + 1) * 128], MUL, [rps[tb], st["r_SZb"]], [r_yg])
                if qb % 4 == 3 or qb == NT_TILES - 1:
                    t0 = (qb // 4) * 512
                    n = (qq + 1) * 128
                    B.store(YGv[:, :, t0:t0 + n], ygv[:, :, 0:n], r_yg)

        nqb = 32 if last_flags[li] else NT_TILES
        units = [(qb, j) for qb in range(nqb) for j in range(2)]
        prev = s1(*units[0])
        for ui, (qb, j) in enumerate(units):
            nxt = s1(*units[ui + 1]) if ui + 1 < len(units) else None
            s2(qb, j, prev)
            prev = nxt

    def phase_C(li):
        B.phase_reset()
        Wcs, r_Wcs = B.alloc(1024, BF16, name="Wcs")
        Wv = Wcs.rearrange("p (g d) -> p g d", g=4)
        for g in range(4):
            B.mm(ps[0][:, 0:128], CCs, wfC[:, g * 128:(g + 1) * 128], True, True, [r_CC, r_wfC], [rps[0]])
            B.mm(ps[1][:, 0:128], SCs, wfC[:, g * 128:(g + 1) * 128], True, True, [r_SC, r_wfC], [rps[1]])
            B.cp("dve", Wv[:, g, 0:128], ps[0][:, 0:128], [rps[0]], [r_Wcs])
            B.act(Wv[:, g, 128:256], ps[1][:, 0:128], AF.Identity, [rps[1]], [r_Wcs], scale=-1.0)
        Yall, r_Y = B.alloc(NT_TILES * 1024, BF16, name="Yall")
        Yv = Yall.rearrange("p (s g d) -> p s g d", s=NT_TILES, g=4)
        finr = B.ring(2, 2048, BF16, dma=True, name="fin")
        FINv = FIN.rearrange("(g p) t -> p g t", p=128)
        k = 0
        for ci in range(8 if last_flags[li] else 9):
            t0 = ci * 512
            n = 512 if ci < 8 else 256
            fin, r_fin = finr.next()
            finv = fin.rearrange("p (g t) -> p g t", g=4)[:, :, 0:n]
            B.load(finv, FINv[:, :, t0:t0 + n], r_fin)
            for tt in range(n // 128):
                st = ci * 4 + tt
                for gp in range(2):
                    bank = k % 4
                    k += 1
                    for g2 in range(2):
                        g = gp * 2 + g2
                        B.mm(ps[bank][:, g2 * 256:(g2 + 1) * 256], finv[:, g, tt * 128:(tt + 1) * 128], Wv[:, g, :], True, True,
                             [r_fin, r_Wcs], [rps[bank]])
                    o = Yv[:, st, gp * 2:gp * 2 + 2, :]
                    i_ = ps[bank][:].rearrange("p (g d) -> p g d", g=2)
                    if k % 2 == 0:
                        B.act(o, i_, AF.Identity, [rps[bank]], [r_Y])
                    else:
                        B.cp("dve", o, i_, [rps[bank]], [r_Y])
        tabr = B.ring(3, 8 * 512, BF16, dma=True, name="ctab")
        tabr2 = B.ring(3, 8 * 512, BF16, dma=True, name="stab")
        szr = B.ring(2, 2048, BF16, dma=True, name="szc")
        ygr = B.ring(2, 2048, BF16, dma=True, name="ygc")
        SZv = SZ[2].rearrange("(g p) t -> p g t", p=128)
        YGv = YG[2].rearrange("(g p) t -> p g t", p=128)
        CLv = cd["CL"].rearrange("(s p) t -> p s t", p=128)
        SLv = cd["SL"].rearrange("(s p) t -> p s t", p=128)
        todo = [(tc, sg) for tc in range(8) for sg in range(4)]

        def tl(tc, sg):
            a, r_a = tabr.next()
            b, r_b = tabr2.next()
            av = a.rearrange("p (s t) -> p s t", s=8)
            bv = b.rearrange("p (s t) -> p s t", s=8)
            B.load(av, CLv[:, sg * 8:(sg + 1) * 8, tc * 512:(tc + 1) * 512], r_a)
            B.load(bv, SLv[:, sg * 8:(sg + 1) * 8, tc * 512:(tc + 1) * 512], r_b)
            return av, r_a, bv, r_b
        pend = [tl(*todo[0]), tl(*todo[1])]
        for i, (tc, sg) in enumerate(todo):
            if i + 2 < len(todo):
                pend.append(tl(*todo[i + 2]))
            av, r_a, bv, r_b = pend.pop(0)
            pb = 4 * (tc % 2)
            if sg == 0:
                sz, r_sz = szr.next()
                szv = sz.rearrange("p (g t) -> p g t", g=4)
                B.load(szv, SZv[:, :, tc * 512:(tc + 1) * 512], r_sz)
            for g in range(4):
                for s8 in range(8):
                    st = sg * 8 + s8
                    B.mm(ps[pb + g][:], Yv[:, st, g, 0:128], av[:, s8, :], st == 0, False, [r_Y, r_a], [rps[pb + g]])
                    B.mm(ps[pb + g][:], Yv[:, st, g, 128:256], bv[:, s8, :], False, st == 31, [r_Y, r_b], [rps[pb + g]])
            if sg == 3:
                yg, r_yg = ygr.next()
                ygv = yg.rearrange("p (g t) -> p g t", g=4)
                for g in range(4):
                    B.tt("dve", ygv[:, g, :], ps[pb + g][:], szv[:, g, :], MUL, [rps[pb + g], r_sz], [r_yg])
                B.store(YGv[:, :, tc * 512:(tc + 1) * 512], ygv, r_yg)
        if last_flags[li]:
            return
        sz, r_sz = szr.next()
        szv = sz.rearrange("p (g t) -> p g t", g=4)[:, :, 0:256]
        B.load(szv, SZv[:, :, TL:T], r_sz)
        yg, r_yg = ygr.next()
        ygv = yg.rearrange("p (g t) -> p g t", g=4)[:, :, 0:256]
        C2 = C256s.rearrange("p (s t) -> p s t", s=2)
        S2 = S256s.rearrange("p (s t) -> p s t", s=2)
        for g in range(4):
            for s in range(2):
                B.mm(ps[g][:, 0:256], Yv[:, 32 + s, g, 0:128], C2[:, s, :], s == 0, False, [r_Y, r_C256], [rps[g]])
                B.mm(ps[g][:, 0:256], Yv[:, 32 + s, g, 128:256], S2[:, s, :], False, s == 1, [r_Y, r_S256], [rps[g]])
            B.tt("dve", ygv[:, g, :], ps[g][:, 0:256], szv[:, g, :], MUL, [rps[g], r_sz], [r_yg])
        B.store(YGv[:, :, TL:T], ygv, r_yg)

    def phase_D(li):
        B.phase_reset()
        rbr = B.ring(8, 512, F32, dma=True, name="rb")
        ebr = B.ring(8, 512, BF16, dma=True, name="eb")
        for pat in range(25):
            for c in range(4):
                rb, r_rb = rbr.next()
                eb, r_eb = ebr.next()
                B.load(rb, rpbg[li, pat, c], r_rb)
                B.act(eb, rb, AF.Exp, [r_rb], [r_eb])
                B.store(ET[pat, c], eb, r_eb)
        B.phase_reset()
        QTc, r_QT = B.alloc(T, BF16, name="QTc")
        KTc, r_KT = B.alloc(T, BF16, name="KTc")
        VA, r_VA = B.alloc(NT_TILES * 132, BF16, name="VAd")
        VAv = VA.rearrange("p (n j e) -> p n j e", n=NT_TILES, j=4)
        SZc, r_SZc = B.alloc(T, BF16, dma=True, name="SZc")
        Eint, r_Eint = B.alloc(5 * 512, BF16, dma=True, name="Eint")
        Eiv = Eint.rearrange("p (o f) -> p o f", o=5)
        edr = B.ring(2, 5 * 512, BF16, dma=True, name="Eedge")
        B.memset("pool", VA, 1.0, [r_VA])
        xr = B.ring(3, 512, F32, dma=True, name="x")
        dvr = B.ring(2, 512, BF16, dma=True, name="dv")
        sqr = B.ring(2, 512, BF16, name="sq")
        rsr = B.ring(2, 512, F32, name="rs")
        pring = B.ring(24, 512, BF16, name="P")
        qmr = B.ring(3, 512, BF16, name="qm")
        onr = B.ring(2, 128, F32, name="On")
        denr = B.ring(2, 8, F32, name="den")
        ygr = B.ring(2, 512, BF16, dma=True, name="yg")
        hmv = headmask.rearrange("p (j q) -> p j q", j=4)
        ETv = ET.rearrange("a c p f -> p a c f")
        cnt = 0
        cntr = dict(sb=0, ob=0)
        for c in range(4):
            B.load(SZc, SZ[3][c * 128:(c + 1) * 128, :], r_SZc)
            B.load(Eiv, ETv[:, 10:15, c, :], r_Eint)
            for ci in range(9):
                t0 = ci * 512
                n = 512 if ci < 8 else 256
                for (src, g, r_g, dest, r_dest) in ((DQ, gqD, r_gqD, QTc, r_QT), (DK, gkD, r_gkD, KTc, r_KT)):
                    x, r_x = xr.next()
                    B.load(x[:, 0:n], src[c * 128:(c + 1) * 128, t0:t0 + n], r_x)
                    sq, r_sq = sqr.next()
                    rs, r_rs = rsr.next()
                    bank = cnt % 4
                    cnt += 1
                    B.tt("pool", sq[:, 0:n], x[:, 0:n], x[:, 0:n], MUL, [r_x], [r_sq])
                    B.mm(ps[bank][:, 0:n], blk32, sq[:, 0:n], True, True, [r_sq, r_blk32], [rps[bank]])
                    B.act(rs[:, 0:n], ps[bank][:, 0:n], AF.Ln, [rps[bank]], [r_rs], scale=1.0 / 32, bias=EPS)
                    B.act(rs[:, 0:n], rs[:, 0:n], AF.Exp, [r_rs], [r_rs], scale=-0.5)
                    B.stt(dest[:, t0:t0 + n], x[:, 0:n], g[:, 0:1], rs[:, 0:n], MUL, MUL, [r_x, r_g, r_rs], [r_dest])
                dv, r_dv = dvr.next()
                B.load(dv[:, 0:n], DV[c * 128:(c + 1) * 128, t0:t0 + n], r_dv)
                for tt in range(n // 128):
                    tile = ci * 4 + tt
                    bank = 6 + tt % 2
                    psb = ps[bank].bitcast(BF16)
                    B.tr(psb[:, 0:128], dv[:, tt * 128:(tt + 1) * 128], ident_b, [r_dv, r_identb], [rps[bank]])
                    B.cp("dve", VAv[:, tile, :, 0:32], psb[:, 0:128].rearrange("p (j e) -> p j e", j=4), [rps[bank]], [r_VA])
            st = dict(yg=None, r_yg=None)

            def s1(qb, c=c):
                lat = qb < 32
                Ev, r_E = None, None
                if lat:
                    tl_ = d_tiles(qb)
                    pb_ = tl_[0][1]
                    if pb_ == 10:
                        Ev, r_E = Eiv, r_Eint
                    else:
                        ed, r_E = edr.next()
                        Ev = ed.rearrange("p (o f) -> p o f", o=5)
                        B.load(Ev, ETv[:, pb_:pb_ + 5, c, :], r_E)
                    tiles = [(kt, o) for o, (kt, _) in enumerate(tl_)] + [(32, None), (33, None)]
                else:
                    tiles = [(32, None), (33, None)]
                qm, r_qm = qmr.next()
                B.tt("pool", qm.rearrange("p (j q) -> p j q", j=4),
                     QTc[:, qb * 128:(qb + 1) * 128].unsqueeze(1).to_broadcast([128, 4, 128]), hmv, MUL,
                     [r_QT, r_hm], [r_qm])
                Ps = []
                for (kt, o) in tiles:
                    bank = cntr["sb"] % 4
                    cntr["sb"] += 1
                    B.mm(ps[bank][:], KTc[:, kt * 128:(kt + 1) * 128], qm, True, True, [r_KT, r_qm], [rps[bank]])
                    Pt, r_P = pring.next()
                    B.act(Pt, ps[bank][:], AF.Exp, [rps[bank]], [r_P], scale=float(32 ** -0.5))
                    if o is not None:
                        B.tt("dve", Pt, Pt, Ev[:, o, :], MUL, [r_P, r_E], [r_P])
                    Ps.append((Pt, r_P, kt))
                return Ps

            def s2(qb, Ps, c=c):
                if qb % 4 == 0:
                    st["yg"], st["r_yg"] = ygr.next()
                yg, r_yg = st["yg"], st["r_yg"]
                obank = 4 + cntr["ob"] % 2
                cntr["ob"] += 1
                Ov = ps[obank][:, 0:132].rearrange("p (h e) -> p h e", e=33)
                for jh in range(4):
                    for ti, (Pt, r_P, kt) in enumerate(Ps):
                        B.mm(ps[obank][:, jh * 33:(jh + 1) * 33], Pt[:, jh * 128:(jh + 1) * 128], VAv[:, kt, jh, :],
                             ti == 0, ti == len(Ps) - 1, [r_P, r_VA], [rps[obank]])
                den, r_den = denr.next()
                B.recip(den[:, 0:4], Ov[:, :, 32], [rps[obank]], [r_den])
                on, r_on = onr.next()
                B.tt("dve", on.rearrange("p (h e) -> p h e", h=4), Ov[:, :, 0:32],
                     den[:, 0:4].unsqueeze(2).to_broadcast([128, 4, 32]), MUL, [rps[obank], r_den], [r_on])
                tb = 6 + qb % 2
                B.tr(ps[tb][:, 0:128], on, ident_f, [r_on, r_identf], [rps[tb]])
                qq = qb % 4
                B.tt("dve", yg[:, qq * 128:(qq + 1) * 128], ps[tb][:, 0:128], SZc[:, qb * 128:(qb + 1) * 128], MUL,
                     [rps[tb], r_SZc], [r_yg])
                if qb % 4 == 3 or qb == NT_TILES - 1:
                    t0 = (qb // 4) * 512
                    n = (qq + 1) * 128
                    B.store(YG[3][c * 128:(c + 1) * 128, t0:t0 + n], yg[:, 0:n], r_yg)

            nqb = 32 if last_flags[li] else NT_TILES
            prev = s1(0)
            for qb in range(nqb):
                nxt = s1(qb + 1) if qb + 1 < nqb else None
                s2(qb, prev)
                prev = nxt

    def phase_merge(li):
        last = last_flags[li]
        hsrc = hT0 if li == 0 else H
        hdst = OUT if last else H
        B.phase_reset()
        nTb, r_nTb = B.alloc(16 * 1088, BF16, dma=True, name="nTb")
        ygb, r_ygb = B.alloc(16 * 1088, BF16, dma=True, name="ygb")
        acc, r_acc = B.alloc(16 * 1088, BF16, name="accT")
        nTv = nTb.rearrange("p (k t) -> p k t", k=16)
        ygv = ygb.rearrange("p (k t) -> p k t", k=16)
        accv = acc.rearrange("p (k t) -> p k t", k=16)
        wgr = B.ring(4, 2048, BF16, dma=True, name="wg")
        wbr = B.ring(4, 512, BF16, dma=True, name="wb")
        sigr = B.ring(3, 416, F32, dma=True, name="sig")
        tmpr = B.ring(3, 416, F32, dma=True, name="tmp")
        a32 = [[B.alloc(416, F32, dma=True, name=f"a32_{i}_{k}") for k in range(3)] for i in range(2)]
        wor = wgr
        htr = Ring(sigr.items + tmpr.items)
        ostr = Ring(a32[0] + a32[1])
        NTv = NT.rearrange("(k p) t -> p k t", p=128)
        gb = 0
        for subs0 in TCHUNKS:
            subs = [s for s in subs0 if not (last and s[2] == 1)]
            tbase = subs[0][0]
            ntot = sum(s[1] for s in subs)
            offs = []
            o_ = 0
            for s in subs:
                offs.append(o_)
                o_ += s[1]
            B.load(nTv[:, :, 0:ntot], NTv[:, :, tbase:tbase + ntot], r_nTb)
            r_ygs = []
            for b in range(4):
                r_b = pg.res(f"ygb{b}", dma=True)
                B.pg.dma("sp", ygv[:, b * 4:(b + 1) * 4, 0:ntot],
                         YG[b].rearrange("(k p) t -> p k t", p=128)[:, :, tbase:tbase + ntot], [], [r_b, r_ygb], r_b)
                r_ygs.append(r_b)
            todo = [(oc, b) for oc in range(16) for b in range(4)]

            def wl(oc, b):
                wg, r_wg = wgr.next()
                wb, r_wb = wbr.next()
                B.load(wg, w_gate_b[li, b, oc], r_wg, q="pool")
                B.load(wb, w_branch_b[li, b, oc], r_wb, q="pool")
                return wg, r_wg, wb, r_wb
            pend = [wl(*todo[0]), wl(*todo[1])]
            for i, (oc, b) in enumerate(todo):
                if i + 2 < len(todo):
                    pend.append(wl(*todo[i + 2]))
                wg, r_wg, wb, r_wb = pend.pop(0)
                for si, (t0, n, j) in enumerate(subs):
                    off = offs[si]
                    gbank = gb % 4
                    pbank = 4 + gb % 4
                    gb += 1
                    for kc in range(KC):
                        B.mm(ps[gbank][:, 0:n], wg[:, kc * 128:(kc + 1) * 128], nTv[:, kc, off:off + n], kc == 0, kc == KC - 1,
                             [r_wg, r_nTb], [rps[gbank]])
                    for kc in range(4):
                        B.mm(ps[pbank][:, 0:n], wb[:, kc * 128:(kc + 1) * 128], ygv[:, b * 4 + kc, off:off + n], kc == 0, kc == 3,
                             [r_wb, r_ygs[b]], [rps[pbank]])
                    sig, r_sig = sigr.next()
                    B.act(sig[:, 0:n], ps[gbank][:, 0:n], AF.Sigmoid, [rps[gbank]], [r_sig])
                    a, r_a = a32[oc % 2][si]
                    if b == 0:
                        B.tt("dve", a[:, 0:n], sig[:, 0:n], ps[pbank][:, 0:n], MUL, [r_sig, rps[pbank]], [r_a])
                    else:
                        tmp, r_tmp = tmpr.next()
                        B.tt("dve", tmp[:, 0:n], sig[:, 0:n], ps[pbank][:, 0:n], MUL, [r_sig, rps[pbank]], [r_tmp])
                        if b < 3:
                            B.tt("dve", a[:, 0:n], a[:, 0:n], tmp[:, 0:n], ADD, [r_a, r_tmp], [r_a])
                        else:
                            B.tt("dve", accv[:, oc, off:off + n], a[:, 0:n], tmp[:, 0:n], ADD, [r_a, r_tmp], [r_acc])
            pend = []

            def wol(oc2):
                w, r = wor.next()
                B.load(w, w_out_b[li, oc2], r, q="pool")
                return w, r
            pend.append(wol(0))
            pend.append(wol(1))
            for oc2 in range(16):
                if oc2 + 2 < 16:
                    pend.append(wol(oc2 + 2))
                w, r_w = pend.pop(0)
                hts = []
                for si, (t0, n, j) in enumerate(subs):
                    ht, r_ht = htr.next()
                    B.load(ht[:, 0:n], hsrc[oc2 * 128:(oc2 + 1) * 128, t0:t0 + n], r_ht)
                    hts.append((ht, r_ht))
                for si, (t0, n, j) in enumerate(subs):
                    off = offs[si]
                    bank = gb % 8
                    gb += 1
                    for kc in range(KC):
                        B.mm(ps[bank][:, 0:n], w[:, kc * 128:(kc + 1) * 128], accv[:, kc, off:off + n], kc == 0, kc == KC - 1,
                             [r_w, r_acc], [rps[bank]])
                    ht, r_ht = hts[si]
                    ost, r_ost = ostr.next()
                    B.stt(ost[:, 0:n], ps[bank][:, 0:n], MTv[:, li, 32 + oc2, j:j + 1], ht[:, 0:n], MUL, ADD,
                          [rps[bank], r_ht, r_MT], [r_ost])
                    B.pg.dma("act", hdst[oc2 * 128:(oc2 + 1) * 128, t0:t0 + n], ost[:, 0:n], [r_ost], [], r_ost)

    phase_adaln()
    for li in range(L):
        pg.epoch = li + 1
        B.phase_reset()
        layer_params(li)
        for nm, fn in (("W", phase_norm_win), ("A", phase_A), ("B", phase_B), ("C", phase_C), ("D", phase_D),
                       ("M", phase_merge)):
            if nm in PHASES:
                fn(li)
    pg.finish()
    pg.emit()
    return B


def make_in_maps(inp, layers, cores):
    sh = host_layout(inp, layers)
    maps = []
    for b in cores:
        m = dict(sh)
        m["hT0"] = np.ascontiguousarray(np.concatenate([inp["x"][b].T, inp["ctx"][b].T], axis=1))
        cv = np.stack([inp["c"][b].reshape(KC, 128).T, inp["c_ctx"].reshape(KC, 128).T], axis=2)
        m["cvec"] = np.ascontiguousarray(cv.reshape(128, 32))
        maps.append(m)
    return maps


def kernel(**inputs):
    inp = {k: np.asarray(v) for k, v in inputs.items()}
    layers = list(range(DEPTH))
    B = build_program(DEPTH, [l == DEPTH - 1 for l in layers])
    maps = make_in_maps(inp, layers, list(range(8)))
    res = run_bass_kernel_spmd(B.nc, maps, core_ids=list(range(8)))
    out = np.stack([np.ascontiguousarray(r["OUT"].T) for r in res.results], axis=0)
    return out.astype(np.float32)
```
